# Optimizing a Trainium2 kernel written in Bass

```python
import math
import jax, jax.numpy as jnp
from jax import lax
import numpy as np

D_MODEL = 1024
BATCH = 2
SEQ = 16384
DEPTH = 1
DEC_BATCH = 32
DEC_SEQ = 2048
PAST_LEN = 128

HYENA_WIDTH = 512
ATTN_WIDTH = 512
N_DIFF_HEADS = 4
DIFF_HEAD_DIM = 64
DIFF_V_DIM = 2 * DIFF_HEAD_DIM
ROT_DIM = DIFF_HEAD_DIM // 4
ROPE_THETA = 500000.0
D_IN = 3 * HYENA_WIDTH + 3 * ATTN_WIDTH
D_FF = 4 * D_MODEL
FILTER_EMB = 33
FILTER_ORDER = 64
FAST_DECAY_PCT = 0.3
SLOW_DECAY_PCT = 1.5
DECAY_TARGET = 1e-2
Q_BLOCK = 128
NORM_EPS = 1e-6
SUBLN_EPS = 1e-5

kernel_name = "hymba_hyena_diffattn_adaln_encoder"


def rms_norm(x, w, eps=NORM_EPS):
    xf = x.astype(jnp.float32)
    y = xf * lax.rsqrt(jnp.mean(xf * xf, axis=-1, keepdims=True) + eps)
    return (y * w.astype(jnp.float32)).astype(x.dtype)


def implicit_filter(L, w1, b1, w2, b2, w3, b3, w4, freq):
    f32 = jnp.float32
    C = HYENA_WIDTH
    t = jnp.linspace(0.0, 1.0, L, dtype=f32)[:, None]
    bands = (FILTER_EMB - 1) // 2
    w = (2.0 * math.pi) * jnp.arange(L, dtype=f32)[:, None] / L
    f = jnp.linspace(1e-4, bands - 1, bands, dtype=f32)[None, :]
    z = jnp.concatenate([t, jnp.cos(f * w), -jnp.sin(f * w)], axis=-1)
    fr = freq.astype(f32)
    h = jnp.sin(fr * (z @ w1.astype(f32) + b1.astype(f32)))
    h = jnp.sin(fr * (h @ w2.astype(f32) + b2.astype(f32)))
    h = jnp.sin(fr * (h @ w3.astype(f32) + b3.astype(f32)))
    h = h @ w4.astype(f32)
    min_decay = math.log(DECAY_TARGET) / SLOW_DECAY_PCT
    max_decay = math.log(DECAY_TARGET) / FAST_DECAY_PCT
    deltas = jnp.linspace(min_decay, max_decay, C, dtype=f32)
    decay = jnp.exp(-t * jnp.abs(deltas)[None, :])
    h_f = h[:, :C] * decay
    h_b = h[:, C:] * decay
    return jnp.concatenate([h_f.at[0].add(h_b[0]), jnp.zeros((1, C), f32), h_b[:0:-1]], axis=0)


def hyena_mixer(u, conv_w, conv_b, filt_w1, filt_b1, filt_w2, filt_b2, filt_w3, filt_b3, filt_w4,
                filt_freq, hyena_bias, hyena_norm_w):
    B, L, _ = u.shape
    up = jnp.pad(u, ((0, 0), (1, 1), (0, 0)))
    u = up[:, :L] * conv_w[0] + up[:, 1:L + 1] * conv_w[1] + up[:, 2:] * conv_w[2] + conv_b
    x0, x1, v = jnp.split(u, 3, axis=-1)
    s = (x1 * v).astype(jnp.float32)
    k = implicit_filter(L, filt_w1, filt_b1, filt_w2, filt_b2, filt_w3, filt_b3, filt_w4, filt_freq)
    n = 2 * L
    y = jnp.fft.irfft(jnp.fft.rfft(s, n=n, axis=1) * jnp.fft.rfft(k, n=n, axis=0)[None], n=n, axis=1)[:, :L]
    y = x0.astype(jnp.float32) * (y + s * hyena_bias.astype(jnp.float32))
    return rms_norm(y, hyena_norm_w).astype(u.dtype)


def partial_rotary(x):
    L = x.shape[1]
    xf = x.astype(jnp.float32)
    inv_freq = ROPE_THETA ** (-jnp.arange(0, ROT_DIM, 2, dtype=jnp.float32) / ROT_DIM)
    ang = jnp.arange(L, dtype=jnp.float32)[:, None] * inv_freq[None, :]
    ang = jnp.concatenate([ang, ang], axis=-1)[None, :, None, :]
    xr, xp = xf[..., :ROT_DIM], xf[..., ROT_DIM:]
    x1, x2 = xr[..., :ROT_DIM // 2], xr[..., ROT_DIM // 2:]
    xr = xr * jnp.cos(ang) + jnp.concatenate([-x2, x1], axis=-1) * jnp.sin(ang)
    return jnp.concatenate([xr, xp], axis=-1)


def diff_attention(q, k, v, lambda_q1, lambda_k1, lambda_q2, lambda_k2, subln_w, lambda_init):
    B, L, _ = q.shape
    H, d, dv = N_DIFF_HEADS, DIFF_HEAD_DIM, DIFF_V_DIM
    q = partial_rotary(q.reshape(B, L, 2 * H, d)) * (d ** -0.5)
    k = partial_rotary(k.reshape(B, L, 2 * H, d))
    kh = k.transpose(0, 2, 1, 3)
    vh = v.reshape(B, L, H, dv).transpose(0, 2, 1, 3).astype(jnp.float32)
    f32 = jnp.float32
    lam = (jnp.exp(jnp.sum(lambda_q1.astype(f32) * lambda_k1.astype(f32)))
           - jnp.exp(jnp.sum(lambda_q2.astype(f32) * lambda_k2.astype(f32))) + lambda_init)
    nblk = L // Q_BLOCK
    qb = q.transpose(0, 2, 1, 3).reshape(B, 2 * H, nblk, Q_BLOCK, d).transpose(2, 0, 1, 3, 4)

    def block(qblk):
        s = jnp.einsum('bhqd,bhkd->bhqk', qblk, kh)
        p = jax.nn.softmax(s, axis=-1).reshape(B, H, 2, Q_BLOCK, L)
        a = p[:, :, 0] - lam * p[:, :, 1]
        return jnp.einsum('bhqk,bhkd->bhqd', a, vh)

    o = lax.map(block, qb)
    o = o.transpose(1, 0, 3, 2, 4).reshape(B, L, H, dv)
    o = rms_norm(o, subln_w, SUBLN_EPS) * (1.0 - lambda_init)
    return o.reshape(B, L, H * dv)


def encoder(x, c, w_ada, b_ada, norm1_w, w_in, conv_w, conv_b, filt_w1, filt_b1, filt_w2, filt_b2,
            filt_w3, filt_b3, filt_w4, filt_freq, hyena_bias, hyena_norm_w, lambda_q1, lambda_k1,
            lambda_q2, lambda_k2, subln_w, w_out, norm2_w, w_mlp1, w_mlp2, final_w):
    HW3 = 3 * HYENA_WIDTH
    for l in range(DEPTH):
        lambda_init = 0.8 - 0.6 * math.exp(-0.3 * l)
        mod = (jax.nn.silu(c) @ w_ada[l] + b_ada[l])[:, None, :]
        sh1, sc1, g1, sh2, sc2, g2 = jnp.split(mod, 6, axis=-1)
        h = rms_norm(x, norm1_w[l]) * (1.0 + sc1) + sh1
        p = h @ w_in[l]
        hy = hyena_mixer(p[..., :HW3], conv_w[l], conv_b[l], filt_w1[l], filt_b1[l], filt_w2[l],
                         filt_b2[l], filt_w3[l], filt_b3[l], filt_w4[l], filt_freq[l],
                         hyena_bias[l], hyena_norm_w[l])
        q = p[..., HW3:HW3 + ATTN_WIDTH]
        k = p[..., HW3 + ATTN_WIDTH:HW3 + 2 * ATTN_WIDTH]
        v = p[..., HW3 + 2 * ATTN_WIDTH:]
        at = diff_attention(q, k, v, lambda_q1[l], lambda_k1[l], lambda_q2[l], lambda_k2[l],
                            subln_w[l], lambda_init).astype(x.dtype)
        x = x + g1 * (jnp.concatenate([hy, at], axis=-1) @ w_out[l])
        h = rms_norm(x, norm2_w[l]) * (1.0 + sc2) + sh2
        x = x + g2 * (jnp.square(jax.nn.relu(h @ w_mlp1[l])) @ w_mlp2[l])
    return rms_norm(x, final_w)


def setup_inputs(seed: int = 0) -> dict:
    key = jax.random.key(seed)
    ks = jax.random.split(key, 32)
    f32 = jnp.float32
    nrm = lambda k, shape, s: jax.random.normal(k, shape, f32) * s
    D, L_ = D_MODEL, DEPTH
    return {
        "x_prompt": nrm(ks[0], (BATCH, SEQ, D), 1.0),
        "x_sample": nrm(ks[1], (DEC_BATCH, DEC_SEQ, D), 1.0),
        "c_prompt": nrm(ks[2], (BATCH, D), 1.0),
        "c_sample": nrm(ks[3], (DEC_BATCH, D), 1.0),
        "w_ada": nrm(ks[4], (L_, D, 6 * D), 0.5 * D ** -0.5),
        "b_ada": nrm(ks[5], (L_, 6 * D), 0.02),
        "norm1_w": 1.0 + nrm(ks[6], (L_, D), 0.02),
        "w_in": nrm(ks[7], (L_, D, D_IN), D ** -0.5),
        "conv_w": nrm(ks[8], (L_, 3, 3 * HYENA_WIDTH), 3 ** -0.5),
        "conv_b": nrm(ks[9], (L_, 3 * HYENA_WIDTH), 0.02),
        "filt_w1": nrm(ks[10], (L_, FILTER_EMB, FILTER_ORDER), FILTER_EMB ** -0.5),
        "filt_b1": nrm(ks[11], (L_, FILTER_ORDER), 0.1),
        "filt_w2": nrm(ks[12], (L_, FILTER_ORDER, FILTER_ORDER), FILTER_ORDER ** -0.5),
        "filt_b2": nrm(ks[13], (L_, FILTER_ORDER), 0.1),
        "filt_w3": nrm(ks[14], (L_, FILTER_ORDER, FILTER_ORDER), FILTER_ORDER ** -0.5),
        "filt_b3": nrm(ks[15], (L_, FILTER_ORDER), 0.1),
        "filt_w4": nrm(ks[16], (L_, FILTER_ORDER, 2 * HYENA_WIDTH), FILTER_ORDER ** -0.5),
        "filt_freq": 1.0 + nrm(ks[17], (L_, FILTER_ORDER), 0.1),
        "hyena_bias": nrm(ks[18], (L_, HYENA_WIDTH), 1.0),
        "hyena_norm_w": 1.0 + nrm(ks[19], (L_, HYENA_WIDTH), 0.02),
        "lambda_q1": nrm(ks[20], (L_, DIFF_HEAD_DIM), 0.1),
        "lambda_k1": nrm(ks[21], (L_, DIFF_HEAD_DIM), 0.1),
        "lambda_q2": nrm(ks[22], (L_, DIFF_HEAD_DIM), 0.1),
        "lambda_k2": nrm(ks[23], (L_, DIFF_HEAD_DIM), 0.1),
        "subln_w": 1.0 + nrm(ks[24], (L_, DIFF_V_DIM), 0.02),
        "w_out": nrm(ks[25], (L_, D, D), D ** -0.5),
        "norm2_w": 1.0 + nrm(ks[26], (L_, D), 0.02),
        "w_mlp1": nrm(ks[27], (L_, D, D_FF), D ** -0.5),
        "w_mlp2": nrm(ks[28], (L_, D_FF, D), D_FF ** -0.5),
        "final_w": 1.0 + nrm(ks[29], (D,), 0.02),
    }


def reference(x_prompt, x_sample, c_prompt, c_sample, w_ada, b_ada, norm1_w, w_in, conv_w, conv_b,
              filt_w1, filt_b1, filt_w2, filt_b2, filt_w3, filt_b3, filt_w4, filt_freq, hyena_bias,
              hyena_norm_w, lambda_q1, lambda_k1, lambda_q2, lambda_k2, subln_w, w_out, norm2_w,
              w_mlp1, w_mlp2, final_w):
    y_prompt = encoder(x_prompt, c_prompt, w_ada, b_ada, norm1_w, w_in, conv_w, conv_b, filt_w1, filt_b1,
                       filt_w2, filt_b2, filt_w3, filt_b3, filt_w4, filt_freq, hyena_bias, hyena_norm_w,
                       lambda_q1, lambda_k1, lambda_q2, lambda_k2, subln_w, w_out, norm2_w, w_mlp1,
                       w_mlp2, final_w)
    y_sample = encoder(x_sample, c_sample, w_ada, b_ada, norm1_w, w_in, conv_w, conv_b, filt_w1, filt_b1,
                       filt_w2, filt_b2, filt_w3, filt_b3, filt_w4, filt_freq, hyena_bias, hyena_norm_w,
                       lambda_q1, lambda_k1, lambda_q2, lambda_k2, subln_w, w_out, norm2_w, w_mlp1,
                       w_mlp2, final_w)
    return (y_prompt, y_sample)
```

```python
import contextlib
import math
import numpy as np
import ml_dtypes
import concourse.bass as bass
import concourse.mybir as mybir
from concourse.bass_utils import run_bass_kernel_spmd

F32 = mybir.dt.float32
BF16 = mybir.dt.bfloat16
AF = mybir.ActivationFunctionType
ALU = mybir.AluOpType

D = 1024
NCH = 8
HW = 512
DFF = 4096
NORM_EPS = 1e-6
SUBLN_EPS = 1e-5
ROT_DIM = 16
ROPE_THETA = 500000.0
FILTER_EMB = 33
FORDER = 64


class Buf:
    __slots__ = ("w", "r", "pr")

    def __init__(self):
        self.w = {}
        self.r = {}
        self.pr = {}


class Eng:
    def __init__(self, name, eng, sem):
        self.name, self.eng, self.sem = name, eng, sem
        self.count = 0
        self.waited = {}

    def wait(self, sem, val):
        k = id(sem)
        if self.waited.get(k, 0) >= val:
            return
        self.eng.wait_ge(sem, val)
        self.waited[k] = val


class K:
    def __init__(self, nc, es):
        self.nc = nc
        self.es = es
        self.sems = {}
        mk = lambda n: es.enter_context(nc.semaphore(n))
        self.pe = Eng("pe", nc.tensor, mk("s_pe"))
        self.act = Eng("act", nc.scalar, mk("s_act"))
        self.dve = Eng("dve", nc.vector, mk("s_dve"))
        self.pool = Eng("pool", nc.gpsimd, mk("s_pool"))
        self.sp = Eng("sp", nc.sync, mk("s_sp"))
        self.engs = [self.pe, self.act, self.dve, self.pool, self.sp]
        self.nq = 8
        self.dq = {}
        for q, e in (("sp", self.sp), ("pool", self.pool)):
            self.dq[q] = dict(eng=e, sems=[mk(f"d_{q}{i}") for i in range(self.nq)], idx=0)
        self.all_dma_events = {}

    def _deps(self, E, reads, writes, disjoint):
        evs = {}

        def add(d):
            for k, (s, v) in d.items():
                if k not in evs or evs[k][1] < v:
                    evs[k] = (s, v)
        for b in reads:
            add(b.w)
        for b in writes:
            add(b.r)
            add(b.pr)
            if not disjoint:
                add(b.w)
        for k, (s, v) in evs.items():
            if E is self.pe and s is self.pe.sem:
                continue
            E.wait(s, v)

    def _commit(self, sem, val, reads, writes):
        k = id(sem)
        for b in reads:
            b.r[k] = (sem, val)
        for b in writes:
            if b.r:
                b.pr = b.r
                b.r = {}
                b.w = {}
            b.w[k] = (sem, val)

    def op(self, E, fn, reads=(), writes=(), disjoint=False):
        self._deps(E, reads, writes, disjoint)
        ins = fn()
        E.count += 1
        ins.then_inc(E.sem, 1)
        self._commit(E.sem, E.count, reads, writes)
        return ins

    def dma(self, q, out, in_, reads=(), writes=(), disjoint=False, **kw):
        Q = self.dq[q]
        E = Q["eng"]
        slot = Q["idx"] % self.nq
        gen = Q["idx"] // self.nq
        Q["idx"] += 1
        sem = Q["sems"][slot]
        E.wait(sem, 16 * gen)
        self._deps(E, reads, writes, disjoint)
        E.eng.dma_start(out=out, in_=in_, **kw).then_inc(sem, 16)
        self._commit(sem, 16 * (gen + 1), reads, writes)
        self.all_dma_events[id(sem)] = (sem, 16 * (gen + 1))

    def barrier(self):
        for E in self.engs:
            for X in self.engs:
                if X is not E and X.count > 0:
                    E.wait(X.sem, X.count)
            for (s, v) in self.all_dma_events.values():
                E.wait(s, v)

    def final_wait(self):
        for (s, v) in self.all_dma_events.values():
            self.sp.wait(s, v)
        for X in self.engs:
            if X is not self.sp and X.count > 0:
                self.sp.wait(X.sem, X.count)


def bcast_rows(ap_row, nparts=128):
    return ap_row.partition_broadcast(nparts)


class Cfg:
    def __init__(self, Lp=16384, Ls=2048, NS=4, NQ=4, debug=False):
        self.Lp, self.Ls, self.NS, self.NQ = Lp, Ls, NS, NQ
        self.own_p = Lp // NQ
        self.nseq = 1 + NS
        self.L = [Lp] + [Ls] * NS
        self.own = [self.own_p] + [Ls] * NS
        self.debug = debug
        self.foff = np.concatenate([[0], np.cumsum(self.L)]).astype(int)
        self.ooff = np.concatenate([[0], np.cumsum(self.own)]).astype(int)
        self.TF = int(self.foff[-1])
        self.TO = int(self.ooff[-1])


def rope_tables(positions):
    inv_freq = (ROPE_THETA ** (-np.arange(0, ROT_DIM, 2, dtype=np.float32) / ROT_DIM)).astype(np.float32)
    ang = positions.astype(np.float32)[:, None] * inv_freq[None, :]
    cos, sin = np.cos(ang).astype(np.float32), np.sin(ang).astype(np.float32)
    n = positions.shape[0]
    C = np.ones((64, n), np.float32)
    S = np.zeros((64, n), np.float32)
    C[0:8] = cos.T
    C[8:16] = cos.T
    S[0:8] = -sin.T
    S[8:16] = sin.T
    return np.concatenate([C, C], 0), np.concatenate([S, S], 0)


def rot_perm():
    p = np.arange(64)
    p[0:8] = np.arange(8, 16)
    p[8:16] = np.arange(0, 8)
    return p


def filter_feats(L):
    f32 = np.float32
    t = np.linspace(0.0, 1.0, L, dtype=f32)[:, None]
    bands = (FILTER_EMB - 1) // 2
    w = (f32(2.0 * math.pi) * np.arange(L, dtype=f32)[:, None] / f32(L)).astype(f32)
    f = np.linspace(1e-4, bands - 1, bands, dtype=f32)[None, :]
    fw = (f * w).astype(f32)
    z = np.concatenate([t, np.cos(fw), -np.sin(fw)], axis=-1).astype(f32)
    min_decay = math.log(1e-2) / 1.5
    max_decay = math.log(1e-2) / 0.3
    deltas = np.linspace(min_decay, max_decay, HW, dtype=f32)
    decay = np.exp(-t * np.abs(deltas)[None, :]).astype(f32)
    return z, decay


class Ring:
    def __init__(self, tiles):
        self.tiles = tiles
        self.bufs = [Buf() for _ in tiles]
        self.i = 0

    def next(self):
        t, b = self.tiles[self.i % len(self.tiles)], self.bufs[self.i % len(self.tiles)]
        self.i += 1
        return t, b


_UID = [0]


def sb(nc, es, name, shape, dt):
    _UID[0] += 1
    return es.enter_context(nc.sbuf_tensor(f"sb{_UID[0]}_{name}", list(shape), dt))


def pm(nc, es, name, shape, dt):
    _UID[0] += 1
    return es.enter_context(nc.psum_tensor(f"ps{_UID[0]}_{name}", list(shape), dt))


WSPEC = [("w_in", D, 3072), ("w_perm", D, 1024), ("w_out", D, D), ("w_mlp1", D, DFF), ("w_mlp2", DFF, D)]


def declare(nc, cfg):
    T = {}
    dbg = cfg.debug

    def inp(name, shape, dt=F32):
        T[name] = nc.dram_tensor(name, list(shape), dt, kind="ExternalInput").ap()

    def scr(name, shape, dt=BF16, out=False):
        kind = "ExternalOutput" if (out or (dbg and name in cfg.debug)) else "Internal"
        T[name] = nc.dram_tensor(name, list(shape), dt, kind=kind).ap()

    ns = cfg.nseq
    inp("xf", [cfg.TF, D]); inp("xo", [cfg.TO, D])
    inp("cT", [128, NCH, ns]); inp("w_ada", [D, 6 * D]); inp("b_ada", [1, 6 * D]); inp("b_adaT", [128, 48])
    inp("nw1T", [128, NCH]); inp("nw2T", [128, NCH]); inp("hnwT", [128, 4])
    for n, kd, nn in WSPEC:
        inp(n, [kd, nn])
        scr(n + "_b", [128, kd // 128, nn])
    inp("ident", [128, 128])
    inp("ropeF_c", [128, cfg.TF]); inp("ropeF_s", [128, cfg.TF])
    inp("ropeO_c", [128, cfg.TO]); inp("ropeO_s", [128, cfg.TO])
    inp("conv_w", [3, 1536]); inp("conv_b", [1, 1536])
    inp("hyena_bias", [1, HW]); inp("final_w", [1, D]); inp("subln_w", [1, 128])
    inp("lam4", [4, 64])
    scr("Gd", [ns, 2, D], F32)
    scr("U", [cfg.TF + 2 * ns, 1536])
    scr("Vd", [cfg.TF, 512])
    scr("KT", [512, cfg.TF])
    scr("QT", [512, cfg.TO])
    if dbg and "Yh_in" in cfg.debug:
        inp("Yh", [cfg.TO, 512])
    else:
        scr("Yh", [cfg.TO, 512], F32)
    scr("Oa", [cfg.TO, 512])
    scr("X1", [cfg.TO, D], F32)
    declare_hyena(nc, cfg, T, inp, scr)
    T["y"] = nc.dram_tensor("y", [cfg.TO, D], F32, kind="ExternalOutput").ap()
    return T


def phase_weights(k, cfg, T):
    nc = k.nc
    with contextlib.ExitStack() as es:
        st = Ring([sb(nc, es, f"wst{i}", [128, 8, 512], F32) for i in range(2)])
        cb = Ring([sb(nc, es, f"wcb{i}", [128, 8, 512], BF16) for i in range(2)])
        i = 0
        for name, kd, nn in WSPEC:
            src = T[name].rearrange("(j p) c -> p j c", p=128)
            dst = T[name + "_b"]
            for j0 in range(0, kd // 128, 8):
                for c0 in range(0, nn, 512):
                    s_t, s_b = st.next()
                    c_t, c_b = cb.next()
                    k.dma("sp", s_t[:], src[:, j0:j0 + 8, c0:c0 + 512], writes=[s_b])
                    E = k.dve if i % 2 == 0 else k.act
                    if E is k.dve:
                        k.op(E, lambda: nc.vector.tensor_copy(out=c_t[:], in_=s_t[:]), reads=[s_b], writes=[c_b])
                    else:
                        k.op(E, lambda: nc.scalar.copy(out=c_t[:], in_=s_t[:]), reads=[s_b], writes=[c_b])
                    k.dma("pool", dst[:, j0:j0 + 8, c0:c0 + 512], c_t[:], reads=[c_b])
                    i += 1
    k.barrier()


def phase_mod(k, cfg, T, G):
    nc = k.nc
    ns = cfg.nseq
    modT, a1, a2 = G["modT"], G["a1"], G["a2"]
    gb = G["gbuf"]
    with contextlib.ExitStack() as es:
        cT = sb(nc, es, "cT", [128, NCH, ns], F32)
        scT = sb(nc, es, "scT", [128, NCH, ns], F32)
        bT = sb(nc, es, "bT", [128, 48], F32)
        n1 = sb(nc, es, "n1", [128, NCH], F32)
        n2 = sb(nc, es, "n2", [128, NCH], F32)
        brow = sb(nc, es, "brow", [1, 6 * D], F32)
        grow = Ring([sb(nc, es, f"grow{i}", [1, 512], F32) for i in range(2)])
        wr = Ring([sb(nc, es, f"wada{i}", [128, 8, 512], F32) for i in range(2)])
        psm = pm(nc, es, "psm", [128, 48, ns], F32)
        psg = Ring([pm(nc, es, f"psg{i}", [1, 512], F32) for i in range(2)])
        b_c, b_s, b_b, b_n, b_br, b_psm = Buf(), Buf(), Buf(), Buf(), Buf(), Buf()
        k.dma("sp", cT[:], T["cT"][:, :, :], writes=[b_c])
        k.dma("sp", bT[:], T["b_adaT"][:, :], writes=[b_b])
        k.dma("sp", n1[:], T["nw1T"][:, :], writes=[b_n])
        k.dma("sp", n2[:], T["nw2T"][:, :], writes=[b_n], disjoint=True)
        k.dma("sp", brow[:], T["b_ada"][:, :], writes=[b_br])
        k.op(k.act, lambda: nc.scalar.activation(out=scT[:], in_=cT[:], func=AF.Silu), reads=[b_c], writes=[b_s])
        wsrc = T["w_ada"].rearrange("(j p) c -> p j c", p=128)
        for pc in range(12):
            w_t, w_b = wr.next()
            k.dma("sp", w_t[:], wsrc[:, :, pc * 512:(pc + 1) * 512], writes=[w_b])
            for mm in range(4):
                m = pc * 4 + mm
                for j in range(NCH):
                    k.op(k.pe, lambda: nc.tensor.matmul(psm[:, m, :], lhsT=w_t[:, j, mm * 128:(mm + 1) * 128],
                                                        rhs=scT[:, j, :], start=(j == 0), stop=(j == NCH - 1)),
                         reads=[w_b, b_s], writes=[b_psm], disjoint=True)
            which = {4: (0, 0), 5: (0, 1), 10: (1, 0), 11: (1, 1)}.get(pc)
            if which is not None:
                gi, half = which
                for s in range(ns):
                    p_t, p_b = psg.next()
                    g_t, g_b = grow.next()
                    for j in range(NCH):
                        k.op(k.pe, lambda: nc.tensor.matmul(p_t[:, :], lhsT=scT[:, j, s:s + 1], rhs=w_t[:, j, :],
                                                            start=(j == 0), stop=(j == NCH - 1)),
                             reads=[w_b, b_s], writes=[p_b])
                    k.op(k.dve, lambda: nc.vector.tensor_tensor(out=g_t[:], in0=p_t[:, :],
                                                                in1=brow[:, pc * 512:(pc + 1) * 512], op=ALU.add),
                         reads=[p_b, b_br], writes=[g_b])
                    k.dma("pool", T["Gd"][s, gi:gi + 1, half * 512:(half + 1) * 512], g_t[:], reads=[g_b])
        for s in range(ns):
            k.op(k.dve, lambda: nc.vector.tensor_tensor(out=modT[:, :, s], in0=psm[:, :, s], in1=bT[:, :], op=ALU.add),
                 reads=[b_psm, b_b], writes=[gb], disjoint=True)
        for s in range(ns):
            k.op(k.dve, lambda: nc.vector.scalar_tensor_tensor(out=a1[:, :, s], in0=modT[:, 8:16, s], scalar=1.0,
                                                               in1=n1[:, :], op0=ALU.add, op1=ALU.mult),
                 reads=[gb, b_n], writes=[gb], disjoint=True)
            k.op(k.dve, lambda: nc.vector.scalar_tensor_tensor(out=a2[:, :, s], in0=modT[:, 32:40, s], scalar=1.0,
                                                               in1=n2[:, :], op0=ALU.add, op1=ALU.mult),
                 reads=[gb, b_n], writes=[gb], disjoint=True)
    k.barrier()


def rms_scale_rows(k, x_t, x_b, ss_t, rs_t, sc_b, junk_t, junk_b, nblk, width, eps, G):
    nc = k.nc
    for tb in range(nblk):
        k.op(k.dve, lambda: nc.vector.scalar_tensor_tensor(out=junk_t[:, 0:width], in0=x_t[:, tb, :], scalar=1.0,
                                                           in1=x_t[:, tb, :], op0=ALU.mult, op1=ALU.mult,
                                                           accum_out=ss_t[:, tb:tb + 1]),
             reads=[x_b], writes=[junk_b, sc_b])
    k.op(k.pool, lambda: nc.gpsimd.tensor_scalar(out=ss_t[:, 0:nblk], in0=ss_t[:, 0:nblk], scalar1=1.0 / width,
                                                 scalar2=eps, op0=ALU.mult, op1=ALU.add),
         reads=[sc_b], writes=[sc_b])
    k.op(k.pool, lambda: nc.gpsimd.tensor_tensor(out=rs_t[:, 0:nblk], in0=ss_t[:, 0:nblk], in1=G["mhalf"][:, 0:nblk],
                                                 op=ALU.pow),
         reads=[sc_b], writes=[sc_b])


def phase_proj(k, cfg, T, G, which):
    nc = k.nc
    full = which == "F"
    xsrc = T["xf"] if full else T["xo"]
    lens = cfg.L if full else cfg.own
    offs = cfg.foff if full else cfg.ooff
    rc, rs_ = (T["ropeF_c"], T["ropeF_s"]) if full else (T["ropeO_c"], T["ropeO_s"])
    ntm = 2048 if full else 0
    with contextlib.ExitStack() as es:
        ident = G["ident"]
        if full:
            wtm = sb(nc, es, "wtm", [128, NCH, 2048], BF16)
        wfm = sb(nc, es, "wfm", [128, NCH, 512], BF16)
        wfp = sb(nc, es, "wfp", [128, NCH, 512], BF16)
        b_w = Buf()
        wb = T["w_in_b"]
        if full:
            k.dma("sp", wtm[:, :, 0:1536], wb[:, :, 0:1536], writes=[b_w])
            k.dma("sp", wtm[:, :, 1536:2048], wb[:, :, 2560:3072], writes=[b_w], disjoint=True)
            k.dma("sp", wfm[:], wb[:, :, 2048:2560], writes=[b_w], disjoint=True)
            k.dma("sp", wfp[:], T["w_perm_b"][:, :, 512:1024], writes=[b_w], disjoint=True)
        else:
            k.dma("sp", wfm[:], wb[:, :, 1536:2048], writes=[b_w])
            k.dma("sp", wfp[:], T["w_perm_b"][:, :, 0:512], writes=[b_w], disjoint=True)
        xr = Ring([sb(nc, es, f"px{i}", [128, 4, D], F32) for i in range(2)])
        xs = Ring([sb(nc, es, f"pxs{i}", [128, 4, D], BF16) for i in range(2)])
        hT = Ring([sb(nc, es, f"phT{i}", [128, NCH, 512], BF16) for i in range(2)])
        ss = Ring([sb(nc, es, f"pss{i}", [128, 8], F32) for i in range(2)])
        rsd = Ring([sb(nc, es, f"prs{i}", [128, 8], F32) for i in range(2)])
        junk = Ring([sb(nc, es, f"pjk{i}", [128, D], F32) for i in range(1)])
        ct = Ring([sb(nc, es, f"pct{i}", [128, 512], F32) for i in range(2)])
        st_ = Ring([sb(nc, es, f"pst{i}", [128, 512], F32) for i in range(2)])
        t1 = Ring([sb(nc, es, f"pt1{i}", [128, 512], F32) for i in range(2)])
        t2 = Ring([sb(nc, es, f"pt2{i}", [128, 512], F32) for i in range(2)])
        fo = Ring([sb(nc, es, f"pfo{i}", [128, 512], BF16) for i in range(3)])
        if full:
            so = Ring([sb(nc, es, f"pso{i}", [128, 4, 2048], BF16) for i in range(2)])
            zt = sb(nc, es, "pzero", [1, 1536], BF16)
            b_z = Buf()
            k.op(k.dve, lambda: nc.vector.memset(zt[:], 0.0), writes=[b_z])
        tp = Ring([pm(nc, es, f"ptp{i}", [128, 2, 512], BF16) for i in range(2)])
        mm = Ring([pm(nc, es, f"pmm{i}", [128, 512], F32) for i in range(6)])
        dst_fm = T["KT"] if full else T["QT"]

        tiles = [(s, t0) for s in range(cfg.nseq) for t0 in range(0, lens[s], 512)]

        def load(idx):
            s, t0 = tiles[idx]
            x_t, x_b = xr.next()
            r0 = int(offs[s]) + t0
            k.dma("sp", x_t[:], xsrc[r0:r0 + 512, :].rearrange("(tb p) c -> p tb c", p=128), writes=[x_b])
            c_t, c_b = ct.next()
            s_t, s_b = st_.next()
            k.dma("sp", c_t[:], rc[:, r0:r0 + 512], writes=[c_b])
            k.dma("sp", s_t[:], rs_[:, r0:r0 + 512], writes=[s_b])
            return (x_t, x_b, c_t, c_b, s_t, s_b)

        nxt = load(0)
        ev = 0
        for idx, (s, t0) in enumerate(tiles):
            x_t, x_b, c_t, c_b, s_t, s_b = nxt
            if idx + 1 < len(tiles):
                nxt = load(idx + 1)
            if full and t0 == 0:
                ub = int(offs[s]) + 2 * s
                k.dma("pool", T["U"][ub:ub + 1, :], zt[:], reads=[b_z])
                k.dma("pool", T["U"][ub + 1 + lens[s]:ub + 2 + lens[s], :], zt[:], reads=[b_z])
            ss_t, sc_b = ss.next()
            rs_t, _ = rsd.next()
            j_t, j_b = junk.next()
            rms_scale_rows(k, x_t, x_b, ss_t, rs_t, sc_b, j_t, j_b, 4, D, NORM_EPS, G)
            xs_t, xs_b = xs.next()
            for tb in range(4):
                k.op(k.dve, lambda: nc.vector.tensor_scalar(out=xs_t[:, tb, :], in0=x_t[:, tb, :], scalar1=rs_t[:, tb:tb + 1],
                                                            scalar2=None, op0=ALU.mult),
                     reads=[x_b, sc_b], writes=[xs_b], disjoint=True)
            h_t, h_b = hT.next()
            for jj in range(0, NCH, 2):
                p_t, p_b = tp.next()
                for c in range(2):
                    for tb in range(4):
                        k.op(k.pe, lambda: nc.tensor.transpose(out=p_t[:, c, tb * 128:(tb + 1) * 128],
                                                               in_=xs_t[:, tb, (jj + c) * 128:(jj + c + 1) * 128],
                                                               identity=ident[:]),
                             reads=[xs_b], writes=[p_b], disjoint=True)
                for c in range(2):
                    j = jj + c
                    k.op(k.act, lambda: nc.scalar.activation(out=h_t[:, j, :], in_=p_t[:, c, :], func=AF.Identity,
                                                             scale=G["a1"][:, j, s:s + 1], bias=G["modT"][:, j, s:s + 1]),
                         reads=[p_b, G["gbuf"]], writes=[h_b], disjoint=True)
            if full:
                so_t, so_b = so.next()
                for tb in range(4):
                    for cc in range(4):
                        m_t, m_b = mm.next()
                        for j in range(NCH):
                            k.op(k.pe, lambda: nc.tensor.matmul(m_t[:, :], lhsT=h_t[:, j, tb * 128:(tb + 1) * 128],
                                                                rhs=wtm[:, j, cc * 512:(cc + 1) * 512],
                                                                start=(j == 0), stop=(j == NCH - 1)),
                                 reads=[h_b, b_w], writes=[m_b])
                        if ev % 2 == 0:
                            k.op(k.act, lambda: nc.scalar.copy(out=so_t[:, tb, cc * 512:(cc + 1) * 512], in_=m_t[:, :]),
                                 reads=[m_b], writes=[so_b], disjoint=True)
                        else:
                            k.op(k.dve, lambda: nc.vector.tensor_copy(out=so_t[:, tb, cc * 512:(cc + 1) * 512], in_=m_t[:, :]),
                                 reads=[m_b], writes=[so_b], disjoint=True)
                        ev += 1
                r0 = int(offs[s]) + t0
                ub = r0 + 2 * s + 1
                k.dma("pool", T["U"][ub:ub + 512, :].rearrange("(tb p) c -> p tb c", p=128), so_t[:, :, 0:1536], reads=[so_b])
                k.dma("pool", T["Vd"][r0:r0 + 512, :].rearrange("(tb p) c -> p tb c", p=128), so_t[:, :, 1536:2048], reads=[so_b])
            for n in range(4):
                m1_t, m1_b = mm.next()
                m2_t, m2_b = mm.next()
                for j in range(NCH):
                    k.op(k.pe, lambda: nc.tensor.matmul(m1_t[:, :], lhsT=wfm[:, j, n * 128:(n + 1) * 128], rhs=h_t[:, j, :],
                                                        start=(j == 0), stop=(j == NCH - 1)),
                         reads=[h_b, b_w], writes=[m1_b])
                for j in range(NCH):
                    k.op(k.pe, lambda: nc.tensor.matmul(m2_t[:, :], lhsT=wfp[:, j, n * 128:(n + 1) * 128], rhs=h_t[:, j, :],
                                                        start=(j == 0), stop=(j == NCH - 1)),
                         reads=[h_b, b_w], writes=[m2_b])
                a_t, a_b = t1.next()
                b_t, b_b = t2.next()
                f_t, f_b = fo.next()
                k.op(k.dve, lambda: nc.vector.tensor_tensor(out=a_t[:], in0=m1_t[:, :], in1=c_t[:], op=ALU.mult),
                     reads=[m1_b, c_b], writes=[a_b])
                k.op(k.dve, lambda: nc.vector.tensor_tensor(out=b_t[:], in0=m2_t[:, :], in1=s_t[:], op=ALU.mult),
                     reads=[m2_b, s_b], writes=[b_b])
                k.op(k.pool, lambda: nc.gpsimd.tensor_tensor(out=f_t[:], in0=a_t[:], in1=b_t[:], op=ALU.add),
                     reads=[a_b, b_b], writes=[f_b])
                r0 = int(offs[s]) + t0
                k.dma("pool", dst_fm[n * 128:(n + 1) * 128, r0:r0 + 512], f_t[:], reads=[f_b])
    k.barrier()


PHASES = ["weights", "mod", "projF", "projO", "filter", "hyena", "attn", "m1", "m2"]


def build(cfg, upto="m2"):
    nc = bass.Bass("TRN2", target_bir_lowering=False)
    T = declare(nc, cfg)
    last = PHASES.index(upto)
    with contextlib.ExitStack() as es:
        k = K(nc, es)
        ns = cfg.nseq
        G = dict(gbuf=Buf(), kh_buf={"p": Buf(), "s": Buf()})
        G["modT"] = sb(nc, es, "modT", [128, 48, ns], F32)
        G["a1"] = sb(nc, es, "a1", [128, NCH, ns], F32)
        G["a2"] = sb(nc, es, "a2", [128, NCH, ns], F32)
        G["ident"] = sb(nc, es, "ident", [128, 128], BF16)
        G["identf"] = sb(nc, es, "identf", [128, 128], F32)
        G["mhalf"] = sb(nc, es, "mhalf", [128, 8], F32)
        k.dma("sp", G["identf"][:], T["ident"][:, :], writes=[G["gbuf"]])
        k.op(k.dve, lambda: nc.vector.tensor_copy(out=G["ident"][:], in_=G["identf"][:]), reads=[G["gbuf"]], writes=[G["gbuf"]])
        k.op(k.dve, lambda: nc.vector.memset(G["mhalf"][:], -0.5), writes=[G["gbuf"]], disjoint=True)
        k.barrier()
        steps = [
            lambda: phase_weights(k, cfg, T),
            lambda: phase_mod(k, cfg, T, G),
            lambda: phase_proj(k, cfg, T, G, "F"),
            lambda: phase_proj(k, cfg, T, G, "O"),
            lambda: phase_filter(k, cfg, T, G),
            lambda: phase_hyena(k, cfg, T, G),
            lambda: phase_attn(k, cfg, T, G),
            lambda: phase_m1(k, cfg, T, G),
            lambda: phase_m2(k, cfg, T, G),
        ]
        for i, st in enumerate(steps):
            if i <= last:
                st()
        k.final_wait()
    return nc


def chunkT(v, ncols):
    return np.ascontiguousarray(np.asarray(v, np.float32).reshape(ncols, 128).T)


def host_inputs(cfg, inp, core):
    f32 = np.float32
    b, r = core // cfg.NQ, core % cfg.NQ
    own = cfg.own_p
    xs = [np.asarray(inp["x_sample"][core * cfg.NS + i], f32) for i in range(cfg.NS)]
    xp = np.asarray(inp["x_prompt"][b], f32)
    m = {}
    m["xf"] = np.ascontiguousarray(np.concatenate([xp] + xs, 0))
    m["xo"] = np.ascontiguousarray(np.concatenate([xp[r * own:(r + 1) * own]] + xs, 0))
    cs = np.stack([np.asarray(inp["c_prompt"][b], f32)] + [np.asarray(inp["c_sample"][core * cfg.NS + i], f32)
                                                             for i in range(cfg.NS)], 0)
    m["cT"] = np.ascontiguousarray(cs.reshape(cfg.nseq, NCH, 128).transpose(2, 1, 0))
    m["w_ada"] = np.ascontiguousarray(inp["w_ada"][0], f32)
    m["b_ada"] = np.ascontiguousarray(inp["b_ada"][0:1], f32)
    m["b_adaT"] = chunkT(inp["b_ada"][0], 48)
    m["nw1T"] = chunkT(inp["norm1_w"][0], NCH)
    m["nw2T"] = chunkT(inp["norm2_w"][0], NCH)
    m["hnwT"] = chunkT(inp["hyena_norm_w"][0], 4)
    w_in = np.asarray(inp["w_in"][0], f32)
    m["w_in"] = np.ascontiguousarray(w_in)
    p64 = rot_perm()
    pq = np.concatenate([1536 + h * 64 + p64 for h in range(8)])
    pk = np.concatenate([2048 + h * 64 + p64 for h in range(8)])
    m["w_perm"] = np.ascontiguousarray(w_in[:, np.concatenate([pq, pk])])
    m["w_out"] = np.ascontiguousarray(inp["w_out"][0], f32)
    m["w_mlp1"] = np.ascontiguousarray(inp["w_mlp1"][0], f32)
    m["w_mlp2"] = np.ascontiguousarray(inp["w_mlp2"][0], f32)
    m["ident"] = np.eye(128, dtype=f32)
    posF = np.concatenate([np.arange(L) for L in cfg.L])
    posO = np.concatenate([r * own + np.arange(own)] + [np.arange(cfg.Ls)] * cfg.NS)
    m["ropeF_c"], m["ropeF_s"] = rope_tables(posF)
    m["ropeO_c"], m["ropeO_s"] = rope_tables(posO)
    m["conv_w"] = np.ascontiguousarray(inp["conv_w"][0], f32)
    m["conv_b"] = np.ascontiguousarray(inp["conv_b"][0:1], f32)
    m["hyena_bias"] = np.ascontiguousarray(inp["hyena_bias"][0:1], f32)
    m["final_w"] = np.ascontiguousarray(np.asarray(inp["final_w"], f32)[None, :])
    m["subln_w"] = np.ascontiguousarray(inp["subln_w"][0:1], f32)
    m["lam4"] = np.ascontiguousarray(np.stack([inp["lambda_q1"][0], inp["lambda_k1"][0], inp["lambda_q2"][0],
                                               inp["lambda_k2"][0]], 0), f32)
    if cfg.debug and "Yh_in" in cfg.debug:
        m["Yh"] = np.ascontiguousarray(inp["_Yh"][core], f32)
    host_tables(cfg, inp, core, m)
    return m


def run(cfg, inp, upto="m2", ncores=8):
    nc = build(cfg, upto)
    maps = [host_inputs(cfg, inp, c) for c in range(ncores)]
    names = set()
    res = run_bass_kernel_spmd(nc, maps, core_ids=list(range(ncores)))
    return res.results


def kernel(**inputs):
    inp = {k_: np.asarray(v) for k_, v in inputs.items()}
    cfg = Cfg()
    res = run(cfg, inp)
    B, S, _ = inp["x_prompt"].shape
    yp = np.zeros((B, S, D), np.float32)
    ysm = np.zeros(inp["x_sample"].shape, np.float32)
    own = cfg.own_p
    for c in range(8):
        y = res[c]["y"]
        b, r = c // cfg.NQ, c % cfg.NQ
        yp[b, r * own:(r + 1) * own] = y[0:own]
        for i in range(cfg.NS):
            ysm[c * cfg.NS + i] = y[own + i * cfg.Ls: own + (i + 1) * cfg.Ls]
    return (yp, ysm)


LAMBDA_INIT = 0.8 - 0.6 * math.exp(-0.3 * 0)


def bc_ap(ap2d, nparts=128):
    n = 1
    for d in ap2d.shape:
        n *= d
    return bass.AP(tensor=ap2d.tensor, offset=ap2d.offset, ap=[[0, nparts], [1, n]])


def phase_attn(k, cfg, T, G):
    nc = k.nc
    Lmax, omax = max(cfg.L), max(cfg.own)
    with contextlib.ExitStack() as es:
        lamt = sb(nc, es, "lamt", [128, 256], F32)
        lj = sb(nc, es, "lj", [128, 64], F32)
        lacc = sb(nc, es, "lacc", [128, 4], F32)
        nlam = sb(nc, es, "nlam", [128, 1], F32)
        slw = sb(nc, es, "slw", [128, 128], F32)
        b_l, b_sl = Buf(), Buf()
        k.dma("sp", lamt[:], bc_ap(T["lam4"]), writes=[b_l])
        k.dma("sp", slw[:], bc_ap(T["subln_w"]), writes=[b_sl])
        k.op(k.dve, lambda: nc.vector.scalar_tensor_tensor(out=lj[:], in0=lamt[:, 0:64], scalar=1.0, in1=lamt[:, 64:128],
                                                           op0=ALU.mult, op1=ALU.mult, accum_out=lacc[:, 0:1]),
             reads=[b_l], writes=[b_l])
        k.op(k.dve, lambda: nc.vector.scalar_tensor_tensor(out=lj[:], in0=lamt[:, 128:192], scalar=1.0, in1=lamt[:, 192:256],
                                                           op0=ALU.mult, op1=ALU.mult, accum_out=lacc[:, 1:2]),
             reads=[b_l], writes=[b_l])
        k.op(k.act, lambda: nc.scalar.activation(out=lacc[:, 2:4], in_=lacc[:, 0:2], func=AF.Exp), reads=[b_l], writes=[b_l])
        k.op(k.dve, lambda: nc.vector.tensor_tensor(out=nlam[:], in0=lacc[:, 3:4], in1=lacc[:, 2:3], op=ALU.subtract),
             reads=[b_l], writes=[b_l])
        k.op(k.dve, lambda: nc.vector.tensor_scalar(out=nlam[:], in0=nlam[:], scalar1=-LAMBDA_INIT, scalar2=None, op0=ALU.add),
             reads=[b_l], writes=[b_l])
        k.op(k.dve, lambda: nc.vector.tensor_scalar(out=slw[:], in0=slw[:], scalar1=(1.0 - LAMBDA_INIT), scalar2=None, op0=ALU.mult),
             reads=[b_sl], writes=[b_sl])

        ktr = Ring([sb(nc, es, f"akt{i}", [128, Lmax], BF16) for i in range(2)])
        v1r = Ring([sb(nc, es, f"av1{i}", [128, Lmax // 128, 128], BF16) for i in range(2)])
        qtr = Ring([sb(nc, es, f"aqt{i}", [128, omax], BF16) for i in range(2)])
        er = Ring([sb(nc, es, f"ae{i}", [128, 512], BF16) for i in range(6)])
        osr = Ring([sb(nc, es, f"aos{i}", [128, 4, 128], BF16) for i in range(2)])
        o_r = Ring([sb(nc, es, f"ao{i}", [128, 128], F32) for i in range(2)])
        jk = sb(nc, es, "ajk", [128, 128], F32)
        b_jk = Buf()
        st = Ring([sb(nc, es, f"ast{i}", [128, 8], F32) for i in range(2)])
        zacc = [sb(nc, es, f"azc{i}", [128, 512], F32) for i in range(4)]
        zb = [Buf(), Buf(), Buf(), Buf()]
        otr = Ring([sb(nc, es, f"aot{i}", [128, 2, 512], F32) for i in range(2)])
        ones1 = sb(nc, es, "aones", [128, 1], F32)
        k.op(k.dve, lambda: nc.vector.memset(ones1[:], 1.0), writes=[b_l], disjoint=True)
        sbank = Ring([pm(nc, es, f"asb{i}", [128, 512], F32) for i in range(3)])
        obank = [pm(nc, es, f"aob{i}", [128, 512], F32) for i in range(2)]
        ob_b = [Buf(), Buf()]
        ebank = [pm(nc, es, f"aeb{i}", [128, 512], F32) for i in range(3)]
        regs = Ring([ebank[b][:, c * 129:(c + 1) * 129] for (b, c) in ((0, 0), (0, 1), (0, 2), (1, 0), (1, 1), (1, 2), (2, 0), (2, 1))])
        fence_b = Buf()

        for s in range(cfg.nseq):
            L, own = cfg.L[s], cfg.own[s]
            f0, o0 = int(cfg.foff[s]), int(cfg.ooff[s])
            nkb = L // 128
            for h in range(4):
                kt, kt_b = ktr.next()
                v1, v1_b = v1r.next()
                qt, qt_b = qtr.next()
                k.dma("sp", kt[:, 0:L], T["KT"][h * 128:(h + 1) * 128, f0:f0 + L], writes=[kt_b])
                k.dma("sp", v1[:, 0:nkb, :], T["Vd"][f0:f0 + L, h * 128:(h + 1) * 128].rearrange("(kb p) c -> p kb c", p=128),
                      writes=[v1_b])
                k.dma("sp", qt[:, 0:own], T["QT"][h * 128:(h + 1) * 128, o0:o0 + own], writes=[qt_b])
                for qc in range(own // 512):
                    def qk(kb):
                        out = []
                        for e in range(2):
                            sp_, sp_b = sbank.next()
                            lo = e * 64
                            k.op(k.pe, lambda: nc.tensor.matmul(sp_[:, :], lhsT=kt[lo:lo + 64, kb * 128:(kb + 1) * 128],
                                                                rhs=qt[lo:lo + 64, qc * 512:(qc + 1) * 512], start=True, stop=True),
                                 reads=[kt_b, qt_b], writes=[sp_b])
                            e_t, e_b = er.next()
                            k.op(k.act, lambda: nc.scalar.activation(out=e_t[:], in_=sp_[:, :], func=AF.Exp, scale=0.125),
                                 reads=[sp_b], writes=[e_b])
                            out.append((e_t, e_b))
                        return out
                    pend = qk(0)
                    used_pool = False
                    used_pool0 = False
                    for kb in range(nkb):
                        nxt = qk(kb + 1) if kb + 1 < nkb else None
                        for e in range(2):
                            e_t, e_b = pend[e]
                            k.op(k.pe, lambda: nc.tensor.matmul(obank[e][:, :], lhsT=v1[:, kb, :], rhs=e_t[:], start=(kb == 0), stop=(kb == nkb - 1)),
                                 reads=[e_b, v1_b], writes=[ob_b[e]])
                            zi, E = e, k.dve
                            first = (kb == 0)
                            if first:
                                k.op(E, lambda: E.eng.tensor_copy(out=zacc[zi][:], in_=e_t[:]), reads=[e_b], writes=[zb[zi]])
                            else:
                                k.op(E, lambda: E.eng.tensor_tensor(out=zacc[zi][:], in0=zacc[zi][:], in1=e_t[:], op=ALU.add),
                                     reads=[e_b, zb[zi]], writes=[zb[zi]])
                        pend = nxt
                    ot, ot_b = otr.next()
                    for e in range(2):
                        k.op(k.act, lambda: nc.scalar.copy(out=ot[:, e, :], in_=obank[e][:, :]), reads=[ob_b[e]], writes=[ot_b], disjoint=True)
                    rr = []
                    for qs in range(4):
                        pair = []
                        for e in range(2):
                            rg, rg_b = regs.next()
                            k.op(k.pe, lambda: nc.tensor.transpose(out=rg[:, 0:128], in_=ot[:, e, qs * 128:(qs + 1) * 128], identity=G["identf"][:]),
                                 reads=[ot_b, G["gbuf"]], writes=[rg_b])
                            zl = ([0, 3] if used_pool0 else [0]) if e == 0 else ([1, 2] if used_pool else [1])
                            for zi_, zi in enumerate(zl):
                                k.op(k.pe, lambda: nc.tensor.matmul(rg[:, 128:129], lhsT=zacc[zi][:, qs * 128:(qs + 1) * 128], rhs=ones1[:, 0:1],
                                                                    start=(zi_ == 0), stop=(zi_ == len(zl) - 1), skip_group_check=True),
                                     reads=[zb[zi], b_l], writes=[rg_b], disjoint=True)
                            pair.append((rg, rg_b))
                        rr.append(pair)
                        if qs % 2 == 1:
                            k.op(k.pe, lambda: nc.tensor.matmul(ebank[2][:, 400:401], lhsT=G["identf"][:, :], rhs=ones1[:, 0:1], start=True, stop=True,
                                                                skip_group_check=True),
                                 reads=[b_l, G["gbuf"]], writes=[fence_b])
                    os_t, os_b = osr.next()
                    for qs in range(4):
                        (a1_, a1b), (a2_, a2b) = rr[qs]
                        s_t, s_b = st.next()
                        o_t, o_b = o_r.next()
                        k.op(k.dve, lambda: nc.vector.reciprocal(out=s_t[:, 0:1], in_=a1_[:, 128:129]), reads=[a1b, a2b, fence_b], writes=[s_b])
                        k.op(k.dve, lambda: nc.vector.reciprocal(out=s_t[:, 1:2], in_=a2_[:, 128:129]), reads=[a2b], writes=[s_b])
                        k.op(k.dve, lambda: nc.vector.tensor_tensor(out=s_t[:, 2:3], in0=s_t[:, 1:2], in1=nlam[:], op=ALU.mult),
                             reads=[s_b, b_l], writes=[s_b])
                        k.op(k.dve, lambda: nc.vector.tensor_scalar(out=o_t[:], in0=a1_[:, 0:128], scalar1=s_t[:, 0:1], scalar2=None,
                                                                    op0=ALU.mult), reads=[a1b, s_b], writes=[o_b])
                        k.op(k.dve, lambda: nc.vector.scalar_tensor_tensor(out=o_t[:], in0=a2_[:, 0:128], scalar=s_t[:, 2:3], in1=o_t[:],
                                                                           op0=ALU.mult, op1=ALU.add),
                             reads=[a2b, s_b, o_b], writes=[o_b])
                        k.op(k.dve, lambda: nc.vector.scalar_tensor_tensor(out=jk[:], in0=o_t[:], scalar=1.0, in1=o_t[:], op0=ALU.mult,
                                                                           op1=ALU.mult, accum_out=s_t[:, 3:4]),
                             reads=[o_b], writes=[b_jk, s_b])
                        k.op(k.pool, lambda: nc.gpsimd.tensor_scalar(out=s_t[:, 4:5], in0=s_t[:, 3:4], scalar1=1.0 / 128, scalar2=SUBLN_EPS,
                                                                     op0=ALU.mult, op1=ALU.add), reads=[s_b], writes=[s_b])
                        k.op(k.pool, lambda: nc.gpsimd.tensor_tensor(out=s_t[:, 5:6], in0=s_t[:, 4:5], in1=G["mhalf"][:, 0:1], op=ALU.pow),
                             reads=[s_b], writes=[s_b])
                        k.op(k.dve, lambda: nc.vector.scalar_tensor_tensor(out=os_t[:, qs, :], in0=o_t[:], scalar=s_t[:, 5:6], in1=slw[:],
                                                                           op0=ALU.mult, op1=ALU.mult),
                             reads=[o_b, s_b, b_sl], writes=[os_b], disjoint=True)
                    r0 = o0 + qc * 512
                    k.dma("pool", T["Oa"][r0:r0 + 512, h * 128:(h + 1) * 128].rearrange("(qs p) c -> p qs c", p=128), os_t[:], reads=[os_b])
    k.barrier()


def transposes_to(k, G, src_t, src_b, tp, dst_t, dst_b, evac):
    nc = k.nc
    for jj in range(0, NCH, 2):
        p_t, p_b = tp.next()
        for c in range(2):
            for tb in range(4):
                k.op(k.pe, lambda: nc.tensor.transpose(out=p_t[:, c, tb * 128:(tb + 1) * 128],
                                                       in_=src_t[:, tb, (jj + c) * 128:(jj + c + 1) * 128], identity=G["ident"][:]),
                     reads=[src_b], writes=[p_b], disjoint=True)
        for c in range(2):
            evac(jj + c, dst_t[:, jj + c, :], p_t[:, c, :], p_b, dst_b)


def phase_m1(k, cfg, T, G):
    nc = k.nc
    with contextlib.ExitStack() as es:
        wo = sb(nc, es, "wo", [128, NCH, D], BF16)
        hnw = sb(nc, es, "hnw", [128, 4], F32)
        b_w = Buf()
        k.dma("sp", wo[:], T["w_out_b"][:, :, :], writes=[b_w])
        k.dma("sp", hnw[:], T["hnwT"][:, :], writes=[b_w], disjoint=True)
        g1 = Ring([sb(nc, es, f"g1{i}", [128, D], F32) for i in range(2)])
        xr = Ring([sb(nc, es, f"mx{i}", [128, 4, D], F32) for i in range(2)])
        yr = Ring([sb(nc, es, f"my{i}", [128, 4, 512], F32) for i in range(2)])
        mix = Ring([sb(nc, es, f"mmix{i}", [128, 4, D], BF16) for i in range(2)])
        mT = Ring([sb(nc, es, f"mmT{i}", [128, NCH, 512], BF16) for i in range(2)])
        ss = Ring([sb(nc, es, f"mss{i}", [128, 8], F32) for i in range(2)])
        rsd = Ring([sb(nc, es, f"mrs{i}", [128, 8], F32) for i in range(2)])
        junk = Ring([sb(nc, es, "mjk", [128, D], F32)])
        tmp = Ring([sb(nc, es, f"mtmp{i}", [128, 512], F32) for i in range(2)])
        tp = Ring([pm(nc, es, f"mtp{i}", [128, 2, 512], BF16) for i in range(2)])
        mm = Ring([pm(nc, es, f"mmm{i}", [128, 512], F32) for i in range(4)])
        for s in range(cfg.nseq):
            g_t, g_b = g1.next()
            k.dma("sp", g_t[:], bc_ap(T["Gd"][s, 0:1, :]), writes=[g_b])
            for t0 in range(0, cfg.own[s], 512):
                r0 = int(cfg.ooff[s]) + t0
                x_t, x_b = xr.next()
                y_t, y_b = yr.next()
                m_t, m_b = mix.next()
                k.dma("sp", x_t[:], T["xo"][r0:r0 + 512, :].rearrange("(tb p) c -> p tb c", p=128), writes=[x_b])
                k.dma("sp", y_t[:], T["Yh"][r0:r0 + 512, :].rearrange("(tb p) c -> p tb c", p=128), writes=[y_b])
                k.dma("sp", m_t[:, :, 512:1024], T["Oa"][r0:r0 + 512, :].rearrange("(tb p) c -> p tb c", p=128), writes=[m_b])
                ss_t, sc_b = ss.next()
                rs_t, _ = rsd.next()
                j_t, j_b = junk.next()
                rms_scale_rows(k, y_t, y_b, ss_t, rs_t, sc_b, j_t, j_b, 4, 512, NORM_EPS, G)
                for tb in range(4):
                    k.op(k.dve, lambda: nc.vector.tensor_scalar(out=m_t[:, tb, 0:512], in0=y_t[:, tb, :], scalar1=rs_t[:, tb:tb + 1],
                                                                scalar2=None, op0=ALU.mult),
                         reads=[y_b, sc_b], writes=[m_b], disjoint=True)
                t_t, t_b = mT.next()

                def evac(j, o, i, pb, db):
                    if j < 4:
                        k.op(k.act, lambda: nc.scalar.activation(out=o, in_=i, func=AF.Identity, scale=hnw[:, j:j + 1]),
                             reads=[pb, b_w], writes=[db], disjoint=True)
                    else:
                        k.op(k.act, lambda: nc.scalar.copy(out=o, in_=i), reads=[pb], writes=[db], disjoint=True)
                transposes_to(k, G, m_t, m_b, tp, t_t, t_b, evac)
                for tb in range(4):
                    for cc in range(2):
                        p_t, p_b = mm.next()
                        for j in range(NCH):
                            k.op(k.pe, lambda: nc.tensor.matmul(p_t[:, :], lhsT=t_t[:, j, tb * 128:(tb + 1) * 128],
                                                                rhs=wo[:, j, cc * 512:(cc + 1) * 512], start=(j == 0), stop=(j == NCH - 1)),
                                 reads=[t_b, b_w], writes=[p_b])
                        q_t, q_b = tmp.next()
                        k.op(k.dve, lambda: nc.vector.tensor_tensor(out=q_t[:], in0=p_t[:, :], in1=g_t[:, cc * 512:(cc + 1) * 512], op=ALU.mult),
                             reads=[p_b, g_b], writes=[q_b])
                        k.op(k.pool, lambda: nc.gpsimd.tensor_tensor(out=x_t[:, tb, cc * 512:(cc + 1) * 512], in0=q_t[:],
                                                                     in1=x_t[:, tb, cc * 512:(cc + 1) * 512], op=ALU.add),
                             reads=[q_b, x_b], writes=[x_b], disjoint=True)
                k.dma("pool", T["X1"][r0:r0 + 512, :].rearrange("(tb p) c -> p tb c", p=128), x_t[:], reads=[x_b])
    k.barrier()


def phase_m2(k, cfg, T, G):
    nc = k.nc
    with contextlib.ExitStack() as es:
        w2 = sb(nc, es, "w2", [128, 32, D], BF16)
        fw = sb(nc, es, "fw", [128, D], F32)
        b_w = Buf()
        for i in range(4):
            k.dma("sp", w2[:, i * 8:(i + 1) * 8, :], T["w_mlp2_b"][:, i * 8:(i + 1) * 8, :], writes=[b_w], disjoint=True)
        k.dma("sp", fw[:], bc_ap(T["final_w"]), writes=[b_w], disjoint=True)
        w1r = Ring([sb(nc, es, f"w1r{i}", [128, NCH, 512], BF16) for i in range(3)])
        g2 = Ring([sb(nc, es, f"g2{i}", [128, D], F32) for i in range(2)])
        xr = Ring([sb(nc, es, f"nx{i}", [128, 4, D], F32) for i in range(2)])
        xs = Ring([sb(nc, es, f"nxs{i}", [128, 4, D], BF16) for i in range(1)])
        hT = Ring([sb(nc, es, f"nhT{i}", [128, NCH, 512], BF16) for i in range(1)])
        aT = Ring([sb(nc, es, f"naT{i}", [128, 32, 512], BF16) for i in range(1)])
        rl = Ring([sb(nc, es, f"nrl{i}", [128, 512], F32) for i in range(3)])
        ss = Ring([sb(nc, es, f"nss{i}", [128, 8], F32) for i in range(2)])
        rsd = Ring([sb(nc, es, f"nrs{i}", [128, 8], F32) for i in range(2)])
        junk = Ring([sb(nc, es, "njk", [128, D], F32)])
        tmp = Ring([sb(nc, es, f"ntmp{i}", [128, 512], F32) for i in range(2)])
        tp = Ring([pm(nc, es, f"ntp{i}", [128, 2, 512], BF16) for i in range(2)])
        mm = Ring([pm(nc, es, f"nmm{i}", [128, 512], F32) for i in range(6)])
        ev = 0
        for s in range(cfg.nseq):
            g_t, g_b = g2.next()
            k.dma("sp", g_t[:], bc_ap(T["Gd"][s, 1:2, :]), writes=[g_b])
            for t0 in range(0, cfg.own[s], 512):
                r0 = int(cfg.ooff[s]) + t0
                x_t, x_b = xr.next()
                k.dma("sp", x_t[:], T["X1"][r0:r0 + 512, :].rearrange("(tb p) c -> p tb c", p=128), writes=[x_b])
                ss_t, sc_b = ss.next()
                rs_t, _ = rsd.next()
                j_t, j_b = junk.next()
                rms_scale_rows(k, x_t, x_b, ss_t, rs_t, sc_b, j_t, j_b, 4, D, NORM_EPS, G)
                xs_t, xs_b = xs.next()
                for tb in range(4):
                    k.op(k.dve, lambda: nc.vector.tensor_scalar(out=xs_t[:, tb, :], in0=x_t[:, tb, :], scalar1=rs_t[:, tb:tb + 1],
                                                                scalar2=None, op0=ALU.mult),
                         reads=[x_b, sc_b], writes=[xs_b], disjoint=True)
                h_t, h_b = hT.next()

                def evac(j, o, i, pb, db):
                    k.op(k.act, lambda: nc.scalar.activation(out=o, in_=i, func=AF.Identity, scale=G["a2"][:, j, s:s + 1],
                                                             bias=G["modT"][:, 24 + j, s:s + 1]),
                         reads=[pb, G["gbuf"]], writes=[db], disjoint=True)
                transposes_to(k, G, xs_t, xs_b, tp, h_t, h_b, evac)
                a_t, a_b = aT.next()
                for mc in range(8):
                    w_t, w_b = w1r.next()
                    k.dma("sp", w_t[:], T["w_mlp1_b"][:, :, mc * 512:(mc + 1) * 512], writes=[w_b])
                    for mi in range(4):
                        m = mc * 4 + mi
                        p_t, p_b = mm.next()
                        for j in range(NCH):
                            k.op(k.pe, lambda: nc.tensor.matmul(p_t[:, :], lhsT=w_t[:, j, mi * 128:(mi + 1) * 128], rhs=h_t[:, j, :],
                                                                start=(j == 0), stop=(j == NCH - 1)),
                                 reads=[w_b, h_b], writes=[p_b])
                        r_t, r_b = rl.next()
                        k.op(k.act, lambda: nc.scalar.activation(out=r_t[:], in_=p_t[:, :], func=AF.Relu), reads=[p_b], writes=[r_b])
                        E = k.dve if ev % 2 == 0 else k.pool
                        ev += 1
                        k.op(E, lambda: E.eng.tensor_tensor(out=a_t[:, m, :], in0=r_t[:], in1=r_t[:], op=ALU.mult),
                             reads=[r_b], writes=[a_b], disjoint=True)
                for tb in range(4):
                    for cc in range(2):
                        p_t, p_b = mm.next()
                        for m in range(32):
                            k.op(k.pe, lambda: nc.tensor.matmul(p_t[:, :], lhsT=a_t[:, m, tb * 128:(tb + 1) * 128],
                                                                rhs=w2[:, m, cc * 512:(cc + 1) * 512], start=(m == 0), stop=(m == 31)),
                                 reads=[a_b, b_w], writes=[p_b])
                        q_t, q_b = tmp.next()
                        k.op(k.dve, lambda: nc.vector.tensor_tensor(out=q_t[:], in0=p_t[:, :], in1=g_t[:, cc * 512:(cc + 1) * 512], op=ALU.mult),
                             reads=[p_b, g_b], writes=[q_b])
                        k.op(k.pool, lambda: nc.gpsimd.tensor_tensor(out=x_t[:, tb, cc * 512:(cc + 1) * 512], in0=q_t[:],
                                                                     in1=x_t[:, tb, cc * 512:(cc + 1) * 512], op=ALU.add),
                             reads=[q_b, x_b], writes=[x_b], disjoint=True)
                ss_t, sc_b = ss.next()
                rs_t, _ = rsd.next()
                rms_scale_rows(k, x_t, x_b, ss_t, rs_t, sc_b, j_t, j_b, 4, D, NORM_EPS, G)
                for tb in range(4):
                    k.op(k.dve, lambda: nc.vector.scalar_tensor_tensor(out=x_t[:, tb, :], in0=x_t[:, tb, :], scalar=rs_t[:, tb:tb + 1],
                                                                       in1=fw[:], op0=ALU.mult, op1=ALU.mult),
                         reads=[x_b, sc_b, b_w], writes=[x_b], disjoint=True)
                k.dma("pool", T["y"][r0:r0 + 512, :].rearrange("(tb p) c -> p tb c", p=128), x_t[:], reads=[x_b])
    k.barrier()


def fft_tables(L, n1_start, m_own):
    f32 = np.float32
    J = L // 128
    N = 2 * L
    G = 128 // J if J <= 128 else 1
    t = {}
    n1 = np.arange(256)[:, None].astype(np.float64)
    k1 = np.arange(256)[None, :].astype(np.float64)
    ph = 2 * np.pi * n1 * k1 / 256.0
    FA = np.stack([np.cos(ph), -np.sin(ph)], 0).reshape(2, 2, 128, 256)
    t["FA"] = FA.astype(f32)
    n2 = np.arange(J)[None, :].astype(np.float64)
    kk = np.arange(256)[:, None].astype(np.float64)
    th = 2 * np.pi * n2 * kk / N
    tw = np.stack([np.cos(th), -np.sin(th), np.sin(th)], 0)
    t["twA"] = np.ascontiguousarray(tw.reshape(3, 2, 128, J).transpose(2, 0, 1, 3)).astype(f32)
    a = np.arange(J)[:, None].astype(np.float64)
    b = np.arange(J)[None, :].astype(np.float64)
    pj = 2 * np.pi * a * b / J
    FBr, FBi = np.cos(pj), -np.sin(pj)
    blk = lambda M: np.kron(np.eye(G), M)
    t["FB"] = np.stack([blk(FBr), blk(FBi), blk(-FBi)], 0).astype(f32)
    q = np.arange(128)
    k1p, n2q = q // J, q % J
    g = np.arange(256 // G)
    k1g = g[None, :] * G + k1p[:, None]
    thb = 2 * np.pi * n2q[:, None] * k1g / N
    t["twB"] = np.stack([np.cos(thb), -np.sin(thb), np.sin(thb)], 1).astype(f32)
    n1o = (n1_start + np.arange(m_own))[None, :].astype(np.float64)
    k1c = np.arange(256)[:, None].astype(np.float64)
    ph2 = 2 * np.pi * n1o * k1c / 256.0
    IA = np.stack([np.cos(ph2) / N, -np.sin(ph2) / N], 0).reshape(2, 2, 128, m_own)
    t["IA"] = IA.astype(f32)
    sel = np.zeros((128, m_own), f32)
    sel[n1_start + np.arange(m_own), np.arange(m_own)] = 1.0
    t["Sel"] = sel
    z, decay = filter_feats(L)
    t["zTf"] = np.ascontiguousarray(z.T)
    zb = np.concatenate([z[0:1], z[:0:-1]], 0)
    t["zTb"] = np.ascontiguousarray(zb.T)
    t["decF"] = decay
    db = np.concatenate([np.zeros((1, HW), f32), decay[:0:-1]], 0)
    t["decB"] = np.ascontiguousarray(db)
    return t


def host_tables(cfg, inp, core, m):
    r = core % cfg.NQ
    mo_p = cfg.own_p // (cfg.Lp // 128)
    tp_ = fft_tables(cfg.Lp, r * mo_p, mo_p)
    ts_ = fft_tables(cfg.Ls, 0, 128)
    for k_, v in tp_.items():
        m[k_ + "_p"] = v
    for k_, v in ts_.items():
        m[k_ + "_s"] = v
    f32 = np.float32
    m["filt_w1"] = np.ascontiguousarray(inp["filt_w1"][0], f32)
    m["filt_w2"] = np.ascontiguousarray(inp["filt_w2"][0], f32)
    m["filt_w3"] = np.ascontiguousarray(inp["filt_w3"][0], f32)
    m["filt_w4"] = np.ascontiguousarray(inp["filt_w4"][0], f32)
    m["filt_bf"] = np.ascontiguousarray(np.stack([inp["filt_b1"][0], inp["filt_b2"][0], inp["filt_b3"][0], inp["filt_freq"][0]], 1), f32)


def declare_hyena(nc, cfg, T, inp, scr):
    for ty, L in (("p", cfg.Lp), ("s", cfg.Ls)):
        J = L // 128
        G = 128 // J
        mo = (cfg.own_p // J) if ty == "p" else 128
        inp("FA_" + ty, [2, 2, 128, 256]); inp("twA_" + ty, [128, 3, 2, J]); inp("FB_" + ty, [3, 128, 128])
        inp("twB_" + ty, [128, 3, 256 // G]); inp("IA_" + ty, [2, 2, 128, mo]); inp("Sel_" + ty, [128, mo])
        inp("zTf_" + ty, [FILTER_EMB, L]); inp("zTb_" + ty, [FILTER_EMB, L]); inp("decF_" + ty, [L, HW]); inp("decB_" + ty, [L, HW])
        scr("Bd_" + ty, [2, 256 * J, 512]); scr("Cd_" + ty, [2, 256 * J, 512]); scr("Kh_" + ty, [2, 256 * J, 512])
    inp("filt_w1", [FILTER_EMB, FORDER]); inp("filt_w2", [FORDER, FORDER]); inp("filt_w3", [FORDER, FORDER])
    inp("filt_w4", [FORDER, 2 * HW]); inp("filt_bf", [FORDER, 4])
    scr("XS", [2, cfg.TO, 512])


def fft_consts(k, es, T, ty, J, mo, need_inv):
    nc = k.nc
    ng = 2 * J
    c = {}
    b = Buf()
    c["buf"] = b
    stg = sb(nc, es, "fst", [128, 1024], F32)
    b_s = Buf()
    c["FA"] = sb(nc, es, "FA", [128, 2, 2, 256], BF16)
    k.dma("sp", stg[:, 0:1024].rearrange("n (a h k) -> n a h k", a=2, h=2), T["FA_" + ty].rearrange("a h n k -> n a h k"), writes=[b_s])
    k.op(k.dve, lambda: nc.vector.tensor_copy(out=c["FA"][:].rearrange("n a h k -> n (a h k)"), in_=stg[:, 0:1024]), reads=[b_s], writes=[b, b_s])
    c["FB"] = sb(nc, es, "FB", [128, 3, 128], BF16)
    k.dma("sp", stg[:, 0:384].rearrange("n (a k) -> n a k", a=3), T["FB_" + ty].rearrange("a n k -> n a k"), writes=[b_s])
    k.op(k.dve, lambda: nc.vector.tensor_copy(out=c["FB"][:].rearrange("n a k -> n (a k)"), in_=stg[:, 0:384]), reads=[b_s], writes=[b, b_s], disjoint=True)
    c["twA"] = sb(nc, es, "twA", [128, 3, 2, J], F32)
    k.dma("sp", c["twA"][:], T["twA_" + ty][:, :, :, :], writes=[b], disjoint=True)
    c["twB"] = sb(nc, es, "twB", [128, 3, ng], F32)
    k.dma("sp", c["twB"][:], T["twB_" + ty][:, :, :], writes=[b], disjoint=True)
    if need_inv:
        c["IA"] = sb(nc, es, "IA", [128, 2, 2, mo], BF16)
        k.dma("sp", stg[:, 0:4 * mo].rearrange("n (a h m) -> n a h m", a=2, h=2), T["IA_" + ty].rearrange("a h n m -> n a h m"), writes=[b_s])
        k.op(k.dve, lambda: nc.vector.tensor_copy(out=c["IA"][:].rearrange("n a h m -> n (a h m)"), in_=stg[:, 0:4 * mo]), reads=[b_s], writes=[b, b_s], disjoint=True)
        c["Sel"] = sb(nc, es, "Sel", [128, mo], BF16)
        k.dma("sp", stg[:, 0:mo], T["Sel_" + ty][:, :], writes=[b_s])
        k.op(k.dve, lambda: nc.vector.tensor_copy(out=c["Sel"][:], in_=stg[:, 0:mo]), reads=[b_s], writes=[b, b_s], disjoint=True)
    return c


def tw_evac(k, p_re, b_re, p_im, b_im, a, bb, cc, cb, o_re, o_im, ob, tmp):
    nc = k.nc
    t1, t1b = tmp.next()
    P = o_re.shape[0]
    k.op(k.act, lambda: nc.scalar.activation(out=t1[0:P, :], in_=p_re, func=AF.Identity, scale=a), reads=[b_re, cb], writes=[t1b])
    k.op(k.dve, lambda: nc.vector.scalar_tensor_tensor(out=o_re, in0=p_im, scalar=cc, in1=t1[0:P, :], op0=ALU.mult, op1=ALU.add),
         reads=[b_im, t1b, cb], writes=[ob], disjoint=True)
    t2, t2b = tmp.next()
    k.op(k.act, lambda: nc.scalar.activation(out=t2[0:P, :], in_=p_re, func=AF.Identity, scale=bb), reads=[b_re, cb], writes=[t2b])
    k.op(k.dve, lambda: nc.vector.scalar_tensor_tensor(out=o_im, in0=p_im, scalar=a, in1=t2[0:P, :], op0=ALU.mult, op1=ALU.add),
         reads=[b_im, t2b, cb], writes=[ob], disjoint=True)


def stage_a(k, c, J, n2, rhs_list, rhs_bufs, mm, tmp, bout, dstB, dst_buf):
    nc = k.nc
    nh = len(rhs_list)
    for ch in range(2):
        pr, prb = mm.next()
        pi, pib = mm.next()
        for cs, (pt, ptb) in enumerate(((pr, prb), (pi, pib))):
            for h in range(nh):
                k.op(k.pe, lambda: nc.tensor.matmul(pt[:, :], lhsT=c["FA"][:, cs, h, ch * 128:(ch + 1) * 128], rhs=rhs_list[h],
                                                    start=(h == 0), stop=(h == nh - 1)),
                     reads=[c["buf"]] + rhs_bufs, writes=[ptb])
        o_t, o_b = bout.next()
        tw = c["twA"]
        tw_evac(k, pr[:, :], prb, pi[:, :], pib, tw[:, 0, ch, n2:n2 + 1], tw[:, 1, ch, n2:n2 + 1], tw[:, 2, ch, n2:n2 + 1], c["buf"],
                o_t[:, 0, :], o_t[:, 1, :], o_b, tmp)
        r0 = ch * 128 * J + n2
        for e in range(2):
            k.dma("pool", dstB[e][r0:r0 + 127 * J + 1:J, :], o_t[:, e, :], reads=[o_b], writes=[dst_buf], disjoint=True)


def phase_filter(k, cfg, T, G):
    nc = k.nc
    PI = math.pi
    for ty, L in (("p", cfg.Lp), ("s", cfg.Ls)):
        J = L // 128
        ng = 2 * J
        with contextlib.ExitStack() as es:
            c = fft_consts(k, es, T, ty, J, 128, False)
            w1 = sb(nc, es, "fw1", [FILTER_EMB, FORDER], F32)
            w2 = sb(nc, es, "fw2", [FORDER, FORDER], F32)
            w3 = sb(nc, es, "fw3", [FORDER, FORDER], F32)
            w4 = sb(nc, es, "fw4", [FORDER, 2 * HW], F32)
            bf = sb(nc, es, "fbf", [FORDER, 4], F32)
            frb = sb(nc, es, "frb", [FORDER, 4], F32)
            npi = sb(nc, es, "npi", [FORDER, 1], F32)
            b_w = Buf()
            k.dma("sp", w1[:], T["filt_w1"][:, :], writes=[b_w])
            k.dma("sp", w2[:], T["filt_w2"][:, :], writes=[b_w], disjoint=True)
            k.dma("sp", w3[:], T["filt_w3"][:, :], writes=[b_w], disjoint=True)
            k.dma("sp", w4[:], T["filt_w4"][:, :], writes=[b_w], disjoint=True)
            k.dma("sp", bf[:], T["filt_bf"][:, :], writes=[b_w], disjoint=True)
            k.op(k.dve, lambda: nc.vector.tensor_scalar(out=frb[:, 0:3], in0=bf[:, 0:3], scalar1=bf[:, 3:4], scalar2=None, op0=ALU.mult),
                 reads=[b_w], writes=[b_w])
            k.op(k.dve, lambda: nc.vector.memset(npi[:], 0.0), writes=[b_w], disjoint=True)
            h3 = [sb(nc, es, f"h3{d}", [FORDER, L], BF16) for d in range(2)]
            w4b = sb(nc, es, "fw4b", [FORDER, 2 * HW], BF16)
            k.op(k.dve, lambda: nc.vector.tensor_copy(out=w4b[:], in_=w4[:]), reads=[b_w], writes=[b_w])
            h3b = [Buf(), Buf()]
            zr = Ring([sb(nc, es, f"fz{i}", [FILTER_EMB, 512], F32) for i in range(2)])
            ar = Ring([sb(nc, es, f"fa{i}", [FORDER, 512], F32) for i in range(2)])
            mr = Ring([sb(nc, es, f"fm{i}", [FORDER, 512], F32) for i in range(2)])
            hr = Ring([sb(nc, es, f"fh{i}", [FORDER, 512], F32) for i in range(2)])
            mm = Ring([pm(nc, es, f"fmm{i}", [128, 512], F32) for i in range(8)])
            ws = [w1, w2, w3]
            for d in range(2):
                zsrc = T[("zTf_" if d == 0 else "zTb_") + ty]
                for c0 in range(0, L, 512):
                    z_t, z_b = zr.next()
                    k.dma("sp", z_t[:], zsrc[:, c0:c0 + 512], writes=[z_b])
                    cur, cur_b, kdim = z_t, z_b, FILTER_EMB
                    for layer in range(3):
                        p_t, p_b = mm.next()
                        k.op(k.pe, lambda: nc.tensor.matmul(p_t[0:FORDER, :], lhsT=ws[layer][0:kdim, :], rhs=cur[0:kdim, :], start=True, stop=True),
                             reads=[b_w, cur_b], writes=[p_b])
                        a_t, a_b = ar.next()
                        m_t, m_b = mr.next()
                        k.op(k.dve, lambda: nc.vector.tensor_scalar(out=a_t[:], in0=p_t[0:FORDER, :], scalar1=bf[:, 3:4], scalar2=frb[:, layer:layer + 1],
                                                                    op0=ALU.mult, op1=ALU.add), reads=[p_b, b_w], writes=[a_b])
                        k.op(k.dve, lambda: nc.vector.tensor_scalar(out=m_t[:], in0=a_t[:], scalar1=PI, scalar2=-2 * PI, op0=ALU.is_gt, op1=ALU.mult),
                             reads=[a_b], writes=[m_b])
                        k.op(k.dve, lambda: nc.vector.tensor_tensor(out=a_t[:], in0=a_t[:], in1=m_t[:], op=ALU.add), reads=[a_b, m_b], writes=[a_b])
                        k.op(k.dve, lambda: nc.vector.tensor_scalar(out=m_t[:], in0=a_t[:], scalar1=-PI, scalar2=2 * PI, op0=ALU.is_lt, op1=ALU.mult),
                             reads=[a_b], writes=[m_b])
                        k.op(k.dve, lambda: nc.vector.tensor_tensor(out=a_t[:], in0=a_t[:], in1=m_t[:], op=ALU.add), reads=[a_b, m_b], writes=[a_b])
                        if layer < 2:
                            h_t, h_b = hr.next()
                            k.op(k.act, lambda: nc.scalar.activation(out=h_t[:], in_=a_t[:], func=AF.Sin), reads=[a_b], writes=[h_b])
                            cur, cur_b, kdim = h_t, h_b, FORDER
                        else:
                            k.op(k.act, lambda: nc.scalar.activation(out=h3[d][:, c0:c0 + 512], in_=a_t[:], func=AF.Sin), reads=[a_b],
                                 writes=[h3b[d]], disjoint=True)
            dr = Ring([sb(nc, es, f"fd{i}", [128, 2, 512], F32) for i in range(2)])
            kf = Ring([sb(nc, es, f"fkf{i}", [128, 512], F32) for i in range(2)])
            kc = Ring([sb(nc, es, f"fkc{i}", [128, 2, 512], BF16) for i in range(2)])
            tmp = Ring([sb(nc, es, f"ftm{i}", [128, 512], F32) for i in range(4)])
            bout = Ring([sb(nc, es, f"fbo{i}", [128, 2, 512], BF16) for i in range(3)])
            Bd = [T["Bd_" + ty][0], T["Bd_" + ty][1]]
            Kh = [T["Kh_" + ty][0], T["Kh_" + ty][1]]
            bd_buf = Buf()
            for n2 in range(J):
                d_t, d_b = dr.next()
                k.dma("sp", d_t[:, 0, :], T["decF_" + ty][n2:n2 + 127 * J + 1:J, :], writes=[d_b])
                k.dma("sp", d_t[:, 1, :], T["decB_" + ty][n2:n2 + 127 * J + 1:J, :], writes=[d_b], disjoint=True)
                pf, pfb = mm.next()
                pb, pbb = mm.next()
                k.op(k.pe, lambda: nc.tensor.matmul(pf[:, :], lhsT=h3[0][:, n2:n2 + 127 * J + 1:J], rhs=w4b[:, 0:512], start=True, stop=True),
                     reads=[h3b[0], b_w], writes=[pfb])
                k.op(k.pe, lambda: nc.tensor.matmul(pb[:, :], lhsT=h3[1][:, n2:n2 + 127 * J + 1:J], rhs=w4b[:, 512:1024], start=True, stop=True),
                     reads=[h3b[1], b_w], writes=[pbb])
                kc_t, kc_b = kc.next()
                if n2 == 0:
                    p0, p0b = mm.next()
                    k.op(k.pe, lambda: nc.tensor.matmul(p0[0:1, :], lhsT=h3[0][:, 0:1], rhs=w4b[:, 512:1024], start=True, stop=True),
                         reads=[h3b[0], b_w], writes=[p0b])
                    kf_t, kf_b = kf.next()
                    k.op(k.dve, lambda: nc.vector.tensor_tensor(out=kf_t[:], in0=pf[:, :], in1=d_t[:, 0, :], op=ALU.mult), reads=[pfb, d_b], writes=[kf_b])
                    k.op(k.dve, lambda: nc.vector.tensor_tensor(out=kf_t[0:1, :], in0=kf_t[0:1, :], in1=p0[0:1, :], op=ALU.add),
                         reads=[kf_b, p0b], writes=[kf_b])
                    k.op(k.dve, lambda: nc.vector.tensor_copy(out=kc_t[:, 0, :], in_=kf_t[:]), reads=[kf_b], writes=[kc_b])
                else:
                    k.op(k.dve, lambda: nc.vector.tensor_tensor(out=kc_t[:, 0, :], in0=pf[:, :], in1=d_t[:, 0, :], op=ALU.mult), reads=[pfb, d_b], writes=[kc_b])
                k.op(k.dve, lambda: nc.vector.tensor_tensor(out=kc_t[:, 1, :], in0=pb[:, :], in1=d_t[:, 1, :], op=ALU.mult), reads=[pbb, d_b],
                     writes=[kc_b], disjoint=True)
                stage_a(k, c, J, n2, [kc_t[:, 0, :], kc_t[:, 1, :]], [kc_b], mm, tmp, bout, Bd, bd_buf)
            br = Ring([sb(nc, es, f"fbr{i}", [128, 2, 512], BF16) for i in range(2)])
            kh_buf = G["kh_buf"][ty]
            for g in range(ng):
                b_t, b_b = br.next()
                for e in range(2):
                    k.dma("sp", b_t[:, e, :], Bd[e][g * 128:(g + 1) * 128, :], reads=[bd_buf], writes=[b_b], disjoint=True)
                xr, xrb = mm.next()
                xi, xib = mm.next()
                FB = c["FB"]
                k.op(k.pe, lambda: nc.tensor.matmul(xr[:, :], lhsT=FB[:, 0, :], rhs=b_t[:, 0, :], start=True, stop=False), reads=[b_b, c["buf"]], writes=[xrb])
                k.op(k.pe, lambda: nc.tensor.matmul(xr[:, :], lhsT=FB[:, 2, :], rhs=b_t[:, 1, :], start=False, stop=True), reads=[b_b, c["buf"]], writes=[xrb])
                k.op(k.pe, lambda: nc.tensor.matmul(xi[:, :], lhsT=FB[:, 1, :], rhs=b_t[:, 0, :], start=True, stop=False), reads=[b_b, c["buf"]], writes=[xib])
                k.op(k.pe, lambda: nc.tensor.matmul(xi[:, :], lhsT=FB[:, 0, :], rhs=b_t[:, 1, :], start=False, stop=True), reads=[b_b, c["buf"]], writes=[xib])
                o_t, o_b = bout.next()
                k.op(k.act, lambda: nc.scalar.copy(out=o_t[:, 0, :], in_=xr[:, :]), reads=[xrb], writes=[o_b])
                k.op(k.dve, lambda: nc.vector.tensor_copy(out=o_t[:, 1, :], in_=xi[:, :]), reads=[xib], writes=[o_b], disjoint=True)
                for e in range(2):
                    k.dma("pool", Kh[e][g * 128:(g + 1) * 128, :], o_t[:, e, :], reads=[o_b], writes=[kh_buf], disjoint=True)
        k.barrier()


def phase_hyena(k, cfg, T, G):
    nc = k.nc
    for ty, seqs in (("p", [0]), ("s", list(range(1, cfg.nseq)))):
        L = cfg.Lp if ty == "p" else cfg.Ls
        J = L // 128
        ng = 2 * J
        mo = (cfg.own_p // J) if ty == "p" else 128
        JC = min(J, 4)
        with contextlib.ExitStack() as es:
            c = fft_consts(k, es, T, ty, J, mo, True)
            cw = sb(nc, es, "hcw", [128, 3, 1536], F32)
            cb = sb(nc, es, "hcb", [128, 1536], F32)
            hb = sb(nc, es, "hhb", [128, 512], F32)
            b_w = Buf()
            k.dma("sp", cw[:].rearrange("p a c -> p (a c)"), bc_ap(T["conv_w"]), writes=[b_w])
            k.dma("sp", cb[:], bc_ap(T["conv_b"]), writes=[b_w], disjoint=True)
            k.dma("sp", hb[:], bc_ap(T["hyena_bias"]), writes=[b_w], disjoint=True)
            ur = Ring([sb(nc, es, f"hu{i}", [128, JC + 2, 1536], BF16) for i in range(2)])
            ta = Ring([sb(nc, es, f"hta{i}", [128, 1024], F32) for i in range(2)])
            tb = Ring([sb(nc, es, f"htb{i}", [128, 1024], F32) for i in range(2)])
            tc = Ring([sb(nc, es, f"htc{i}", [128, 1024], F32) for i in range(2)])
            s32 = Ring([sb(nc, es, f"hs32{i}", [128, 512], F32) for i in range(2)])
            sg = Ring([sb(nc, es, f"hsg{i}", [128, 3, 512], BF16) for i in range(3)])
            xso = Ring([sb(nc, es, f"hxo{i}", [128, 2, 512], BF16) for i in range(3)])
            tmp = Ring([sb(nc, es, f"htm{i}", [128, 512], F32) for i in range(4)])
            bout = Ring([sb(nc, es, f"hbo{i}", [128, 2, 512], BF16) for i in range(3)])
            br = Ring([sb(nc, es, f"hbr{i}", [128, 4, 512], BF16) for i in range(2)])
            pw = Ring([sb(nc, es, f"hpw{i}", [128, 512], F32) for i in range(4)])
            yy = Ring([sb(nc, es, f"hyy{i}", [128, 2, 512], BF16) for i in range(2)])
            dr_ = Ring([sb(nc, es, f"hdr{i}", [128, 4, 512], BF16) for i in range(2)])
            xl = Ring([sb(nc, es, f"hxl{i}", [128, 2, 512], BF16) for i in range(2)])
            yo = Ring([sb(nc, es, f"hyo{i}", [128, 512], F32) for i in range(2)])
            mm = Ring([pm(nc, es, f"hmm{i}", [128, 512], F32) for i in range(8)])
            Bd = [T["Bd_" + ty][0], T["Bd_" + ty][1]]
            Cd = [T["Cd_" + ty][0], T["Cd_" + ty][1]]
            Kh = [T["Kh_" + ty][0], T["Kh_" + ty][1]]
            XS = [T["XS"][0], T["XS"][1]]
            bd_buf, cd_buf, xs_buf = Buf(), Buf(), Buf()
            kh_buf = G["kh_buf"][ty]
            U = T["U"]
            for s in seqs:
                ub = int(cfg.foff[s]) + 2 * s
                o0 = int(cfg.ooff[s])
                for j0 in range(0, J, JC):
                    u_t, u_b = ur.next()
                    src = bass.AP(tensor=U.tensor, offset=U.offset + (ub + j0) * 1536, ap=[[J * 1536, 128], [1536, JC + 2], [1, 1536]])
                    k.dma("sp", u_t[:], src, writes=[u_b])
                    for jj in range(JC):
                        n2 = j0 + jj
                        a_t, a_b = ta.next()
                        b_t, b_b = tb.next()
                        c_t, c_b = tc.next()
                        k.op(k.dve, lambda: nc.vector.tensor_tensor(out=a_t[:], in0=u_t[:, jj, 512:1536], in1=cw[:, 0, 512:1536], op=ALU.mult), reads=[u_b, b_w], writes=[a_b])
                        k.op(k.dve, lambda: nc.vector.tensor_tensor(out=b_t[:], in0=u_t[:, jj + 1, 512:1536], in1=cw[:, 1, 512:1536], op=ALU.mult), reads=[u_b, b_w], writes=[b_b])
                        k.op(k.dve, lambda: nc.vector.tensor_tensor(out=a_t[:], in0=a_t[:], in1=b_t[:], op=ALU.add), reads=[a_b, b_b], writes=[a_b])
                        k.op(k.dve, lambda: nc.vector.tensor_tensor(out=b_t[:], in0=u_t[:, jj + 2, 512:1536], in1=cw[:, 2, 512:1536], op=ALU.mult), reads=[u_b, b_w, a_b], writes=[b_b])
                        k.op(k.dve, lambda: nc.vector.tensor_tensor(out=a_t[:], in0=a_t[:], in1=b_t[:], op=ALU.add), reads=[a_b, b_b], writes=[a_b])
                        k.op(k.dve, lambda: nc.vector.tensor_tensor(out=a_t[:], in0=a_t[:], in1=cb[:, 512:1536], op=ALU.add), reads=[a_b, b_w], writes=[a_b])
                        x0c, tq = c_t[:, 0:512], c_t[:, 512:1024]
                        k.op(k.pool, lambda: nc.gpsimd.tensor_tensor(out=x0c, in0=u_t[:, jj, 0:512], in1=cw[:, 0, 0:512], op=ALU.mult), reads=[u_b, b_w], writes=[c_b])
                        k.op(k.pool, lambda: nc.gpsimd.tensor_tensor(out=tq, in0=u_t[:, jj + 1, 0:512], in1=cw[:, 1, 0:512], op=ALU.mult), reads=[u_b, b_w, c_b], writes=[c_b])
                        k.op(k.pool, lambda: nc.gpsimd.tensor_tensor(out=x0c, in0=x0c, in1=tq, op=ALU.add), reads=[c_b], writes=[c_b])
                        k.op(k.pool, lambda: nc.gpsimd.tensor_tensor(out=tq, in0=u_t[:, jj + 2, 0:512], in1=cw[:, 2, 0:512], op=ALU.mult), reads=[u_b, b_w, c_b], writes=[c_b])
                        k.op(k.pool, lambda: nc.gpsimd.tensor_tensor(out=x0c, in0=x0c, in1=tq, op=ALU.add), reads=[c_b], writes=[c_b])
                        k.op(k.pool, lambda: nc.gpsimd.tensor_tensor(out=x0c, in0=x0c, in1=cb[:, 0:512], op=ALU.add), reads=[c_b, b_w], writes=[c_b])
                        s_t, s_b = s32.next()
                        g_t, g_b = sg.next()
                        k.op(k.dve, lambda: nc.vector.tensor_tensor(out=s_t[:], in0=a_t[:, 0:512], in1=a_t[:, 512:1024], op=ALU.mult), reads=[a_b], writes=[s_b])
                        k.op(k.act, lambda: nc.scalar.copy(out=g_t[:, 0, :], in_=s_t[:]), reads=[s_b], writes=[g_b])
                        k.op(k.act, lambda: nc.scalar.copy(out=g_t[:, 1, :], in_=x0c), reads=[c_b], writes=[g_b], disjoint=True)
                        k.op(k.pool, lambda: nc.gpsimd.tensor_tensor(out=tq, in0=s_t[:], in1=hb[:], op=ALU.mult), reads=[s_b, b_w, c_b], writes=[c_b])
                        k.op(k.pool, lambda: nc.gpsimd.tensor_tensor(out=g_t[:, 2, :], in0=tq, in1=x0c, op=ALU.mult), reads=[c_b], writes=[g_b], disjoint=True)
                        x_t, x_b = xso.next()
                        for e in range(2):
                            p_t, p_b = mm.next()
                            k.op(k.pe, lambda: nc.tensor.matmul(p_t[0:mo, :], lhsT=c["Sel"][:, :], rhs=g_t[:, 1 + e, :], start=True, stop=True),
                                 reads=[g_b, c["buf"]], writes=[p_b])
                            k.op(k.act, lambda: nc.scalar.copy(out=x_t[0:mo, e, :], in_=p_t[0:mo, :]), reads=[p_b], writes=[x_b], disjoint=True)
                            k.dma("pool", XS[e][o0 + n2:o0 + n2 + (mo - 1) * J + 1:J, :], x_t[0:mo, e, :], reads=[x_b], writes=[xs_buf], disjoint=True)
                        stage_a(k, c, J, n2, [g_t[:, 0, :]], [g_b], mm, tmp, bout, Bd, bd_buf)
                for g in range(ng):
                    b_t, b_b = br.next()
                    for e in range(2):
                        k.dma("sp", b_t[:, e, :], Bd[e][g * 128:(g + 1) * 128, :], reads=[bd_buf], writes=[b_b], disjoint=True)
                        k.dma("sp", b_t[:, 2 + e, :], Kh[e][g * 128:(g + 1) * 128, :], reads=[kh_buf], writes=[b_b], disjoint=True)
                    xr, xrb = mm.next()
                    xi, xib = mm.next()
                    FB = c["FB"]
                    k.op(k.pe, lambda: nc.tensor.matmul(xr[:, :], lhsT=FB[:, 0, :], rhs=b_t[:, 0, :], start=True, stop=False), reads=[b_b, c["buf"]], writes=[xrb])
                    k.op(k.pe, lambda: nc.tensor.matmul(xr[:, :], lhsT=FB[:, 2, :], rhs=b_t[:, 1, :], start=False, stop=True), reads=[b_b, c["buf"]], writes=[xrb])
                    k.op(k.pe, lambda: nc.tensor.matmul(xi[:, :], lhsT=FB[:, 1, :], rhs=b_t[:, 0, :], start=True, stop=False), reads=[b_b, c["buf"]], writes=[xib])
                    k.op(k.pe, lambda: nc.tensor.matmul(xi[:, :], lhsT=FB[:, 0, :], rhs=b_t[:, 1, :], start=False, stop=True), reads=[b_b, c["buf"]], writes=[xib])
                    m1, m1b = pw.next()
                    m2, m2b = pw.next()
                    m3, m3b = pw.next()
                    m4, m4b = pw.next()
                    y_t, y_b = yy.next()
                    k.op(k.dve, lambda: nc.vector.tensor_tensor(out=m1[:], in0=xr[:, :], in1=b_t[:, 2, :], op=ALU.mult), reads=[xrb, b_b], writes=[m1b])
                    k.op(k.dve, lambda: nc.vector.tensor_tensor(out=m2[:], in0=xi[:, :], in1=b_t[:, 3, :], op=ALU.mult), reads=[xib, b_b], writes=[m2b])
                    k.op(k.pool, lambda: nc.gpsimd.tensor_tensor(out=y_t[:, 0, :], in0=m1[:], in1=m2[:], op=ALU.subtract), reads=[m1b, m2b], writes=[y_b])
                    k.op(k.dve, lambda: nc.vector.tensor_tensor(out=m3[:], in0=xr[:, :], in1=b_t[:, 3, :], op=ALU.mult), reads=[xrb, b_b], writes=[m3b])
                    k.op(k.dve, lambda: nc.vector.tensor_tensor(out=m4[:], in0=xi[:, :], in1=b_t[:, 2, :], op=ALU.mult), reads=[xib, b_b], writes=[m4b])
                    k.op(k.pool, lambda: nc.gpsimd.tensor_tensor(out=y_t[:, 1, :], in0=m3[:], in1=m4[:], op=ALU.add), reads=[m3b, m4b], writes=[y_b], disjoint=True)
                    cr, crb = mm.next()
                    ci, cib = mm.next()
                    k.op(k.pe, lambda: nc.tensor.matmul(cr[:, :], lhsT=FB[:, 0, :], rhs=y_t[:, 0, :], start=True, stop=False), reads=[y_b, c["buf"]], writes=[crb])
                    k.op(k.pe, lambda: nc.tensor.matmul(cr[:, :], lhsT=FB[:, 1, :], rhs=y_t[:, 1, :], start=False, stop=True), reads=[y_b, c["buf"]], writes=[crb])
                    k.op(k.pe, lambda: nc.tensor.matmul(ci[:, :], lhsT=FB[:, 0, :], rhs=y_t[:, 1, :], start=True, stop=False), reads=[y_b, c["buf"]], writes=[cib])
                    k.op(k.pe, lambda: nc.tensor.matmul(ci[:, :], lhsT=FB[:, 2, :], rhs=y_t[:, 0, :], start=False, stop=True), reads=[y_b, c["buf"]], writes=[cib])
                    o_t, o_b = bout.next()
                    tw = c["twB"]
                    tw_evac(k, cr[:, :], crb, ci[:, :], cib, tw[:, 0, g:g + 1], tw[:, 2, g:g + 1], tw[:, 1, g:g + 1], c["buf"],
                            o_t[:, 0, :], o_t[:, 1, :], o_b, tmp)
                    for e in range(2):
                        k.dma("pool", Cd[e][g * 128:(g + 1) * 128, :], o_t[:, e, :], reads=[o_b], writes=[cd_buf], disjoint=True)
                for n2 in range(J):
                    d_t, d_b = dr_.next()
                    for ch in range(2):
                        r0 = ch * 128 * J + n2
                        for e in range(2):
                            k.dma("sp", d_t[:, ch * 2 + e, :], Cd[e][r0:r0 + 127 * J + 1:J, :], reads=[cd_buf], writes=[d_b], disjoint=True)
                    x_t, x_b = xl.next()
                    for e in range(2):
                        k.dma("sp", x_t[0:mo, e, :], XS[e][o0 + n2:o0 + n2 + (mo - 1) * J + 1:J, :], reads=[xs_buf], writes=[x_b], disjoint=True)
                    p_t, p_b = mm.next()
                    i = 0
                    for ch in range(2):
                        for e in range(2):
                            k.op(k.pe, lambda: nc.tensor.matmul(p_t[0:mo, :], lhsT=c["IA"][:, e, ch, :], rhs=d_t[:, ch * 2 + e, :], start=(i == 0), stop=(i == 3)),
                                 reads=[d_b, c["buf"]], writes=[p_b])
                            i += 1
                    q_t, q_b = tmp.next()
                    y_t, y_b = yo.next()
                    k.op(k.dve, lambda: nc.vector.tensor_tensor(out=q_t[0:mo, :], in0=p_t[0:mo, :], in1=x_t[0:mo, 0, :], op=ALU.mult), reads=[p_b, x_b], writes=[q_b])
                    k.op(k.pool, lambda: nc.gpsimd.tensor_tensor(out=y_t[0:mo, :], in0=q_t[0:mo, :], in1=x_t[0:mo, 1, :], op=ALU.add), reads=[q_b, x_b], writes=[y_b])
                    k.dma("pool", T["Yh"][o0 + n2:o0 + n2 + (mo - 1) * J + 1:J, :], y_t[0:mo, :], reads=[y_b])
        k.barrier()
```

```python
import contextlib
import math
import numpy as np
import ml_dtypes
import concourse.bass as bass
import concourse.mybir as mybir
from concourse.bass_utils import run_bass_kernel_spmd

F32 = mybir.dt.float32
BF16 = mybir.dt.bfloat16
AF = mybir.ActivationFunctionType
ALU = mybir.AluOpType

D = 1024
NCH = 8
HW = 512
DFF = 4096
NORM_EPS = 1e-6
SUBLN_EPS = 1e-5
ROT_DIM = 16
ROPE_THETA = 500000.0
FILTER_EMB = 33
FORDER = 64


class Buf:
    __slots__ = ("w", "r", "pr")

    def __init__(self):
        self.w = {}
        self.r = {}
        self.pr = {}


class Eng:
    def __init__(self, name, eng, sem):
        self.name, self.eng, self.sem = name, eng, sem
        self.count = 0
        self.waited = {}

    def wait(self, sem, val):
        k = id(sem)
        if self.waited.get(k, 0) >= val:
            return
        self.eng.wait_ge(sem, val)
        self.waited[k] = val


class K:
    def __init__(self, nc, es):
        self.nc = nc
        self.es = es
        self.sems = {}
        mk = lambda n: es.enter_context(nc.semaphore(n))
        self.pe = Eng("pe", nc.tensor, mk("s_pe"))
        self.act = Eng("act", nc.scalar, mk("s_act"))
        self.dve = Eng("dve", nc.vector, mk("s_dve"))
        self.pool = Eng("pool", nc.gpsimd, mk("s_pool"))
        self.sp = Eng("sp", nc.sync, mk("s_sp"))
        self.engs = [self.pe, self.act, self.dve, self.pool, self.sp]
        self.nq = 8
        self.dq = {}
        for q, e in (("sp", self.sp), ("pool", self.pool)):
            self.dq[q] = dict(eng=e, sems=[mk(f"d_{q}{i}") for i in range(self.nq)], idx=0)
        self.all_dma_events = {}

    def _deps(self, E, reads, writes, disjoint):
        evs = {}

        def add(d):
            for k, (s, v) in d.items():
                if k not in evs or evs[k][1] < v:
                    evs[k] = (s, v)
        for b in reads:
            add(b.w)
        for b in writes:
            add(b.r)
            add(b.pr)
            if not disjoint:
                add(b.w)
        for k, (s, v) in evs.items():
            if E is self.pe and s is self.pe.sem:
                continue
            E.wait(s, v)

    def _commit(self, sem, val, reads, writes):
        k = id(sem)
        for b in reads:
            b.r[k] = (sem, val)
        for b in writes:
            if b.r:
                b.pr = b.r
                b.r = {}
                b.w = {}
            b.w[k] = (sem, val)

    def op(self, E, fn, reads=(), writes=(), disjoint=False):
        self._deps(E, reads, writes, disjoint)
        ins = fn()
        E.count += 1
        ins.then_inc(E.sem, 1)
        self._commit(E.sem, E.count, reads, writes)
        return ins

    def dma(self, q, out, in_, reads=(), writes=(), disjoint=False, **kw):
        Q = self.dq[q]
        E = Q["eng"]
        slot = Q["idx"] % self.nq
        gen = Q["idx"] // self.nq
        Q["idx"] += 1
        sem = Q["sems"][slot]
        E.wait(sem, 16 * gen)
        self._deps(E, reads, writes, disjoint)
        E.eng.dma_start(out=out, in_=in_, **kw).then_inc(sem, 16)
        self._commit(sem, 16 * (gen + 1), reads, writes)
        self.all_dma_events[id(sem)] = (sem, 16 * (gen + 1))

    def barrier(self):
        for E in self.engs:
            for X in self.engs:
                if X is not E and X.count > 0:
                    E.wait(X.sem, X.count)
            for (s, v) in self.all_dma_events.values():
                E.wait(s, v)

    def final_wait(self):
        for (s, v) in self.all_dma_events.values():
            self.sp.wait(s, v)
        for X in self.engs:
            if X is not self.sp and X.count > 0:
                self.sp.wait(X.sem, X.count)


def bcast_rows(ap_row, nparts=128):
    return ap_row.partition_broadcast(nparts)


class Cfg:
    def __init__(self, Lp=16384, Ls=2048, NS=4, NQ=4, debug=False):
        self.Lp, self.Ls, self.NS, self.NQ = Lp, Ls, NS, NQ
        self.own_p = Lp // NQ
        self.nseq = 1 + NS
        self.L = [Lp] + [Ls] * NS
        self.own = [self.own_p] + [Ls] * NS
        self.debug = debug
        self.foff = np.concatenate([[0], np.cumsum(self.L)]).astype(int)
        self.ooff = np.concatenate([[0], np.cumsum(self.own)]).astype(int)
        self.TF = int(self.foff[-1])
        self.TO = int(self.ooff[-1])


def rope_tables(positions):
    inv_freq = (ROPE_THETA ** (-np.arange(0, ROT_DIM, 2, dtype=np.float32) / ROT_DIM)).astype(np.float32)
    ang = positions.astype(np.float32)[:, None] * inv_freq[None, :]
    cos, sin = np.cos(ang).astype(np.float32), np.sin(ang).astype(np.float32)
    n = positions.shape[0]
    C = np.ones((64, n), np.float32)
    S = np.zeros((64, n), np.float32)
    C[0:8] = cos.T
    C[8:16] = cos.T
    S[0:8] = -sin.T
    S[8:16] = sin.T
    return np.concatenate([C, C], 0), np.concatenate([S, S], 0)


def rot_perm():
    p = np.arange(64)
    p[0:8] = np.arange(8, 16)
    p[8:16] = np.arange(0, 8)
    return p


def filter_feats(L):
    f32 = np.float32
    t = np.linspace(0.0, 1.0, L, dtype=f32)[:, None]
    bands = (FILTER_EMB - 1) // 2
    w = (f32(2.0 * math.pi) * np.arange(L, dtype=f32)[:, None] / f32(L)).astype(f32)
    f = np.linspace(1e-4, bands - 1, bands, dtype=f32)[None, :]
    fw = (f * w).astype(f32)
    z = np.concatenate([t, np.cos(fw), -np.sin(fw)], axis=-1).astype(f32)
    min_decay = math.log(1e-2) / 1.5
    max_decay = math.log(1e-2) / 0.3
    deltas = np.linspace(min_decay, max_decay, HW, dtype=f32)
    decay = np.exp(-t * np.abs(deltas)[None, :]).astype(f32)
    return z, decay


class Ring:
    def __init__(self, tiles):
        self.tiles = tiles
        self.bufs = [Buf() for _ in tiles]
        self.i = 0

    def next(self):
        t, b = self.tiles[self.i % len(self.tiles)], self.bufs[self.i % len(self.tiles)]
        self.i += 1
        return t, b


_UID = [0]


def sb(nc, es, name, shape, dt):
    _UID[0] += 1
    return es.enter_context(nc.sbuf_tensor(f"sb{_UID[0]}_{name}", list(shape), dt))


def pm(nc, es, name, shape, dt):
    _UID[0] += 1
    return es.enter_context(nc.psum_tensor(f"ps{_UID[0]}_{name}", list(shape), dt))


WSPEC = [("w_in", D, 3072), ("w_perm", D, 1024), ("w_out", D, D), ("w_mlp1", D, DFF), ("w_mlp2", DFF, D)]


def declare(nc, cfg):
    T = {}
    dbg = cfg.debug

    def inp(name, shape, dt=F32):
        T[name] = nc.dram_tensor(name, list(shape), dt, kind="ExternalInput").ap()

    def scr(name, shape, dt=BF16, out=False):
        kind = "ExternalOutput" if (out or (dbg and name in cfg.debug)) else "Internal"
        T[name] = nc.dram_tensor(name, list(shape), dt, kind=kind).ap()

    ns = cfg.nseq
    inp("xf", [cfg.TF, D]); inp("xo", [cfg.TO, D])
    inp("cT", [128, NCH, ns]); inp("w_ada", [D, 6 * D]); inp("b_ada", [1, 6 * D]); inp("b_adaT", [128, 48])
    inp("nw1T", [128, NCH]); inp("nw2T", [128, NCH]); inp("hnwT", [128, 4])
    for n, kd, nn in WSPEC:
        inp(n, [kd, nn])
        scr(n + "_b", [128, kd // 128, nn])
    inp("ident", [128, 128])
    inp("ropeF_c", [128, cfg.TF]); inp("ropeF_s", [128, cfg.TF])
    inp("ropeO_c", [128, cfg.TO]); inp("ropeO_s", [128, cfg.TO])
    inp("conv_w", [3, 1536]); inp("conv_b", [1, 1536])
    inp("hyena_bias", [1, HW]); inp("final_w", [1, D]); inp("subln_w", [1, 128])
    inp("lam4", [4, 64])
    scr("Gd", [ns, 2, D], F32)
    scr("U", [cfg.TF + 2 * ns, 1536])
    scr("Vd", [cfg.TF, 512])
    scr("KT", [512, cfg.TF])
    scr("QT", [512, cfg.TO])
    if dbg and "Yh_in" in cfg.debug:
        inp("Yh", [cfg.TO, 512])
    else:
        scr("Yh", [cfg.TO, 512], F32)
    scr("Oa", [cfg.TO, 512])
    scr("X1", [cfg.TO, D], F32)
    declare_hyena(nc, cfg, T, inp, scr)
    T["y"] = nc.dram_tensor("y", [cfg.TO, D], F32, kind="ExternalOutput").ap()
    return T


def phase_weights(k, cfg, T):
    nc = k.nc
    with contextlib.ExitStack() as es:
        st = Ring([sb(nc, es, f"wst{i}", [128, 8, 512], F32) for i in range(2)])
        cb = Ring([sb(nc, es, f"wcb{i}", [128, 8, 512], BF16) for i in range(2)])
        i = 0
        for name, kd, nn in WSPEC:
            src = T[name].rearrange("(j p) c -> p j c", p=128)
            dst = T[name + "_b"]
            for j0 in range(0, kd // 128, 8):
                for c0 in range(0, nn, 512):
                    s_t, s_b = st.next()
                    c_t, c_b = cb.next()
                    k.dma("sp", s_t[:], src[:, j0:j0 + 8, c0:c0 + 512], writes=[s_b])
                    E = k.dve if i % 2 == 0 else k.act
                    if E is k.dve:
                        k.op(E, lambda: nc.vector.tensor_copy(out=c_t[:], in_=s_t[:]), reads=[s_b], writes=[c_b])
                    else:
                        k.op(E, lambda: nc.scalar.copy(out=c_t[:], in_=s_t[:]), reads=[s_b], writes=[c_b])
                    k.dma("pool", dst[:, j0:j0 + 8, c0:c0 + 512], c_t[:], reads=[c_b])
                    i += 1
    k.barrier()


def phase_mod(k, cfg, T, G):
    nc = k.nc
    ns = cfg.nseq
    modT, a1, a2 = G["modT"], G["a1"], G["a2"]
    gb = G["gbuf"]
    with contextlib.ExitStack() as es:
        cT = sb(nc, es, "cT", [128, NCH, ns], F32)
        scT = sb(nc, es, "scT", [128, NCH, ns], F32)
        bT = sb(nc, es, "bT", [128, 48], F32)
        n1 = sb(nc, es, "n1", [128, NCH], F32)
        n2 = sb(nc, es, "n2", [128, NCH], F32)
        brow = sb(nc, es, "brow", [1, 6 * D], F32)
        grow = Ring([sb(nc, es, f"grow{i}", [1, 512], F32) for i in range(2)])
        wr = Ring([sb(nc, es, f"wada{i}", [128, 8, 512], F32) for i in range(2)])
        psm = pm(nc, es, "psm", [128, 48, ns], F32)
        psg = Ring([pm(nc, es, f"psg{i}", [1, 512], F32) for i in range(2)])
        b_c, b_s, b_b, b_n, b_br, b_psm = Buf(), Buf(), Buf(), Buf(), Buf(), Buf()
        k.dma("sp", cT[:], T["cT"][:, :, :], writes=[b_c])
        k.dma("sp", bT[:], T["b_adaT"][:, :], writes=[b_b])
        k.dma("sp", n1[:], T["nw1T"][:, :], writes=[b_n])
        k.dma("sp", n2[:], T["nw2T"][:, :], writes=[b_n], disjoint=True)
        k.dma("sp", brow[:], T["b_ada"][:, :], writes=[b_br])
        k.op(k.act, lambda: nc.scalar.activation(out=scT[:], in_=cT[:], func=AF.Silu), reads=[b_c], writes=[b_s])
        wsrc = T["w_ada"].rearrange("(j p) c -> p j c", p=128)
        for pc in range(12):
            w_t, w_b = wr.next()
            k.dma("sp", w_t[:], wsrc[:, :, pc * 512:(pc + 1) * 512], writes=[w_b])
            for mm in range(4):
                m = pc * 4 + mm
                for j in range(NCH):
                    k.op(k.pe, lambda: nc.tensor.matmul(psm[:, m, :], lhsT=w_t[:, j, mm * 128:(mm + 1) * 128],
                                                        rhs=scT[:, j, :], start=(j == 0), stop=(j == NCH - 1)),
                         reads=[w_b, b_s], writes=[b_psm], disjoint=True)
            which = {4: (0, 0), 5: (0, 1), 10: (1, 0), 11: (1, 1)}.get(pc)
            if which is not None:
                gi, half = which
                for s in range(ns):
                    p_t, p_b = psg.next()
                    g_t, g_b = grow.next()
                    for j in range(NCH):
                        k.op(k.pe, lambda: nc.tensor.matmul(p_t[:, :], lhsT=scT[:, j, s:s + 1], rhs=w_t[:, j, :],
                                                            start=(j == 0), stop=(j == NCH - 1)),
                             reads=[w_b, b_s], writes=[p_b])
                    k.op(k.dve, lambda: nc.vector.tensor_tensor(out=g_t[:], in0=p_t[:, :],
                                                                in1=brow[:, pc * 512:(pc + 1) * 512], op=ALU.add),
                         reads=[p_b, b_br], writes=[g_b])
                    k.dma("pool", T["Gd"][s, gi:gi + 1, half * 512:(half + 1) * 512], g_t[:], reads=[g_b])
        for s in range(ns):
            k.op(k.dve, lambda: nc.vector.tensor_tensor(out=modT[:, :, s], in0=psm[:, :, s], in1=bT[:, :], op=ALU.add),
                 reads=[b_psm, b_b], writes=[gb], disjoint=True)
        for s in range(ns):
            k.op(k.dve, lambda: nc.vector.scalar_tensor_tensor(out=a1[:, :, s], in0=modT[:, 8:16, s], scalar=1.0,
                                                               in1=n1[:, :], op0=ALU.add, op1=ALU.mult),
                 reads=[gb, b_n], writes=[gb], disjoint=True)
            k.op(k.dve, lambda: nc.vector.scalar_tensor_tensor(out=a2[:, :, s], in0=modT[:, 32:40, s], scalar=1.0,
                                                               in1=n2[:, :], op0=ALU.add, op1=ALU.mult),
                 reads=[gb, b_n], writes=[gb], disjoint=True)
    k.barrier()


def rms_scale_rows(k, x_t, x_b, ss_t, rs_t, sc_b, junk_t, junk_b, nblk, width, eps, G):
    nc = k.nc
    for tb in range(nblk):
        k.op(k.dve, lambda: nc.vector.scalar_tensor_tensor(out=junk_t[:, 0:width], in0=x_t[:, tb, :], scalar=1.0,
                                                           in1=x_t[:, tb, :], op0=ALU.mult, op1=ALU.mult,
                                                           accum_out=ss_t[:, tb:tb + 1]),
             reads=[x_b], writes=[junk_b, sc_b])
    k.op(k.pool, lambda: nc.gpsimd.tensor_scalar(out=ss_t[:, 0:nblk], in0=ss_t[:, 0:nblk], scalar1=1.0 / width,
                                                 scalar2=eps, op0=ALU.mult, op1=ALU.add),
         reads=[sc_b], writes=[sc_b])
    k.op(k.pool, lambda: nc.gpsimd.tensor_tensor(out=rs_t[:, 0:nblk], in0=ss_t[:, 0:nblk], in1=G["mhalf"][:, 0:nblk],
                                                 op=ALU.pow),
         reads=[sc_b], writes=[sc_b])


def phase_proj(k, cfg, T, G, which):
    nc = k.nc
    full = which == "F"
    xsrc = T["xf"] if full else T["xo"]
    lens = cfg.L if full else cfg.own
    offs = cfg.foff if full else cfg.ooff
    rc, rs_ = (T["ropeF_c"], T["ropeF_s"]) if full else (T["ropeO_c"], T["ropeO_s"])
    ntm = 2048 if full else 0
    with contextlib.ExitStack() as es:
        ident = G["ident"]
        if full:
            wtm = sb(nc, es, "wtm", [128, NCH, 2048], BF16)
        wfm = sb(nc, es, "wfm", [128, NCH, 512], BF16)
        wfp = sb(nc, es, "wfp", [128, NCH, 512], BF16)
        b_w = Buf()
        wb = T["w_in_b"]
        if full:
            k.dma("sp", wtm[:, :, 0:1536], wb[:, :, 0:1536], writes=[b_w])
            k.dma("sp", wtm[:, :, 1536:2048], wb[:, :, 2560:3072], writes=[b_w], disjoint=True)
            k.dma("sp", wfm[:], wb[:, :, 2048:2560], writes=[b_w], disjoint=True)
            k.dma("sp", wfp[:], T["w_perm_b"][:, :, 512:1024], writes=[b_w], disjoint=True)
        else:
            k.dma("sp", wfm[:], wb[:, :, 1536:2048], writes=[b_w])
            k.dma("sp", wfp[:], T["w_perm_b"][:, :, 0:512], writes=[b_w], disjoint=True)
        xr = Ring([sb(nc, es, f"px{i}", [128, 4, D], F32) for i in range(2)])
        xs = Ring([sb(nc, es, f"pxs{i}", [128, 4, D], BF16) for i in range(2)])
        hT = Ring([sb(nc, es, f"phT{i}", [128, NCH, 512], BF16) for i in range(2)])
        ss = Ring([sb(nc, es, f"pss{i}", [128, 8], F32) for i in range(2)])
        rsd = Ring([sb(nc, es, f"prs{i}", [128, 8], F32) for i in range(2)])
        junk = Ring([sb(nc, es, f"pjk{i}", [128, D], F32) for i in range(1)])
        ct = Ring([sb(nc, es, f"pct{i}", [128, 512], F32) for i in range(2)])
        st_ = Ring([sb(nc, es, f"pst{i}", [128, 512], F32) for i in range(2)])
        t1 = Ring([sb(nc, es, f"pt1{i}", [128, 512], F32) for i in range(2)])
        t2 = Ring([sb(nc, es, f"pt2{i}", [128, 512], F32) for i in range(2)])
        fo = Ring([sb(nc, es, f"pfo{i}", [128, 512], BF16) for i in range(3)])
        if full:
            so = Ring([sb(nc, es, f"pso{i}", [128, 4, 2048], BF16) for i in range(2)])
            zt = sb(nc, es, "pzero", [1, 1536], BF16)
            b_z = Buf()
            k.op(k.dve, lambda: nc.vector.memset(zt[:], 0.0), writes=[b_z])
        tp = Ring([pm(nc, es, f"ptp{i}", [128, 2, 512], BF16) for i in range(2)])
        mm = Ring([pm(nc, es, f"pmm{i}", [128, 512], F32) for i in range(6)])
        dst_fm = T["KT"] if full else T["QT"]

        tiles = [(s, t0) for s in range(cfg.nseq) for t0 in range(0, lens[s], 512)]

        def load_x(idx):
            s, t0 = tiles[idx]
            x_t, x_b = xr.next()
            r0 = int(offs[s]) + t0
            k.dma("sp", x_t[:], xsrc[r0:r0 + 512, :].rearrange("(tb p) c -> p tb c", p=128), writes=[x_b])
            return (x_t, x_b)

        def load_tab(idx):
            s, t0 = tiles[idx]
            r0 = int(offs[s]) + t0
            c_t, c_b = ct.next()
            s_t, s_b = st_.next()
            k.dma("sp", c_t[:], rc[:, r0:r0 + 512], writes=[c_b])
            k.dma("sp", s_t[:], rs_[:, r0:r0 + 512], writes=[s_b])
            return (c_t, c_b, s_t, s_b)

        def prepare(idx, x_t, x_b):
            s, t0 = tiles[idx]
            ss_t, sc_b = ss.next()
            rs_t, _ = rsd.next()
            j_t, j_b = junk.next()
            rms_scale_rows(k, x_t, x_b, ss_t, rs_t, sc_b, j_t, j_b, 4, D, NORM_EPS, G)
            xs_t, xs_b = xs.next()
            for tb in range(4):
                k.op(k.dve, lambda: nc.vector.tensor_scalar(out=xs_t[:, tb, :], in0=x_t[:, tb, :], scalar1=rs_t[:, tb:tb + 1],
                                                            scalar2=None, op0=ALU.mult),
                     reads=[x_b, sc_b], writes=[xs_b], disjoint=True)
            h_t, h_b = hT.next()
            for jj in range(0, NCH, 2):
                p_t, p_b = tp.next()
                for c in range(2):
                    for tb in range(4):
                        k.op(k.pe, lambda: nc.tensor.transpose(out=p_t[:, c, tb * 128:(tb + 1) * 128],
                                                               in_=xs_t[:, tb, (jj + c) * 128:(jj + c + 1) * 128],
                                                               identity=ident[:]),
                             reads=[xs_b], writes=[p_b], disjoint=True)
                for c in range(2):
                    j = jj + c
                    k.op(k.act, lambda: nc.scalar.activation(out=h_t[:, j, :], in_=p_t[:, c, :], func=AF.Identity,
                                                             scale=G["a1"][:, j, s:s + 1], bias=G["modT"][:, j, s:s + 1]),
                         reads=[p_b, G["gbuf"]], writes=[h_b], disjoint=True)
            return (h_t, h_b)

        ev = 0
        ntl = len(tiles)
        xcur = load_x(0)
        hcur = prepare(0, *xcur)
        xnext = load_x(1) if ntl > 1 else None
        tabcur = load_tab(0)
        for idx, (s, t0) in enumerate(tiles):
            h_t, h_b = hcur
            c_t, c_b, s_t, s_b = tabcur
            if idx + 1 < ntl:
                tabnext = load_tab(idx + 1)
            if full and t0 == 0:
                ub = int(offs[s]) + 2 * s
                k.dma("pool", T["U"][ub:ub + 1, :], zt[:], reads=[b_z])
                k.dma("pool", T["U"][ub + 1 + lens[s]:ub + 2 + lens[s], :], zt[:], reads=[b_z])
            if full:
                so_t, so_b = so.next()
                for tb in range(4):
                    for cc in range(4):
                        m_t, m_b = mm.next()
                        for j in range(NCH):
                            k.op(k.pe, lambda: nc.tensor.matmul(m_t[:, :], lhsT=h_t[:, j, tb * 128:(tb + 1) * 128],
                                                                rhs=wtm[:, j, cc * 512:(cc + 1) * 512],
                                                                start=(j == 0), stop=(j == NCH - 1)),
                                 reads=[h_b, b_w], writes=[m_b])
                        if ev % 2 == 0:
                            k.op(k.act, lambda: nc.scalar.copy(out=so_t[:, tb, cc * 512:(cc + 1) * 512], in_=m_t[:, :]),
                                 reads=[m_b], writes=[so_b], disjoint=True)
                        else:
                            k.op(k.dve, lambda: nc.vector.tensor_copy(out=so_t[:, tb, cc * 512:(cc + 1) * 512], in_=m_t[:, :]),
                                 reads=[m_b], writes=[so_b], disjoint=True)
                        ev += 1
                r0 = int(offs[s]) + t0
                ub = r0 + 2 * s + 1
                k.dma("pool", T["U"][ub:ub + 512, :].rearrange("(tb p) c -> p tb c", p=128), so_t[:, :, 0:1536], reads=[so_b])
                k.dma("pool", T["Vd"][r0:r0 + 512, :].rearrange("(tb p) c -> p tb c", p=128), so_t[:, :, 1536:2048], reads=[so_b])
            if idx + 1 < ntl:
                hnext = prepare(idx + 1, *xnext)
                xnext = load_x(idx + 2) if idx + 2 < ntl else None
            for n in range(4):
                m1_t, m1_b = mm.next()
                m2_t, m2_b = mm.next()
                for j in range(NCH):
                    k.op(k.pe, lambda: nc.tensor.matmul(m1_t[:, :], lhsT=wfm[:, j, n * 128:(n + 1) * 128], rhs=h_t[:, j, :],
                                                        start=(j == 0), stop=(j == NCH - 1)),
                         reads=[h_b, b_w], writes=[m1_b])
                for j in range(NCH):
                    k.op(k.pe, lambda: nc.tensor.matmul(m2_t[:, :], lhsT=wfp[:, j, n * 128:(n + 1) * 128], rhs=h_t[:, j, :],
                                                        start=(j == 0), stop=(j == NCH - 1)),
                         reads=[h_b, b_w], writes=[m2_b])
                a_t, a_b = t1.next()
                b_t, b_b = t2.next()
                f_t, f_b = fo.next()
                k.op(k.dve, lambda: nc.vector.tensor_tensor(out=a_t[:], in0=m1_t[:, :], in1=c_t[:], op=ALU.mult),
                     reads=[m1_b, c_b], writes=[a_b])
                k.op(k.dve, lambda: nc.vector.tensor_tensor(out=b_t[:], in0=m2_t[:, :], in1=s_t[:], op=ALU.mult),
                     reads=[m2_b, s_b], writes=[b_b])
                k.op(k.pool, lambda: nc.gpsimd.tensor_tensor(out=f_t[:], in0=a_t[:], in1=b_t[:], op=ALU.add),
                     reads=[a_b, b_b], writes=[f_b])
                r0 = int(offs[s]) + t0
                k.dma("pool", dst_fm[n * 128:(n + 1) * 128, r0:r0 + 512], f_t[:], reads=[f_b])
            if idx + 1 < ntl:
                hcur = hnext
                tabcur = tabnext
    k.barrier()


PHASES = ["weights", "mod", "projF", "projO", "filter", "hyena", "attn", "m1", "m2"]


def build(cfg, upto="m2"):
    nc = bass.Bass("TRN2", target_bir_lowering=False)
    T = declare(nc, cfg)
    last = PHASES.index(upto)
    with contextlib.ExitStack() as es:
        k = K(nc, es)
        ns = cfg.nseq
        G = dict(gbuf=Buf(), kh_buf={"p": Buf(), "s": Buf()})
        G["modT"] = sb(nc, es, "modT", [128, 48, ns], F32)
        G["a1"] = sb(nc, es, "a1", [128, NCH, ns], F32)
        G["a2"] = sb(nc, es, "a2", [128, NCH, ns], F32)
        G["ident"] = sb(nc, es, "ident", [128, 128], BF16)
        G["identf"] = sb(nc, es, "identf", [128, 128], F32)
        G["mhalf"] = sb(nc, es, "mhalf", [128, 8], F32)
        k.dma("sp", G["identf"][:], T["ident"][:, :], writes=[G["gbuf"]])
        k.op(k.dve, lambda: nc.vector.tensor_copy(out=G["ident"][:], in_=G["identf"][:]), reads=[G["gbuf"]], writes=[G["gbuf"]])
        k.op(k.dve, lambda: nc.vector.memset(G["mhalf"][:], -0.5), writes=[G["gbuf"]], disjoint=True)
        k.barrier()
        steps = [
            lambda: phase_weights(k, cfg, T),
            lambda: phase_mod(k, cfg, T, G),
            lambda: phase_proj(k, cfg, T, G, "F"),
            lambda: phase_proj(k, cfg, T, G, "O"),
            lambda: phase_filter(k, cfg, T, G),
            lambda: phase_hyena(k, cfg, T, G),
            lambda: phase_attn(k, cfg, T, G),
            lambda: phase_m1(k, cfg, T, G),
            lambda: phase_m2(k, cfg, T, G),
        ]
        for i, st in enumerate(steps):
            if i <= last:
                st()
        k.final_wait()
    return nc


def chunkT(v, ncols):
    return np.ascontiguousarray(np.asarray(v, np.float32).reshape(ncols, 128).T)


def host_inputs(cfg, inp, core):
    f32 = np.float32
    b, r = core // cfg.NQ, core % cfg.NQ
    own = cfg.own_p
    xs = [np.asarray(inp["x_sample"][core * cfg.NS + i], f32) for i in range(cfg.NS)]
    xp = np.asarray(inp["x_prompt"][b], f32)
    m = {}
    m["xf"] = np.ascontiguousarray(np.concatenate([xp] + xs, 0))
    m["xo"] = np.ascontiguousarray(np.concatenate([xp[r * own:(r + 1) * own]] + xs, 0))
    cs = np.stack([np.asarray(inp["c_prompt"][b], f32)] + [np.asarray(inp["c_sample"][core * cfg.NS + i], f32)
                                                             for i in range(cfg.NS)], 0)
    m["cT"] = np.ascontiguousarray(cs.reshape(cfg.nseq, NCH, 128).transpose(2, 1, 0))
    m["w_ada"] = np.ascontiguousarray(inp["w_ada"][0], f32)
    m["b_ada"] = np.ascontiguousarray(inp["b_ada"][0:1], f32)
    m["b_adaT"] = chunkT(inp["b_ada"][0], 48)
    m["nw1T"] = chunkT(inp["norm1_w"][0], NCH)
    m["nw2T"] = chunkT(inp["norm2_w"][0], NCH)
    m["hnwT"] = chunkT(inp["hyena_norm_w"][0], 4)
    w_in = np.asarray(inp["w_in"][0], f32)
    m["w_in"] = np.ascontiguousarray(w_in)
    p64 = rot_perm()
    pq = np.concatenate([1536 + h * 64 + p64 for h in range(8)])
    pk = np.concatenate([2048 + h * 64 + p64 for h in range(8)])
    m["w_perm"] = np.ascontiguousarray(w_in[:, np.concatenate([pq, pk])])
    m["w_out"] = np.ascontiguousarray(inp["w_out"][0], f32)
    m["w_mlp1"] = np.ascontiguousarray(inp["w_mlp1"][0], f32)
    m["w_mlp2"] = np.ascontiguousarray(inp["w_mlp2"][0], f32)
    m["ident"] = np.eye(128, dtype=f32)
    posF = np.concatenate([np.arange(L) for L in cfg.L])
    posO = np.concatenate([r * own + np.arange(own)] + [np.arange(cfg.Ls)] * cfg.NS)
    m["ropeF_c"], m["ropeF_s"] = rope_tables(posF)
    m["ropeO_c"], m["ropeO_s"] = rope_tables(posO)
    m["conv_w"] = np.ascontiguousarray(inp["conv_w"][0], f32)
    m["conv_b"] = np.ascontiguousarray(inp["conv_b"][0:1], f32)
    m["hyena_bias"] = np.ascontiguousarray(inp["hyena_bias"][0:1], f32)
    m["final_w"] = np.ascontiguousarray(np.asarray(inp["final_w"], f32)[None, :])
    m["subln_w"] = np.ascontiguousarray(inp["subln_w"][0:1], f32)
    m["lam4"] = np.ascontiguousarray(np.stack([inp["lambda_q1"][0], inp["lambda_k1"][0], inp["lambda_q2"][0],
                                               inp["lambda_k2"][0]], 0), f32)
    if cfg.debug and "Yh_in" in cfg.debug:
        m["Yh"] = np.ascontiguousarray(inp["_Yh"][core], f32)
    host_tables(cfg, inp, core, m)
    return m


def run(cfg, inp, upto="m2", ncores=8):
    nc = build(cfg, upto)
    maps = [host_inputs(cfg, inp, c) for c in range(ncores)]
    names = set()
    res = run_bass_kernel_spmd(nc, maps, core_ids=list(range(ncores)))
    return res.results


def kernel(**inputs):
    inp = {k_: np.asarray(v) for k_, v in inputs.items()}
    cfg = Cfg()
    res = run(cfg, inp)
    B, S, _ = inp["x_prompt"].shape
    yp = np.zeros((B, S, D), np.float32)
    ysm = np.zeros(inp["x_sample"].shape, np.float32)
    own = cfg.own_p
    for c in range(8):
        y = res[c]["y"]
        b, r = c // cfg.NQ, c % cfg.NQ
        yp[b, r * own:(r + 1) * own] = y[0:own]
        for i in range(cfg.NS):
            ysm[c * cfg.NS + i] = y[own + i * cfg.Ls: own + (i + 1) * cfg.Ls]
    return (yp, ysm)


LAMBDA_INIT = 0.8 - 0.6 * math.exp(-0.3 * 0)


def bc_ap(ap2d, nparts=128):
    n = 1
    for d in ap2d.shape:
        n *= d
    return bass.AP(tensor=ap2d.tensor, offset=ap2d.offset, ap=[[0, nparts], [1, n]])


def phase_attn(k, cfg, T, G):
    nc = k.nc
    Lmax, omax = max(cfg.L), max(cfg.own)
    with contextlib.ExitStack() as es:
        lamt = sb(nc, es, "lamt", [128, 256], F32)
        lj = sb(nc, es, "lj", [128, 64], F32)
        lacc = sb(nc, es, "lacc", [128, 4], F32)
        nlam = sb(nc, es, "nlam", [128, 1], F32)
        slw = sb(nc, es, "slw", [128, 128], F32)
        b_l, b_sl = Buf(), Buf()
        k.dma("sp", lamt[:], bc_ap(T["lam4"]), writes=[b_l])
        k.dma("sp", slw[:], bc_ap(T["subln_w"]), writes=[b_sl])
        k.op(k.dve, lambda: nc.vector.scalar_tensor_tensor(out=lj[:], in0=lamt[:, 0:64], scalar=1.0, in1=lamt[:, 64:128],
                                                           op0=ALU.mult, op1=ALU.mult, accum_out=lacc[:, 0:1]),
             reads=[b_l], writes=[b_l])
        k.op(k.dve, lambda: nc.vector.scalar_tensor_tensor(out=lj[:], in0=lamt[:, 128:192], scalar=1.0, in1=lamt[:, 192:256],
                                                           op0=ALU.mult, op1=ALU.mult, accum_out=lacc[:, 1:2]),
             reads=[b_l], writes=[b_l])
        k.op(k.act, lambda: nc.scalar.activation(out=lacc[:, 2:4], in_=lacc[:, 0:2], func=AF.Exp), reads=[b_l], writes=[b_l])
        k.op(k.dve, lambda: nc.vector.tensor_tensor(out=nlam[:], in0=lacc[:, 3:4], in1=lacc[:, 2:3], op=ALU.subtract),
             reads=[b_l], writes=[b_l])
        k.op(k.dve, lambda: nc.vector.tensor_scalar(out=nlam[:], in0=nlam[:], scalar1=-LAMBDA_INIT, scalar2=None, op0=ALU.add),
             reads=[b_l], writes=[b_l])
        k.op(k.dve, lambda: nc.vector.tensor_scalar(out=slw[:], in0=slw[:], scalar1=(1.0 - LAMBDA_INIT), scalar2=None, op0=ALU.mult),
             reads=[b_sl], writes=[b_sl])

        ktr = Ring([sb(nc, es, f"akt{i}", [128, Lmax], BF16) for i in range(2)])
        v1r = Ring([sb(nc, es, f"av1{i}", [128, Lmax // 128, 128], BF16) for i in range(2)])
        qtr = Ring([sb(nc, es, f"aqt{i}", [128, omax], BF16) for i in range(2)])
        er = Ring([sb(nc, es, f"ae{i}", [128, 512], BF16) for i in range(6)])
        osr = Ring([sb(nc, es, f"aos{i}", [128, 4, 128], BF16) for i in range(2)])
        o_r = Ring([sb(nc, es, f"ao{i}", [128, 128], F32) for i in range(2)])
        jk = sb(nc, es, "ajk", [128, 128], F32)
        b_jk = Buf()
        st = Ring([sb(nc, es, f"ast{i}", [128, 8], F32) for i in range(2)])
        zacc = [sb(nc, es, f"azc{i}", [128, 512], F32) for i in range(4)]
        zb = [Buf(), Buf(), Buf(), Buf()]
        otr = Ring([sb(nc, es, f"aot{i}", [128, 2, 512], F32) for i in range(2)])
        ones1 = sb(nc, es, "aones", [128, 1], F32)
        k.op(k.dve, lambda: nc.vector.memset(ones1[:], 1.0), writes=[b_l], disjoint=True)
        sbank = Ring([pm(nc, es, f"asb{i}", [128, 512], F32) for i in range(3)])
        obank = [pm(nc, es, f"aob{i}", [128, 512], F32) for i in range(2)]
        ob_b = [Buf(), Buf()]
        ebank = [pm(nc, es, f"aeb{i}", [128, 512], F32) for i in range(3)]
        regs = Ring([ebank[b][:, c * 129:(c + 1) * 129] for (b, c) in ((0, 0), (0, 1), (0, 2), (1, 0), (1, 1), (1, 2), (2, 0), (2, 1))])
        fence_b = Buf()

        for s in range(cfg.nseq):
            L, own = cfg.L[s], cfg.own[s]
            f0, o0 = int(cfg.foff[s]), int(cfg.ooff[s])
            nkb = L // 128
            for h in range(4):
                kt, kt_b = ktr.next()
                v1, v1_b = v1r.next()
                qt, qt_b = qtr.next()
                k.dma("sp", kt[:, 0:L], T["KT"][h * 128:(h + 1) * 128, f0:f0 + L], writes=[kt_b])
                k.dma("sp", v1[:, 0:nkb, :], T["Vd"][f0:f0 + L, h * 128:(h + 1) * 128].rearrange("(kb p) c -> p kb c", p=128),
                      writes=[v1_b])
                k.dma("sp", qt[:, 0:own], T["QT"][h * 128:(h + 1) * 128, o0:o0 + own], writes=[qt_b])
                for qc in range(own // 512):
                    def qk(kb):
                        out = []
                        for e in range(2):
                            sp_, sp_b = sbank.next()
                            lo = e * 64
                            k.op(k.pe, lambda: nc.tensor.matmul(sp_[:, :], lhsT=kt[lo:lo + 64, kb * 128:(kb + 1) * 128],
                                                                rhs=qt[lo:lo + 64, qc * 512:(qc + 1) * 512], start=True, stop=True),
                                 reads=[kt_b, qt_b], writes=[sp_b])
                            e_t, e_b = er.next()
                            k.op(k.act, lambda: nc.scalar.activation(out=e_t[:], in_=sp_[:, :], func=AF.Exp, scale=0.125),
                                 reads=[sp_b], writes=[e_b])
                            out.append((e_t, e_b))
                        return out
                    pend = qk(0)
                    used_pool = False
                    used_pool0 = False
                    for kb in range(nkb):
                        nxt = qk(kb + 1) if kb + 1 < nkb else None
                        for e in range(2):
                            e_t, e_b = pend[e]
                            k.op(k.pe, lambda: nc.tensor.matmul(obank[e][:, :], lhsT=v1[:, kb, :], rhs=e_t[:], start=(kb == 0), stop=(kb == nkb - 1)),
                                 reads=[e_b, v1_b], writes=[ob_b[e]])
                            zi, E = e, k.dve
                            first = (kb == 0)
                            if first:
                                k.op(E, lambda: E.eng.tensor_copy(out=zacc[zi][:], in_=e_t[:]), reads=[e_b], writes=[zb[zi]])
                            else:
                                k.op(E, lambda: E.eng.tensor_tensor(out=zacc[zi][:], in0=zacc[zi][:], in1=e_t[:], op=ALU.add),
                                     reads=[e_b, zb[zi]], writes=[zb[zi]])
                        pend = nxt
                    ot, ot_b = otr.next()
                    for e in range(2):
                        k.op(k.act, lambda: nc.scalar.copy(out=ot[:, e, :], in_=obank[e][:, :]), reads=[ob_b[e]], writes=[ot_b], disjoint=True)
                    rr = []
                    for qs in range(4):
                        pair = []
                        for e in range(2):
                            rg, rg_b = regs.next()
                            k.op(k.pe, lambda: nc.tensor.transpose(out=rg[:, 0:128], in_=ot[:, e, qs * 128:(qs + 1) * 128], identity=G["identf"][:]),
                                 reads=[ot_b, G["gbuf"]], writes=[rg_b])
                            zl = ([0, 3] if used_pool0 else [0]) if e == 0 else ([1, 2] if used_pool else [1])
                            for zi_, zi in enumerate(zl):
                                k.op(k.pe, lambda: nc.tensor.matmul(rg[:, 128:129], lhsT=zacc[zi][:, qs * 128:(qs + 1) * 128], rhs=ones1[:, 0:1],
                                                                    start=(zi_ == 0), stop=(zi_ == len(zl) - 1), skip_group_check=True),
                                     reads=[zb[zi], b_l], writes=[rg_b], disjoint=True)
                            pair.append((rg, rg_b))
                        rr.append(pair)
                        if qs % 2 == 1:
                            k.op(k.pe, lambda: nc.tensor.matmul(ebank[2][:, 400:401], lhsT=G["identf"][:, :], rhs=ones1[:, 0:1], start=True, stop=True,
                                                                skip_group_check=True),
                                 reads=[b_l, G["gbuf"]], writes=[fence_b])
                    os_t, os_b = osr.next()
                    for qs in range(4):
                        (a1_, a1b), (a2_, a2b) = rr[qs]
                        s_t, s_b = st.next()
                        o_t, o_b = o_r.next()
                        k.op(k.dve, lambda: nc.vector.reciprocal(out=s_t[:, 0:1], in_=a1_[:, 128:129]), reads=[a1b, a2b, fence_b], writes=[s_b])
                        k.op(k.dve, lambda: nc.vector.reciprocal(out=s_t[:, 1:2], in_=a2_[:, 128:129]), reads=[a2b], writes=[s_b])
                        k.op(k.dve, lambda: nc.vector.tensor_tensor(out=s_t[:, 2:3], in0=s_t[:, 1:2], in1=nlam[:], op=ALU.mult),
                             reads=[s_b, b_l], writes=[s_b])
                        k.op(k.dve, lambda: nc.vector.tensor_scalar(out=o_t[:], in0=a1_[:, 0:128], scalar1=s_t[:, 0:1], scalar2=None,
                                                                    op0=ALU.mult), reads=[a1b, s_b], writes=[o_b])
                        k.op(k.dve, lambda: nc.vector.scalar_tensor_tensor(out=o_t[:], in0=a2_[:, 0:128], scalar=s_t[:, 2:3], in1=o_t[:],
                                                                           op0=ALU.mult, op1=ALU.add),
                             reads=[a2b, s_b, o_b], writes=[o_b])
                        k.op(k.dve, lambda: nc.vector.scalar_tensor_tensor(out=jk[:], in0=o_t[:], scalar=1.0, in1=o_t[:], op0=ALU.mult,
                                                                           op1=ALU.mult, accum_out=s_t[:, 3:4]),
                             reads=[o_b], writes=[b_jk, s_b])
                        k.op(k.pool, lambda: nc.gpsimd.tensor_scalar(out=s_t[:, 4:5], in0=s_t[:, 3:4], scalar1=1.0 / 128, scalar2=SUBLN_EPS,
                                                                     op0=ALU.mult, op1=ALU.add), reads=[s_b], writes=[s_b])
                        k.op(k.pool, lambda: nc.gpsimd.tensor_tensor(out=s_t[:, 5:6], in0=s_t[:, 4:5], in1=G["mhalf"][:, 0:1], op=ALU.pow),
                             reads=[s_b], writes=[s_b])
                        k.op(k.dve, lambda: nc.vector.scalar_tensor_tensor(out=os_t[:, qs, :], in0=o_t[:], scalar=s_t[:, 5:6], in1=slw[:],
                                                                           op0=ALU.mult, op1=ALU.mult),
                             reads=[o_b, s_b, b_sl], writes=[os_b], disjoint=True)
                    r0 = o0 + qc * 512
                    k.dma("pool", T["Oa"][r0:r0 + 512, h * 128:(h + 1) * 128].rearrange("(qs p) c -> p qs c", p=128), os_t[:], reads=[os_b])
    k.barrier()


def transposes_to(k, G, src_t, src_b, tp, dst_t, dst_b, evac):
    nc = k.nc
    for jj in range(0, NCH, 2):
        p_t, p_b = tp.next()
        for c in range(2):
            for tb in range(4):
                k.op(k.pe, lambda: nc.tensor.transpose(out=p_t[:, c, tb * 128:(tb + 1) * 128],
                                                       in_=src_t[:, tb, (jj + c) * 128:(jj + c + 1) * 128], identity=G["ident"][:]),
                     reads=[src_b], writes=[p_b], disjoint=True)
        for c in range(2):
            evac(jj + c, dst_t[:, jj + c, :], p_t[:, c, :], p_b, dst_b)


def phase_m1(k, cfg, T, G):
    nc = k.nc
    with contextlib.ExitStack() as es:
        wo = sb(nc, es, "wo", [128, NCH, D], BF16)
        hnw = sb(nc, es, "hnw", [128, 4], F32)
        b_w = Buf()
        k.dma("sp", wo[:], T["w_out_b"][:, :, :], writes=[b_w])
        k.dma("sp", hnw[:], T["hnwT"][:, :], writes=[b_w], disjoint=True)
        g1 = Ring([sb(nc, es, f"g1{i}", [128, D], F32) for i in range(2)])
        xr = Ring([sb(nc, es, f"mx{i}", [128, 4, D], F32) for i in range(2)])
        yr = Ring([sb(nc, es, f"my{i}", [128, 4, 512], F32) for i in range(2)])
        mix = Ring([sb(nc, es, f"mmix{i}", [128, 4, D], BF16) for i in range(2)])
        mT = Ring([sb(nc, es, f"mmT{i}", [128, NCH, 512], BF16) for i in range(2)])
        ss = Ring([sb(nc, es, f"mss{i}", [128, 8], F32) for i in range(2)])
        rsd = Ring([sb(nc, es, f"mrs{i}", [128, 8], F32) for i in range(2)])
        junk = Ring([sb(nc, es, "mjk", [128, D], F32)])
        tmp = Ring([sb(nc, es, f"mtmp{i}", [128, 512], F32) for i in range(2)])
        tp = Ring([pm(nc, es, f"mtp{i}", [128, 2, 512], BF16) for i in range(2)])
        mm = Ring([pm(nc, es, f"mmm{i}", [128, 512], F32) for i in range(4)])
        for s in range(cfg.nseq):
            g_t, g_b = g1.next()
            k.dma("sp", g_t[:], bc_ap(T["Gd"][s, 0:1, :]), writes=[g_b])
            for t0 in range(0, cfg.own[s], 512):
                r0 = int(cfg.ooff[s]) + t0
                x_t, x_b = xr.next()
                y_t, y_b = yr.next()
                m_t, m_b = mix.next()
                k.dma("sp", x_t[:], T["xo"][r0:r0 + 512, :].rearrange("(tb p) c -> p tb c", p=128), writes=[x_b])
                k.dma("sp", y_t[:], T["Yh"][r0:r0 + 512, :].rearrange("(tb p) c -> p tb c", p=128), writes=[y_b])
                k.dma("sp", m_t[:, :, 512:1024], T["Oa"][r0:r0 + 512, :].rearrange("(tb p) c -> p tb c", p=128), writes=[m_b])
                ss_t, sc_b = ss.next()
                rs_t, _ = rsd.next()
                j_t, j_b = junk.next()
                rms_scale_rows(k, y_t, y_b, ss_t, rs_t, sc_b, j_t, j_b, 4, 512, NORM_EPS, G)
                for tb in range(4):
                    k.op(k.dve, lambda: nc.vector.tensor_scalar(out=m_t[:, tb, 0:512], in0=y_t[:, tb, :], scalar1=rs_t[:, tb:tb + 1],
                                                                scalar2=None, op0=ALU.mult),
                         reads=[y_b, sc_b], writes=[m_b], disjoint=True)
                t_t, t_b = mT.next()

                def evac(j, o, i, pb, db):
                    if j < 4:
                        k.op(k.act, lambda: nc.scalar.activation(out=o, in_=i, func=AF.Identity, scale=hnw[:, j:j + 1]),
                             reads=[pb, b_w], writes=[db], disjoint=True)
                    else:
                        k.op(k.act, lambda: nc.scalar.copy(out=o, in_=i), reads=[pb], writes=[db], disjoint=True)
                transposes_to(k, G, m_t, m_b, tp, t_t, t_b, evac)
                for tb in range(4):
                    for cc in range(2):
                        p_t, p_b = mm.next()
                        for j in range(NCH):
                            k.op(k.pe, lambda: nc.tensor.matmul(p_t[:, :], lhsT=t_t[:, j, tb * 128:(tb + 1) * 128],
                                                                rhs=wo[:, j, cc * 512:(cc + 1) * 512], start=(j == 0), stop=(j == NCH - 1)),
                                 reads=[t_b, b_w], writes=[p_b])
                        q_t, q_b = tmp.next()
                        k.op(k.dve, lambda: nc.vector.tensor_tensor(out=q_t[:], in0=p_t[:, :], in1=g_t[:, cc * 512:(cc + 1) * 512], op=ALU.mult),
                             reads=[p_b, g_b], writes=[q_b])
                        k.op(k.pool, lambda: nc.gpsimd.tensor_tensor(out=x_t[:, tb, cc * 512:(cc + 1) * 512], in0=q_t[:],
                                                                     in1=x_t[:, tb, cc * 512:(cc + 1) * 512], op=ALU.add),
                             reads=[q_b, x_b], writes=[x_b], disjoint=True)
                k.dma("pool", T["X1"][r0:r0 + 512, :].rearrange("(tb p) c -> p tb c", p=128), x_t[:], reads=[x_b])
    k.barrier()


def phase_m2(k, cfg, T, G):
    nc = k.nc
    with contextlib.ExitStack() as es:
        w2 = sb(nc, es, "w2", [128, 32, D], BF16)
        fw = sb(nc, es, "fw", [128, D], F32)
        b_w = Buf()
        for i in range(4):
            k.dma("sp", w2[:, i * 8:(i + 1) * 8, :], T["w_mlp2_b"][:, i * 8:(i + 1) * 8, :], writes=[b_w], disjoint=True)
        k.dma("sp", fw[:], bc_ap(T["final_w"]), writes=[b_w], disjoint=True)
        w1r = Ring([sb(nc, es, f"w1r{i}", [128, NCH, 512], BF16) for i in range(3)])
        g2 = Ring([sb(nc, es, f"g2{i}", [128, D], F32) for i in range(2)])
        xr = Ring([sb(nc, es, f"nx{i}", [128, 4, D], F32) for i in range(2)])
        xs = Ring([sb(nc, es, f"nxs{i}", [128, 4, D], BF16) for i in range(1)])
        hT = Ring([sb(nc, es, f"nhT{i}", [128, NCH, 512], BF16) for i in range(1)])
        aT = Ring([sb(nc, es, f"naT{i}", [128, 32, 512], BF16) for i in range(1)])
        rl = Ring([sb(nc, es, f"nrl{i}", [128, 512], F32) for i in range(3)])
        ss = Ring([sb(nc, es, f"nss{i}", [128, 8], F32) for i in range(2)])
        rsd = Ring([sb(nc, es, f"nrs{i}", [128, 8], F32) for i in range(2)])
        junk = Ring([sb(nc, es, "njk", [128, D], F32)])
        tmp = Ring([sb(nc, es, f"ntmp{i}", [128, 512], F32) for i in range(2)])
        tp = Ring([pm(nc, es, f"ntp{i}", [128, 2, 512], BF16) for i in range(2)])
        mm = Ring([pm(nc, es, f"nmm{i}", [128, 512], F32) for i in range(6)])
        ev = 0
        for s in range(cfg.nseq):
            g_t, g_b = g2.next()
            k.dma("sp", g_t[:], bc_ap(T["Gd"][s, 1:2, :]), writes=[g_b])
            for t0 in range(0, cfg.own[s], 512):
                r0 = int(cfg.ooff[s]) + t0
                x_t, x_b = xr.next()
                k.dma("sp", x_t[:], T["X1"][r0:r0 + 512, :].rearrange("(tb p) c -> p tb c", p=128), writes=[x_b])
                ss_t, sc_b = ss.next()
                rs_t, _ = rsd.next()
                j_t, j_b = junk.next()
                rms_scale_rows(k, x_t, x_b, ss_t, rs_t, sc_b, j_t, j_b, 4, D, NORM_EPS, G)
                xs_t, xs_b = xs.next()
                for tb in range(4):
                    k.op(k.dve, lambda: nc.vector.tensor_scalar(out=xs_t[:, tb, :], in0=x_t[:, tb, :], scalar1=rs_t[:, tb:tb + 1],
                                                                scalar2=None, op0=ALU.mult),
                         reads=[x_b, sc_b], writes=[xs_b], disjoint=True)
                h_t, h_b = hT.next()

                def evac(j, o, i, pb, db):
                    k.op(k.act, lambda: nc.scalar.activation(out=o, in_=i, func=AF.Identity, scale=G["a2"][:, j, s:s + 1],
                                                             bias=G["modT"][:, 24 + j, s:s + 1]),
                         reads=[pb, G["gbuf"]], writes=[db], disjoint=True)
                transposes_to(k, G, xs_t, xs_b, tp, h_t, h_b, evac)
                a_t, a_b = aT.next()
                for mc in range(8):
                    w_t, w_b = w1r.next()
                    k.dma("sp", w_t[:], T["w_mlp1_b"][:, :, mc * 512:(mc + 1) * 512], writes=[w_b])
                    for mi in range(4):
                        m = mc * 4 + mi
                        p_t, p_b = mm.next()
                        for j in range(NCH):
                            k.op(k.pe, lambda: nc.tensor.matmul(p_t[:, :], lhsT=w_t[:, j, mi * 128:(mi + 1) * 128], rhs=h_t[:, j, :],
                                                                start=(j == 0), stop=(j == NCH - 1)),
                                 reads=[w_b, h_b], writes=[p_b])
                        r_t, r_b = rl.next()
                        k.op(k.act, lambda: nc.scalar.activation(out=r_t[:], in_=p_t[:, :], func=AF.Relu), reads=[p_b], writes=[r_b])
                        E = k.dve if ev % 2 == 0 else k.pool
                        ev += 1
                        k.op(E, lambda: E.eng.tensor_tensor(out=a_t[:, m, :], in0=r_t[:], in1=r_t[:], op=ALU.mult),
                             reads=[r_b], writes=[a_b], disjoint=True)
                for tb in range(4):
                    for cc in range(2):
                        p_t, p_b = mm.next()
                        for m in range(32):
                            k.op(k.pe, lambda: nc.tensor.matmul(p_t[:, :], lhsT=a_t[:, m, tb * 128:(tb + 1) * 128],
                                                                rhs=w2[:, m, cc * 512:(cc + 1) * 512], start=(m == 0), stop=(m == 31)),
                                 reads=[a_b, b_w], writes=[p_b])
                        q_t, q_b = tmp.next()
                        k.op(k.dve, lambda: nc.vector.tensor_tensor(out=q_t[:], in0=p_t[:, :], in1=g_t[:, cc * 512:(cc + 1) * 512], op=ALU.mult),
                             reads=[p_b, g_b], writes=[q_b])
                        k.op(k.pool, lambda: nc.gpsimd.tensor_tensor(out=x_t[:, tb, cc * 512:(cc + 1) * 512], in0=q_t[:],
                                                                     in1=x_t[:, tb, cc * 512:(cc + 1) * 512], op=ALU.add),
                             reads=[q_b, x_b], writes=[x_b], disjoint=True)
                ss_t, sc_b = ss.next()
                rs_t, _ = rsd.next()
                rms_scale_rows(k, x_t, x_b, ss_t, rs_t, sc_b, j_t, j_b, 4, D, NORM_EPS, G)
                for tb in range(4):
                    k.op(k.dve, lambda: nc.vector.scalar_tensor_tensor(out=x_t[:, tb, :], in0=x_t[:, tb, :], scalar=rs_t[:, tb:tb + 1],
                                                                       in1=fw[:], op0=ALU.mult, op1=ALU.mult),
                         reads=[x_b, sc_b, b_w], writes=[x_b], disjoint=True)
                k.dma("pool", T["y"][r0:r0 + 512, :].rearrange("(tb p) c -> p tb c", p=128), x_t[:], reads=[x_b])
    k.barrier()


def fft_tables(L, n1_start, m_own):
    f32 = np.float32
    J = L // 128
    N = 2 * L
    G = 128 // J if J <= 128 else 1
    t = {}
    n1 = np.arange(256)[:, None].astype(np.float64)
    k1 = np.arange(256)[None, :].astype(np.float64)
    ph = 2 * np.pi * n1 * k1 / 256.0
    FA = np.stack([np.cos(ph), -np.sin(ph)], 0).reshape(2, 2, 128, 256)
    t["FA"] = FA.astype(f32)
    n2 = np.arange(J)[None, :].astype(np.float64)
    kk = np.arange(256)[:, None].astype(np.float64)
    th = 2 * np.pi * n2 * kk / N
    tw = np.stack([np.cos(th), -np.sin(th), np.sin(th)], 0)
    t["twA"] = np.ascontiguousarray(tw.reshape(3, 2, 128, J).transpose(2, 0, 1, 3)).astype(f32)
    a = np.arange(J)[:, None].astype(np.float64)
    b = np.arange(J)[None, :].astype(np.float64)
    pj = 2 * np.pi * a * b / J
    FBr, FBi = np.cos(pj), -np.sin(pj)
    blk = lambda M: np.kron(np.eye(G), M)
    t["FB"] = np.stack([blk(FBr), blk(FBi), blk(-FBi)], 0).astype(f32)
    q = np.arange(128)
    k1p, n2q = q // J, q % J
    g = np.arange(256 // G)
    k1g = g[None, :] * G + k1p[:, None]
    thb = 2 * np.pi * n2q[:, None] * k1g / N
    t["twB"] = np.stack([np.cos(thb), -np.sin(thb), np.sin(thb)], 1).astype(f32)
    n1o = (n1_start + np.arange(m_own))[None, :].astype(np.float64)
    k1c = np.arange(256)[:, None].astype(np.float64)
    ph2 = 2 * np.pi * n1o * k1c / 256.0
    IA = np.stack([np.cos(ph2) / N, -np.sin(ph2) / N], 0).reshape(2, 2, 128, m_own)
    t["IA"] = IA.astype(f32)
    sel = np.zeros((128, m_own), f32)
    sel[n1_start + np.arange(m_own), np.arange(m_own)] = 1.0
    t["Sel"] = sel
    z, decay = filter_feats(L)
    t["zTf"] = np.ascontiguousarray(z.T)
    zb = np.concatenate([z[0:1], z[:0:-1]], 0)
    t["zTb"] = np.ascontiguousarray(zb.T)
    t["decF"] = decay
    db = np.concatenate([np.zeros((1, HW), f32), decay[:0:-1]], 0)
    t["decB"] = np.ascontiguousarray(db)
    return t


def host_tables(cfg, inp, core, m):
    r = core % cfg.NQ
    mo_p = cfg.own_p // (cfg.Lp // 128)
    tp_ = fft_tables(cfg.Lp, r * mo_p, mo_p)
    ts_ = fft_tables(cfg.Ls, 0, 128)
    for k_, v in tp_.items():
        m[k_ + "_p"] = v
    for k_, v in ts_.items():
        m[k_ + "_s"] = v
    f32 = np.float32
    m["filt_w1"] = np.ascontiguousarray(inp["filt_w1"][0], f32)
    m["filt_w2"] = np.ascontiguousarray(inp["filt_w2"][0], f32)
    m["filt_w3"] = np.ascontiguousarray(inp["filt_w3"][0], f32)
    m["filt_w4"] = np.ascontiguousarray(inp["filt_w4"][0], f32)
    m["filt_bf"] = np.ascontiguousarray(np.stack([inp["filt_b1"][0], inp["filt_b2"][0], inp["filt_b3"][0], inp["filt_freq"][0]], 1), f32)


def declare_hyena(nc, cfg, T, inp, scr):
    for ty, L in (("p", cfg.Lp), ("s", cfg.Ls)):
        J = L // 128
        G = 128 // J
        mo = (cfg.own_p // J) if ty == "p" else 128
        inp("FA_" + ty, [2, 2, 128, 256]); inp("twA_" + ty, [128, 3, 2, J]); inp("FB_" + ty, [3, 128, 128])
        inp("twB_" + ty, [128, 3, 256 // G]); inp("IA_" + ty, [2, 2, 128, mo]); inp("Sel_" + ty, [128, mo])
        inp("zTf_" + ty, [FILTER_EMB, L]); inp("zTb_" + ty, [FILTER_EMB, L]); inp("decF_" + ty, [L, HW]); inp("decB_" + ty, [L, HW])
        scr("Bd_" + ty, [2, 256 * J, 512]); scr("Cd_" + ty, [2, 256 * J, 512]); scr("Kh_" + ty, [2, 256 * J, 512])
    inp("filt_w1", [FILTER_EMB, FORDER]); inp("filt_w2", [FORDER, FORDER]); inp("filt_w3", [FORDER, FORDER])
    inp("filt_w4", [FORDER, 2 * HW]); inp("filt_bf", [FORDER, 4])
    scr("XS", [2, cfg.TO, 512])


def fft_consts(k, es, T, ty, J, mo, need_inv):
    nc = k.nc
    ng = 2 * J
    c = {}
    b = Buf()
    c["buf"] = b
    stg = sb(nc, es, "fst", [128, 1024], F32)
    b_s = Buf()
    c["FA"] = sb(nc, es, "FA", [128, 2, 2, 256], BF16)
    k.dma("sp", stg[:, 0:1024].rearrange("n (a h k) -> n a h k", a=2, h=2), T["FA_" + ty].rearrange("a h n k -> n a h k"), writes=[b_s])
    k.op(k.dve, lambda: nc.vector.tensor_copy(out=c["FA"][:].rearrange("n a h k -> n (a h k)"), in_=stg[:, 0:1024]), reads=[b_s], writes=[b, b_s])
    c["FB"] = sb(nc, es, "FB", [128, 3, 128], BF16)
    k.dma("sp", stg[:, 0:384].rearrange("n (a k) -> n a k", a=3), T["FB_" + ty].rearrange("a n k -> n a k"), writes=[b_s])
    k.op(k.dve, lambda: nc.vector.tensor_copy(out=c["FB"][:].rearrange("n a k -> n (a k)"), in_=stg[:, 0:384]), reads=[b_s], writes=[b, b_s], disjoint=True)
    c["twA"] = sb(nc, es, "twA", [128, 3, 2, J], F32)
    k.dma("sp", c["twA"][:], T["twA_" + ty][:, :, :, :], writes=[b], disjoint=True)
    c["twB"] = sb(nc, es, "twB", [128, 3, ng], F32)
    k.dma("sp", c["twB"][:], T["twB_" + ty][:, :, :], writes=[b], disjoint=True)
    if need_inv:
        c["IA"] = sb(nc, es, "IA", [128, 2, 2, mo], BF16)
        k.dma("sp", stg[:, 0:4 * mo].rearrange("n (a h m) -> n a h m", a=2, h=2), T["IA_" + ty].rearrange("a h n m -> n a h m"), writes=[b_s])
        k.op(k.dve, lambda: nc.vector.tensor_copy(out=c["IA"][:].rearrange("n a h m -> n (a h m)"), in_=stg[:, 0:4 * mo]), reads=[b_s], writes=[b, b_s], disjoint=True)
        c["Sel"] = sb(nc, es, "Sel", [128, mo], BF16)
        k.dma("sp", stg[:, 0:mo], T["Sel_" + ty][:, :], writes=[b_s])
        k.op(k.dve, lambda: nc.vector.tensor_copy(out=c["Sel"][:], in_=stg[:, 0:mo]), reads=[b_s], writes=[b, b_s], disjoint=True)
    return c


def tw_evac(k, p_re, b_re, p_im, b_im, a, bb, cc, cb, o_re, o_im, ob, tmp):
    nc = k.nc
    t1, t1b = tmp.next()
    P = o_re.shape[0]
    k.op(k.act, lambda: nc.scalar.activation(out=t1[0:P, :], in_=p_re, func=AF.Identity, scale=a), reads=[b_re, cb], writes=[t1b])
    k.op(k.dve, lambda: nc.vector.scalar_tensor_tensor(out=o_re, in0=p_im, scalar=cc, in1=t1[0:P, :], op0=ALU.mult, op1=ALU.add),
         reads=[b_im, t1b, cb], writes=[ob], disjoint=True)
    t2, t2b = tmp.next()
    k.op(k.act, lambda: nc.scalar.activation(out=t2[0:P, :], in_=p_re, func=AF.Identity, scale=bb), reads=[b_re, cb], writes=[t2b])
    k.op(k.dve, lambda: nc.vector.scalar_tensor_tensor(out=o_im, in0=p_im, scalar=a, in1=t2[0:P, :], op0=ALU.mult, op1=ALU.add),
         reads=[b_im, t2b, cb], writes=[ob], disjoint=True)


def stage_a(k, c, J, n2, rhs_list, rhs_bufs, mm, tmp, bout, dstB, dst_buf):
    nc = k.nc
    nh = len(rhs_list)
    for ch in range(2):
        pr, prb = mm.next()
        pi, pib = mm.next()
        for cs, (pt, ptb) in enumerate(((pr, prb), (pi, pib))):
            for h in range(nh):
                k.op(k.pe, lambda: nc.tensor.matmul(pt[:, :], lhsT=c["FA"][:, cs, h, ch * 128:(ch + 1) * 128], rhs=rhs_list[h],
                                                    start=(h == 0), stop=(h == nh - 1)),
                     reads=[c["buf"]] + rhs_bufs, writes=[ptb])
        o_t, o_b = bout.next()
        tw = c["twA"]
        tw_evac(k, pr[:, :], prb, pi[:, :], pib, tw[:, 0, ch, n2:n2 + 1], tw[:, 1, ch, n2:n2 + 1], tw[:, 2, ch, n2:n2 + 1], c["buf"],
                o_t[:, 0, :], o_t[:, 1, :], o_b, tmp)
        r0 = ch * 128 * J + n2
        R = 256 * J
        d0 = dstB[0]
        dst = bass.AP(tensor=d0.tensor, offset=d0.offset + r0 * 512, ap=[[J * 512, 128], [R * 512, 2], [1, 512]])
        k.dma("pool", dst, o_t[:], reads=[o_b], writes=[dst_buf], disjoint=True)


def phase_filter(k, cfg, T, G):
    nc = k.nc
    PI = math.pi
    for ty, L in (("p", cfg.Lp), ("s", cfg.Ls)):
        J = L // 128
        ng = 2 * J
        with contextlib.ExitStack() as es:
            c = fft_consts(k, es, T, ty, J, 128, False)
            w1 = sb(nc, es, "fw1", [FILTER_EMB, FORDER], F32)
            w2 = sb(nc, es, "fw2", [FORDER, FORDER], F32)
            w3 = sb(nc, es, "fw3", [FORDER, FORDER], F32)
            w4 = sb(nc, es, "fw4", [FORDER, 2 * HW], F32)
            bf = sb(nc, es, "fbf", [FORDER, 4], F32)
            frb = sb(nc, es, "frb", [FORDER, 4], F32)
            npi = sb(nc, es, "npi", [FORDER, 1], F32)
            b_w = Buf()
            k.dma("sp", w1[:], T["filt_w1"][:, :], writes=[b_w])
            k.dma("sp", w2[:], T["filt_w2"][:, :], writes=[b_w], disjoint=True)
            k.dma("sp", w3[:], T["filt_w3"][:, :], writes=[b_w], disjoint=True)
            k.dma("sp", w4[:], T["filt_w4"][:, :], writes=[b_w], disjoint=True)
            k.dma("sp", bf[:], T["filt_bf"][:, :], writes=[b_w], disjoint=True)
            k.op(k.dve, lambda: nc.vector.tensor_scalar(out=frb[:, 0:3], in0=bf[:, 0:3], scalar1=bf[:, 3:4], scalar2=None, op0=ALU.mult),
                 reads=[b_w], writes=[b_w])
            k.op(k.dve, lambda: nc.vector.memset(npi[:], 0.0), writes=[b_w], disjoint=True)
            h3 = [sb(nc, es, f"h3{d}", [FORDER, L], BF16) for d in range(2)]
            w4b = sb(nc, es, "fw4b", [FORDER, 2 * HW], BF16)
            k.op(k.dve, lambda: nc.vector.tensor_copy(out=w4b[:], in_=w4[:]), reads=[b_w], writes=[b_w])
            h3b = [Buf(), Buf()]
            zr = Ring([sb(nc, es, f"fz{i}", [FILTER_EMB, 512], F32) for i in range(2)])
            ar = Ring([sb(nc, es, f"fa{i}", [FORDER, 512], F32) for i in range(2)])
            mr = Ring([sb(nc, es, f"fm{i}", [FORDER, 512], F32) for i in range(2)])
            hr = Ring([sb(nc, es, f"fh{i}", [FORDER, 512], F32) for i in range(2)])
            mm = Ring([pm(nc, es, f"fmm{i}", [128, 512], F32) for i in range(8)])
            ws = [w1, w2, w3]
            for d in range(2):
                zsrc = T[("zTf_" if d == 0 else "zTb_") + ty]
                for c0 in range(0, L, 512):
                    z_t, z_b = zr.next()
                    k.dma("sp", z_t[:], zsrc[:, c0:c0 + 512], writes=[z_b])
                    cur, cur_b, kdim = z_t, z_b, FILTER_EMB
                    for layer in range(3):
                        p_t, p_b = mm.next()
                        k.op(k.pe, lambda: nc.tensor.matmul(p_t[0:FORDER, :], lhsT=ws[layer][0:kdim, :], rhs=cur[0:kdim, :], start=True, stop=True),
                             reads=[b_w, cur_b], writes=[p_b])
                        a_t, a_b = ar.next()
                        m_t, m_b = mr.next()
                        k.op(k.dve, lambda: nc.vector.tensor_scalar(out=a_t[:], in0=p_t[0:FORDER, :], scalar1=bf[:, 3:4], scalar2=frb[:, layer:layer + 1],
                                                                    op0=ALU.mult, op1=ALU.add), reads=[p_b, b_w], writes=[a_b])
                        k.op(k.dve, lambda: nc.vector.tensor_scalar(out=m_t[:], in0=a_t[:], scalar1=PI, scalar2=-2 * PI, op0=ALU.is_gt, op1=ALU.mult),
                             reads=[a_b], writes=[m_b])
                        k.op(k.dve, lambda: nc.vector.tensor_tensor(out=a_t[:], in0=a_t[:], in1=m_t[:], op=ALU.add), reads=[a_b, m_b], writes=[a_b])
                        k.op(k.dve, lambda: nc.vector.tensor_scalar(out=m_t[:], in0=a_t[:], scalar1=-PI, scalar2=2 * PI, op0=ALU.is_lt, op1=ALU.mult),
                             reads=[a_b], writes=[m_b])
                        k.op(k.dve, lambda: nc.vector.tensor_tensor(out=a_t[:], in0=a_t[:], in1=m_t[:], op=ALU.add), reads=[a_b, m_b], writes=[a_b])
                        if layer < 2:
                            h_t, h_b = hr.next()
                            k.op(k.act, lambda: nc.scalar.activation(out=h_t[:], in_=a_t[:], func=AF.Sin), reads=[a_b], writes=[h_b])
                            cur, cur_b, kdim = h_t, h_b, FORDER
                        else:
                            k.op(k.act, lambda: nc.scalar.activation(out=h3[d][:, c0:c0 + 512], in_=a_t[:], func=AF.Sin), reads=[a_b],
                                 writes=[h3b[d]], disjoint=True)
            dr = Ring([sb(nc, es, f"fd{i}", [128, 2, 512], F32) for i in range(2)])
            kf = Ring([sb(nc, es, f"fkf{i}", [128, 512], F32) for i in range(2)])
            kc = Ring([sb(nc, es, f"fkc{i}", [128, 2, 512], BF16) for i in range(2)])
            tmp = Ring([sb(nc, es, f"ftm{i}", [128, 512], F32) for i in range(4)])
            bout = Ring([sb(nc, es, f"fbo{i}", [128, 2, 512], BF16) for i in range(3)])
            Bd = [T["Bd_" + ty][0], T["Bd_" + ty][1]]
            Kh = [T["Kh_" + ty][0], T["Kh_" + ty][1]]
            bd_buf = Buf()
            for n2 in range(J):
                d_t, d_b = dr.next()
                k.dma("sp", d_t[:, 0, :], T["decF_" + ty][n2:n2 + 127 * J + 1:J, :], writes=[d_b])
                k.dma("sp", d_t[:, 1, :], T["decB_" + ty][n2:n2 + 127 * J + 1:J, :], writes=[d_b], disjoint=True)
                pf, pfb = mm.next()
                pb, pbb = mm.next()
                k.op(k.pe, lambda: nc.tensor.matmul(pf[:, :], lhsT=h3[0][:, n2:n2 + 127 * J + 1:J], rhs=w4b[:, 0:512], start=True, stop=True),
                     reads=[h3b[0], b_w], writes=[pfb])
                k.op(k.pe, lambda: nc.tensor.matmul(pb[:, :], lhsT=h3[1][:, n2:n2 + 127 * J + 1:J], rhs=w4b[:, 512:1024], start=True, stop=True),
                     reads=[h3b[1], b_w], writes=[pbb])
                kc_t, kc_b = kc.next()
                if n2 == 0:
                    p0, p0b = mm.next()
                    k.op(k.pe, lambda: nc.tensor.matmul(p0[0:1, :], lhsT=h3[0][:, 0:1], rhs=w4b[:, 512:1024], start=True, stop=True),
                         reads=[h3b[0], b_w], writes=[p0b])
                    kf_t, kf_b = kf.next()
                    k.op(k.dve, lambda: nc.vector.tensor_tensor(out=kf_t[:], in0=pf[:, :], in1=d_t[:, 0, :], op=ALU.mult), reads=[pfb, d_b], writes=[kf_b])
                    k.op(k.dve, lambda: nc.vector.tensor_tensor(out=kf_t[0:1, :], in0=kf_t[0:1, :], in1=p0[0:1, :], op=ALU.add),
                         reads=[kf_b, p0b], writes=[kf_b])
                    k.op(k.dve, lambda: nc.vector.tensor_copy(out=kc_t[:, 0, :], in_=kf_t[:]), reads=[kf_b], writes=[kc_b])
                else:
                    k.op(k.dve, lambda: nc.vector.tensor_tensor(out=kc_t[:, 0, :], in0=pf[:, :], in1=d_t[:, 0, :], op=ALU.mult), reads=[pfb, d_b], writes=[kc_b])
                k.op(k.dve, lambda: nc.vector.tensor_tensor(out=kc_t[:, 1, :], in0=pb[:, :], in1=d_t[:, 1, :], op=ALU.mult), reads=[pbb, d_b],
                     writes=[kc_b], disjoint=True)
                stage_a(k, c, J, n2, [kc_t[:, 0, :], kc_t[:, 1, :]], [kc_b], mm, tmp, bout, Bd, bd_buf)
            br = Ring([sb(nc, es, f"fbr{i}", [128, 2, 512], BF16) for i in range(2)])
            kh_buf = G["kh_buf"][ty]
            for g in range(ng):
                b_t, b_b = br.next()
                for e in range(2):
                    k.dma("sp", b_t[:, e, :], Bd[e][g * 128:(g + 1) * 128, :], reads=[bd_buf], writes=[b_b], disjoint=True)
                xr, xrb = mm.next()
                xi, xib = mm.next()
                FB = c["FB"]
                k.op(k.pe, lambda: nc.tensor.matmul(xr[:, :], lhsT=FB[:, 0, :], rhs=b_t[:, 0, :], start=True, stop=False), reads=[b_b, c["buf"]], writes=[xrb])
                k.op(k.pe, lambda: nc.tensor.matmul(xr[:, :], lhsT=FB[:, 2, :], rhs=b_t[:, 1, :], start=False, stop=True), reads=[b_b, c["buf"]], writes=[xrb])
                k.op(k.pe, lambda: nc.tensor.matmul(xi[:, :], lhsT=FB[:, 1, :], rhs=b_t[:, 0, :], start=True, stop=False), reads=[b_b, c["buf"]], writes=[xib])
                k.op(k.pe, lambda: nc.tensor.matmul(xi[:, :], lhsT=FB[:, 0, :], rhs=b_t[:, 1, :], start=False, stop=True), reads=[b_b, c["buf"]], writes=[xib])
                o_t, o_b = bout.next()
                k.op(k.act, lambda: nc.scalar.copy(out=o_t[:, 0, :], in_=xr[:, :]), reads=[xrb], writes=[o_b])
                k.op(k.dve, lambda: nc.vector.tensor_copy(out=o_t[:, 1, :], in_=xi[:, :]), reads=[xib], writes=[o_b], disjoint=True)
                for e in range(2):
                    k.dma("pool", Kh[e][g * 128:(g + 1) * 128, :], o_t[:, e, :], reads=[o_b], writes=[kh_buf], disjoint=True)
        k.barrier()


def phase_hyena(k, cfg, T, G):
    nc = k.nc
    for ty, seqs in (("p", [0]), ("s", list(range(1, cfg.nseq)))):
        L = cfg.Lp if ty == "p" else cfg.Ls
        J = L // 128
        ng = 2 * J
        mo = (cfg.own_p // J) if ty == "p" else 128
        JC = min(J, 4)
        with contextlib.ExitStack() as es:
            c = fft_consts(k, es, T, ty, J, mo, True)
            cw = sb(nc, es, "hcw", [128, 3, 1536], F32)
            cb = sb(nc, es, "hcb", [128, 1536], F32)
            hb = sb(nc, es, "hhb", [128, 512], F32)
            b_w = Buf()
            k.dma("sp", cw[:].rearrange("p a c -> p (a c)"), bc_ap(T["conv_w"]), writes=[b_w])
            k.dma("sp", cb[:], bc_ap(T["conv_b"]), writes=[b_w], disjoint=True)
            k.dma("sp", hb[:], bc_ap(T["hyena_bias"]), writes=[b_w], disjoint=True)
            ur = Ring([sb(nc, es, f"hu{i}", [128, JC + 2, 1536], BF16) for i in range(2)])
            ta = Ring([sb(nc, es, f"hta{i}", [128, 1024], F32) for i in range(2)])
            tb = Ring([sb(nc, es, f"htb{i}", [128, 1024], F32) for i in range(2)])
            tc = Ring([sb(nc, es, f"htc{i}", [128, 1024], F32) for i in range(2)])
            s32 = Ring([sb(nc, es, f"hs32{i}", [128, 512], F32) for i in range(2)])
            sg = Ring([sb(nc, es, f"hsg{i}", [128, 3, 512], BF16) for i in range(3)])
            xso = Ring([sb(nc, es, f"hxo{i}", [128, 2, 512], BF16) for i in range(3)])
            tmp = Ring([sb(nc, es, f"htm{i}", [128, 512], F32) for i in range(4)])
            bout = Ring([sb(nc, es, f"hbo{i}", [128, 2, 512], BF16) for i in range(3)])
            br = Ring([sb(nc, es, f"hbr{i}", [128, 4, 512], BF16) for i in range(2)])
            pw = Ring([sb(nc, es, f"hpw{i}", [128, 512], F32) for i in range(4)])
            yy = Ring([sb(nc, es, f"hyy{i}", [128, 2, 512], BF16) for i in range(3)])
            dr_ = Ring([sb(nc, es, f"hdr{i}", [128, 4, 512], BF16) for i in range(2)])
            xl = Ring([sb(nc, es, f"hxl{i}", [128, 2, 512], BF16) for i in range(2)])
            yo = Ring([sb(nc, es, f"hyo{i}", [128, 512], F32) for i in range(2)])
            mm = Ring([pm(nc, es, f"hmm{i}", [128, 512], F32) for i in range(8)])
            Bd = [T["Bd_" + ty][0], T["Bd_" + ty][1]]
            Cd = [T["Cd_" + ty][0], T["Cd_" + ty][1]]
            Kh = [T["Kh_" + ty][0], T["Kh_" + ty][1]]
            XS = [T["XS"][0], T["XS"][1]]
            bd_buf, cd_buf, xs_buf = Buf(), Buf(), Buf()
            kh_buf = G["kh_buf"][ty]
            U = T["U"]
            for s in seqs:
                ub = int(cfg.foff[s]) + 2 * s
                o0 = int(cfg.ooff[s])
                def conv_part(u_t, u_b, jj):
                    a_t, a_b = ta.next()
                    b_t, b_b = tb.next()
                    c_t, c_b = tc.next()
                    k.op(k.dve, lambda: nc.vector.tensor_tensor(out=a_t[:], in0=u_t[:, jj, 512:1536], in1=cw[:, 0, 512:1536], op=ALU.mult), reads=[u_b, b_w], writes=[a_b])
                    k.op(k.dve, lambda: nc.vector.tensor_tensor(out=b_t[:], in0=u_t[:, jj + 1, 512:1536], in1=cw[:, 1, 512:1536], op=ALU.mult), reads=[u_b, b_w], writes=[b_b])
                    k.op(k.dve, lambda: nc.vector.tensor_tensor(out=a_t[:], in0=a_t[:], in1=b_t[:], op=ALU.add), reads=[a_b, b_b], writes=[a_b])
                    k.op(k.dve, lambda: nc.vector.tensor_tensor(out=b_t[:], in0=u_t[:, jj + 2, 512:1536], in1=cw[:, 2, 512:1536], op=ALU.mult), reads=[u_b, b_w, a_b], writes=[b_b])
                    k.op(k.dve, lambda: nc.vector.tensor_tensor(out=a_t[:], in0=a_t[:], in1=b_t[:], op=ALU.add), reads=[a_b, b_b], writes=[a_b])
                    k.op(k.dve, lambda: nc.vector.tensor_tensor(out=a_t[:], in0=a_t[:], in1=cb[:, 512:1536], op=ALU.add), reads=[a_b, b_w], writes=[a_b])
                    x0c, tq = c_t[:, 0:512], c_t[:, 512:1024]
                    k.op(k.pool, lambda: nc.gpsimd.tensor_tensor(out=x0c, in0=u_t[:, jj, 0:512], in1=cw[:, 0, 0:512], op=ALU.mult), reads=[u_b, b_w], writes=[c_b])
                    k.op(k.pool, lambda: nc.gpsimd.tensor_tensor(out=tq, in0=u_t[:, jj + 1, 0:512], in1=cw[:, 1, 0:512], op=ALU.mult), reads=[u_b, b_w, c_b], writes=[c_b])
                    k.op(k.pool, lambda: nc.gpsimd.tensor_tensor(out=x0c, in0=x0c, in1=tq, op=ALU.add), reads=[c_b], writes=[c_b])
                    k.op(k.pool, lambda: nc.gpsimd.tensor_tensor(out=tq, in0=u_t[:, jj + 2, 0:512], in1=cw[:, 2, 0:512], op=ALU.mult), reads=[u_b, b_w, c_b], writes=[c_b])
                    k.op(k.pool, lambda: nc.gpsimd.tensor_tensor(out=x0c, in0=x0c, in1=tq, op=ALU.add), reads=[c_b], writes=[c_b])
                    k.op(k.pool, lambda: nc.gpsimd.tensor_tensor(out=x0c, in0=x0c, in1=cb[:, 0:512], op=ALU.add), reads=[c_b, b_w], writes=[c_b])
                    s_t, s_b = s32.next()
                    g_t, g_b = sg.next()
                    k.op(k.dve, lambda: nc.vector.tensor_tensor(out=s_t[:], in0=a_t[:, 0:512], in1=a_t[:, 512:1024], op=ALU.mult), reads=[a_b], writes=[s_b])
                    k.op(k.act, lambda: nc.scalar.copy(out=g_t[:, 0, :], in_=s_t[:]), reads=[s_b], writes=[g_b])
                    k.op(k.act, lambda: nc.scalar.copy(out=g_t[:, 1, :], in_=x0c), reads=[c_b], writes=[g_b], disjoint=True)
                    k.op(k.pool, lambda: nc.gpsimd.tensor_tensor(out=tq, in0=s_t[:], in1=hb[:], op=ALU.mult), reads=[s_b, b_w, c_b], writes=[c_b])
                    k.op(k.pool, lambda: nc.gpsimd.tensor_tensor(out=g_t[:, 2, :], in0=tq, in1=x0c, op=ALU.mult), reads=[c_b], writes=[g_b], disjoint=True)
                    return g_t, g_b

                def post_part(n2, g_t, g_b):
                    x_t, x_b = xso.next()
                    for e in range(2):
                        p_t, p_b = mm.next()
                        k.op(k.pe, lambda: nc.tensor.matmul(p_t[0:mo, :], lhsT=c["Sel"][:, :], rhs=g_t[:, 1 + e, :], start=True, stop=True),
                             reads=[g_b, c["buf"]], writes=[p_b])
                        k.op(k.act, lambda: nc.scalar.copy(out=x_t[0:mo, e, :], in_=p_t[0:mo, :]), reads=[p_b], writes=[x_b], disjoint=True)
                    x0_ = XS[0]
                    xdst = bass.AP(tensor=x0_.tensor, offset=x0_.offset + (o0 + n2) * 512, ap=[[J * 512, mo], [cfg.TO * 512, 2], [1, 512]])
                    k.dma("pool", xdst, x_t[0:mo, :, :], reads=[x_b], writes=[xs_buf], disjoint=True)
                    stage_a(k, c, J, n2, [g_t[:, 0, :]], [g_b], mm, tmp, bout, Bd, bd_buf)

                prev = None
                for j0 in range(0, J, JC):
                    u_t, u_b = ur.next()
                    src = bass.AP(tensor=U.tensor, offset=U.offset + (ub + j0) * 1536, ap=[[J * 1536, 128], [1536, JC + 2], [1, 1536]])
                    k.dma("sp", u_t[:], src, writes=[u_b])
                    for jj in range(JC):
                        cur = (j0 + jj,) + conv_part(u_t, u_b, jj)
                        if prev is not None:
                            post_part(*prev)
                        prev = cur
                post_part(*prev)
                FB = c["FB"]

                def front(g):
                        b_t, b_b = br.next()
                        for e in range(2):
                            k.dma("sp", b_t[:, e, :], Bd[e][g * 128:(g + 1) * 128, :], reads=[bd_buf], writes=[b_b], disjoint=True)
                            k.dma("sp", b_t[:, 2 + e, :], Kh[e][g * 128:(g + 1) * 128, :], reads=[kh_buf], writes=[b_b], disjoint=True)
                        xr, xrb = mm.next()
                        xi, xib = mm.next()
                        FB = c["FB"]
                        k.op(k.pe, lambda: nc.tensor.matmul(xr[:, :], lhsT=FB[:, 0, :], rhs=b_t[:, 0, :], start=True, stop=False), reads=[b_b, c["buf"]], writes=[xrb])
                        k.op(k.pe, lambda: nc.tensor.matmul(xr[:, :], lhsT=FB[:, 2, :], rhs=b_t[:, 1, :], start=False, stop=True), reads=[b_b, c["buf"]], writes=[xrb])
                        k.op(k.pe, lambda: nc.tensor.matmul(xi[:, :], lhsT=FB[:, 1, :], rhs=b_t[:, 0, :], start=True, stop=False), reads=[b_b, c["buf"]], writes=[xib])
                        k.op(k.pe, lambda: nc.tensor.matmul(xi[:, :], lhsT=FB[:, 0, :], rhs=b_t[:, 1, :], start=False, stop=True), reads=[b_b, c["buf"]], writes=[xib])
                        m1, m1b = pw.next()
                        m2, m2b = pw.next()
                        m3, m3b = pw.next()
                        m4, m4b = pw.next()
                        y_t, y_b = yy.next()
                        k.op(k.dve, lambda: nc.vector.tensor_tensor(out=m1[:], in0=xr[:, :], in1=b_t[:, 2, :], op=ALU.mult), reads=[xrb, b_b], writes=[m1b])
                        k.op(k.dve, lambda: nc.vector.tensor_tensor(out=m2[:], in0=xi[:, :], in1=b_t[:, 3, :], op=ALU.mult), reads=[xib, b_b], writes=[m2b])
                        k.op(k.pool, lambda: nc.gpsimd.tensor_tensor(out=y_t[:, 0, :], in0=m1[:], in1=m2[:], op=ALU.subtract), reads=[m1b, m2b], writes=[y_b])
                        k.op(k.dve, lambda: nc.vector.tensor_tensor(out=m3[:], in0=xr[:, :], in1=b_t[:, 3, :], op=ALU.mult), reads=[xrb, b_b], writes=[m3b])
                        k.op(k.dve, lambda: nc.vector.tensor_tensor(out=m4[:], in0=xi[:, :], in1=b_t[:, 2, :], op=ALU.mult), reads=[xib, b_b], writes=[m4b])
                        k.op(k.pool, lambda: nc.gpsimd.tensor_tensor(out=y_t[:, 1, :], in0=m3[:], in1=m4[:], op=ALU.add), reads=[m3b, m4b], writes=[y_b], disjoint=True)
                        return y_t, y_b

                def back(g, y_t, y_b):
                        cr, crb = mm.next()
                        ci, cib = mm.next()
                        k.op(k.pe, lambda: nc.tensor.matmul(cr[:, :], lhsT=FB[:, 0, :], rhs=y_t[:, 0, :], start=True, stop=False), reads=[y_b, c["buf"]], writes=[crb])
                        k.op(k.pe, lambda: nc.tensor.matmul(cr[:, :], lhsT=FB[:, 1, :], rhs=y_t[:, 1, :], start=False, stop=True), reads=[y_b, c["buf"]], writes=[crb])
                        k.op(k.pe, lambda: nc.tensor.matmul(ci[:, :], lhsT=FB[:, 0, :], rhs=y_t[:, 1, :], start=True, stop=False), reads=[y_b, c["buf"]], writes=[cib])
                        k.op(k.pe, lambda: nc.tensor.matmul(ci[:, :], lhsT=FB[:, 2, :], rhs=y_t[:, 0, :], start=False, stop=True), reads=[y_b, c["buf"]], writes=[cib])
                        o_t, o_b = bout.next()
                        tw = c["twB"]
                        tw_evac(k, cr[:, :], crb, ci[:, :], cib, tw[:, 0, g:g + 1], tw[:, 2, g:g + 1], tw[:, 1, g:g + 1], c["buf"],
                                o_t[:, 0, :], o_t[:, 1, :], o_b, tmp)
                        c0_ = Cd[0]
                        cdst = bass.AP(tensor=c0_.tensor, offset=c0_.offset + g * 128 * 512, ap=[[512, 128], [256 * J * 512, 2], [1, 512]])
                        k.dma("pool", cdst, o_t[:], reads=[o_b], writes=[cd_buf], disjoint=True)

                prevg = None
                for g in range(ng):
                    curg = (g,) + front(g)
                    if prevg is not None:
                        back(*prevg)
                    prevg = curg
                back(*prevg)
                for n2 in range(J):
                    d_t, d_b = dr_.next()
                    for ch in range(2):
                        r0 = ch * 128 * J + n2
                        for e in range(2):
                            k.dma("sp", d_t[:, ch * 2 + e, :], Cd[e][r0:r0 + 127 * J + 1:J, :], reads=[cd_buf], writes=[d_b], disjoint=True)
                    x_t, x_b = xl.next()
                    for e in range(2):
                        k.dma("sp", x_t[0:mo, e, :], XS[e][o0 + n2:o0 + n2 + (mo - 1) * J + 1:J, :], reads=[xs_buf], writes=[x_b], disjoint=True)
                    p_t, p_b = mm.next()
                    i = 0
                    for ch in range(2):
                        for e in range(2):
                            k.op(k.pe, lambda: nc.tensor.matmul(p_t[0:mo, :], lhsT=c["IA"][:, e, ch, :], rhs=d_t[:, ch * 2 + e, :], start=(i == 0), stop=(i == 3)),
                                 reads=[d_b, c["buf"]], writes=[p_b])
                            i += 1
                    q_t, q_b = tmp.next()
                    y_t, y_b = yo.next()
                    k.op(k.dve, lambda: nc.vector.tensor_tensor(out=q_t[0:mo, :], in0=p_t[0:mo, :], in1=x_t[0:mo, 0, :], op=ALU.mult), reads=[p_b, x_b], writes=[q_b])
                    k.op(k.pool, lambda: nc.gpsimd.tensor_tensor(out=y_t[0:mo, :], in0=q_t[0:mo, :], in1=x_t[0:mo, 1, :], op=ALU.add), reads=[q_b, x_b], writes=[y_b])
                    k.dma("pool", T["Yh"][o0 + n2:o0 + n2 + (mo - 1) * J + 1:J, :], y_t[0:mo, :], reads=[y_b])
        k.barrier()
```

```python
import contextlib
import math
import numpy as np
import ml_dtypes
import concourse.bass as bass
import concourse.mybir as mybir
from concourse.bass_utils import run_bass_kernel_spmd

F32 = mybir.dt.float32
BF16 = mybir.dt.bfloat16
AF = mybir.ActivationFunctionType
ALU = mybir.AluOpType

D = 1024
NCH = 8
HW = 512
DFF = 4096
NORM_EPS = 1e-6
SUBLN_EPS = 1e-5
ROT_DIM = 16
ROPE_THETA = 500000.0
FILTER_EMB = 33
FORDER = 64


class Buf:
    __slots__ = ("w", "r", "pr")

    def __init__(self):
        self.w = {}
        self.r = {}
        self.pr = {}


class Eng:
    def __init__(self, name, eng, sem):
        self.name, self.eng, self.sem = name, eng, sem
        self.count = 0
        self.waited = {}

    def wait(self, sem, val):
        k = id(sem)
        if self.waited.get(k, 0) >= val:
            return
        self.eng.wait_ge(sem, val)
        self.waited[k] = val


class K:
    def __init__(self, nc, es):
        self.nc = nc
        self.es = es
        self.sems = {}
        mk = lambda n: es.enter_context(nc.semaphore(n))
        self.pe = Eng("pe", nc.tensor, mk("s_pe"))
        self.act = Eng("act", nc.scalar, mk("s_act"))
        self.dve = Eng("dve", nc.vector, mk("s_dve"))
        self.pool = Eng("pool", nc.gpsimd, mk("s_pool"))
        self.sp = Eng("sp", nc.sync, mk("s_sp"))
        self.engs = [self.pe, self.act, self.dve, self.pool, self.sp]
        self.nq = 8
        self.dq = {}
        for q, e in (("sp", self.sp), ("pool", self.pool)):
            self.dq[q] = dict(eng=e, sems=[mk(f"d_{q}{i}") for i in range(self.nq)], idx=0)
        self.all_dma_events = {}

    def _deps(self, E, reads, writes, disjoint):
        evs = {}

        def add(d):
            for k, (s, v) in d.items():
                if k not in evs or evs[k][1] < v:
                    evs[k] = (s, v)
        for b in reads:
            add(b.w)
        for b in writes:
            add(b.r)
            add(b.pr)
            if not disjoint:
                add(b.w)
        for k, (s, v) in evs.items():
            if E is self.pe and s is self.pe.sem:
                continue
            E.wait(s, v)

    def _commit(self, sem, val, reads, writes):
        k = id(sem)
        for b in reads:
            b.r[k] = (sem, val)
        for b in writes:
            if b.r:
                b.pr = b.r
                b.r = {}
                b.w = {}
            b.w[k] = (sem, val)

    def op(self, E, fn, reads=(), writes=(), disjoint=False):
        self._deps(E, reads, writes, disjoint)
        ins = fn()
        E.count += 1
        ins.then_inc(E.sem, 1)
        self._commit(E.sem, E.count, reads, writes)
        return ins

    def dma(self, q, out, in_, reads=(), writes=(), disjoint=False, **kw):
        Q = self.dq[q]
        E = Q["eng"]
        slot = Q["idx"] % self.nq
        gen = Q["idx"] // self.nq
        Q["idx"] += 1
        sem = Q["sems"][slot]
        E.wait(sem, 16 * gen)
        self._deps(E, reads, writes, disjoint)
        E.eng.dma_start(out=out, in_=in_, **kw).then_inc(sem, 16)
        self._commit(sem, 16 * (gen + 1), reads, writes)
        self.all_dma_events[id(sem)] = (sem, 16 * (gen + 1))

    def barrier(self):
        for E in self.engs:
            for X in self.engs:
                if X is not E and X.count > 0:
                    E.wait(X.sem, X.count)
            for (s, v) in self.all_dma_events.values():
                E.wait(s, v)

    def final_wait(self):
        for (s, v) in self.all_dma_events.values():
            self.sp.wait(s, v)
        for X in self.engs:
            if X is not self.sp and X.count > 0:
                self.sp.wait(X.sem, X.count)


def bcast_rows(ap_row, nparts=128):
    return ap_row.partition_broadcast(nparts)


class Cfg:
    def __init__(self, Lp=16384, Ls=2048, NS=4, NQ=4, debug=False):
        self.Lp, self.Ls, self.NS, self.NQ = Lp, Ls, NS, NQ
        self.own_p = Lp // NQ
        self.nseq = 1 + NS
        self.L = [Lp] + [Ls] * NS
        self.own = [self.own_p] + [Ls] * NS
        self.debug = debug
        self.foff = np.concatenate([[0], np.cumsum(self.L)]).astype(int)
        self.ooff = np.concatenate([[0], np.cumsum(self.own)]).astype(int)
        self.TF = int(self.foff[-1])
        self.TO = int(self.ooff[-1])


def rope_tables(positions):
    inv_freq = (ROPE_THETA ** (-np.arange(0, ROT_DIM, 2, dtype=np.float32) / ROT_DIM)).astype(np.float32)
    ang = positions.astype(np.float32)[:, None] * inv_freq[None, :]
    cos, sin = np.cos(ang).astype(np.float32), np.sin(ang).astype(np.float32)
    n = positions.shape[0]
    C = np.ones((64, n), np.float32)
    S = np.zeros((64, n), np.float32)
    C[0:8] = cos.T
    C[8:16] = cos.T
    S[0:8] = -sin.T
    S[8:16] = sin.T
    return np.concatenate([C, C], 0), np.concatenate([S, S], 0)


def rot_perm():
    p = np.arange(64)
    p[0:8] = np.arange(8, 16)
    p[8:16] = np.arange(0, 8)
    return p


def filter_feats(L):
    f32 = np.float32
    t = np.linspace(0.0, 1.0, L, dtype=f32)[:, None]
    bands = (FILTER_EMB - 1) // 2
    w = (f32(2.0 * math.pi) * np.arange(L, dtype=f32)[:, None] / f32(L)).astype(f32)
    f = np.linspace(1e-4, bands - 1, bands, dtype=f32)[None, :]
    fw = (f * w).astype(f32)
    z = np.concatenate([t, np.cos(fw), -np.sin(fw)], axis=-1).astype(f32)
    min_decay = math.log(1e-2) / 1.5
    max_decay = math.log(1e-2) / 0.3
    deltas = np.linspace(min_decay, max_decay, HW, dtype=f32)
    decay = np.exp(-t * np.abs(deltas)[None, :]).astype(f32)
    return z, decay


class Ring:
    def __init__(self, tiles):
        self.tiles = tiles
        self.bufs = [Buf() for _ in tiles]
        self.i = 0

    def next(self):
        t, b = self.tiles[self.i % len(self.tiles)], self.bufs[self.i % len(self.tiles)]
        self.i += 1
        return t, b


_UID = [0]


def sb(nc, es, name, shape, dt):
    _UID[0] += 1
    return es.enter_context(nc.sbuf_tensor(f"sb{_UID[0]}_{name}", list(shape), dt))


def pm(nc, es, name, shape, dt):
    _UID[0] += 1
    return es.enter_context(nc.psum_tensor(f"ps{_UID[0]}_{name}", list(shape), dt))


WSPEC = [("w_in", D, 3072), ("w_perm", D, 1024), ("w_out", D, D), ("w_mlp1", D, DFF), ("w_mlp2", DFF, D)]


def declare(nc, cfg):
    T = {}
    dbg = cfg.debug

    def inp(name, shape, dt=F32):
        T[name] = nc.dram_tensor(name, list(shape), dt, kind="ExternalInput").ap()

    def scr(name, shape, dt=BF16, out=False):
        kind = "ExternalOutput" if (out or (dbg and name in cfg.debug)) else "Internal"
        T[name] = nc.dram_tensor(name, list(shape), dt, kind=kind).ap()

    ns = cfg.nseq
    inp("xf", [cfg.TF, D]); inp("xo", [cfg.TO, D])
    inp("cT", [128, NCH, ns]); inp("w_ada", [D, 6 * D]); inp("b_ada", [1, 6 * D]); inp("b_adaT", [128, 48])
    inp("nw1T", [128, NCH]); inp("nw2T", [128, NCH]); inp("hnwT", [128, 4])
    for n, kd, nn in WSPEC:
        inp(n, [kd, nn])
        scr(n + "_b", [128, kd // 128, nn])
    inp("ident", [128, 128])
    inp("ropeF_c", [128, cfg.TF]); inp("ropeF_s", [128, cfg.TF])
    inp("ropeO_c", [128, cfg.TO]); inp("ropeO_s", [128, cfg.TO])
    inp("conv_w", [3, 1536]); inp("conv_b", [1, 1536])
    inp("hyena_bias", [1, HW]); inp("final_w", [1, D]); inp("subln_w", [1, 128])
    inp("lam4", [4, 64])
    scr("Gd", [ns, 2, D], F32)
    scr("U", [cfg.TF + 2 * ns, 1536])
    scr("Vd", [cfg.TF, 512])
    scr("KT", [512, cfg.TF])
    scr("QT", [512, cfg.TO])
    if dbg and "Yh_in" in cfg.debug:
        inp("Yh", [cfg.TO, 512])
    else:
        scr("Yh", [cfg.TO, 512], F32)
    scr("Oa", [cfg.TO, 512])
    scr("X1", [cfg.TO, D], F32)
    declare_hyena(nc, cfg, T, inp, scr)
    T["y"] = nc.dram_tensor("y", [cfg.TO, D], F32, kind="ExternalOutput").ap()
    return T


def phase_weights(k, cfg, T):
    nc = k.nc
    with contextlib.ExitStack() as es:
        st = Ring([sb(nc, es, f"wst{i}", [128, 8, 512], F32) for i in range(2)])
        cb = Ring([sb(nc, es, f"wcb{i}", [128, 8, 512], BF16) for i in range(2)])
        i = 0
        for name, kd, nn in WSPEC:
            src = T[name].rearrange("(j p) c -> p j c", p=128)
            dst = T[name + "_b"]
            for j0 in range(0, kd // 128, 8):
                for c0 in range(0, nn, 512):
                    s_t, s_b = st.next()
                    c_t, c_b = cb.next()
                    k.dma("sp", s_t[:], src[:, j0:j0 + 8, c0:c0 + 512], writes=[s_b])
                    E = k.dve if i % 2 == 0 else k.act
                    if E is k.dve:
                        k.op(E, lambda: nc.vector.tensor_copy(out=c_t[:], in_=s_t[:]), reads=[s_b], writes=[c_b])
                    else:
                        k.op(E, lambda: nc.scalar.copy(out=c_t[:], in_=s_t[:]), reads=[s_b], writes=[c_b])
                    k.dma("pool", dst[:, j0:j0 + 8, c0:c0 + 512], c_t[:], reads=[c_b])
                    i += 1
    k.barrier()


def phase_mod(k, cfg, T, G):
    nc = k.nc
    ns = cfg.nseq
    modT, a1, a2 = G["modT"], G["a1"], G["a2"]
    gb = G["gbuf"]
    with contextlib.ExitStack() as es:
        cT = sb(nc, es, "cT", [128, NCH, ns], F32)
        scT = sb(nc, es, "scT", [128, NCH, ns], F32)
        bT = sb(nc, es, "bT", [128, 48], F32)
        n1 = sb(nc, es, "n1", [128, NCH], F32)
        n2 = sb(nc, es, "n2", [128, NCH], F32)
        brow = sb(nc, es, "brow", [1, 6 * D], F32)
        grow = Ring([sb(nc, es, f"grow{i}", [1, 512], F32) for i in range(2)])
        wr = Ring([sb(nc, es, f"wada{i}", [128, 8, 512], F32) for i in range(2)])
        psm = pm(nc, es, "psm", [128, 48, ns], F32)
        psg = Ring([pm(nc, es, f"psg{i}", [1, 512], F32) for i in range(2)])
        b_c, b_s, b_b, b_n, b_br, b_psm = Buf(), Buf(), Buf(), Buf(), Buf(), Buf()
        k.dma("sp", cT[:], T["cT"][:, :, :], writes=[b_c])
        k.dma("sp", bT[:], T["b_adaT"][:, :], writes=[b_b])
        k.dma("sp", n1[:], T["nw1T"][:, :], writes=[b_n])
        k.dma("sp", n2[:], T["nw2T"][:, :], writes=[b_n], disjoint=True)
        k.dma("sp", brow[:], T["b_ada"][:, :], writes=[b_br])
        k.op(k.act, lambda: nc.scalar.activation(out=scT[:], in_=cT[:], func=AF.Silu), reads=[b_c], writes=[b_s])
        wsrc = T["w_ada"].rearrange("(j p) c -> p j c", p=128)
        for pc in range(12):
            w_t, w_b = wr.next()
            k.dma("sp", w_t[:], wsrc[:, :, pc * 512:(pc + 1) * 512], writes=[w_b])
            for mm in range(4):
                m = pc * 4 + mm
                for j in range(NCH):
                    k.op(k.pe, lambda: nc.tensor.matmul(psm[:, m, :], lhsT=w_t[:, j, mm * 128:(mm + 1) * 128],
                                                        rhs=scT[:, j, :], start=(j == 0), stop=(j == NCH - 1)),
                         reads=[w_b, b_s], writes=[b_psm], disjoint=True)
            which = {4: (0, 0), 5: (0, 1), 10: (1, 0), 11: (1, 1)}.get(pc)
            if which is not None:
                gi, half = which
                for s in range(ns):
                    p_t, p_b = psg.next()
                    g_t, g_b = grow.next()
                    for j in range(NCH):
                        k.op(k.pe, lambda: nc.tensor.matmul(p_t[:, :], lhsT=scT[:, j, s:s + 1], rhs=w_t[:, j, :],
                                                            start=(j == 0), stop=(j == NCH - 1)),
                             reads=[w_b, b_s], writes=[p_b])
                    k.op(k.dve, lambda: nc.vector.tensor_tensor(out=g_t[:], in0=p_t[:, :],
                                                                in1=brow[:, pc * 512:(pc + 1) * 512], op=ALU.add),
                         reads=[p_b, b_br], writes=[g_b])
                    k.dma("pool", T["Gd"][s, gi:gi + 1, half * 512:(half + 1) * 512], g_t[:], reads=[g_b])
        for s in range(ns):
            k.op(k.dve, lambda: nc.vector.tensor_tensor(out=modT[:, :, s], in0=psm[:, :, s], in1=bT[:, :], op=ALU.add),
                 reads=[b_psm, b_b], writes=[gb], disjoint=True)
        for s in range(ns):
            k.op(k.dve, lambda: nc.vector.scalar_tensor_tensor(out=a1[:, :, s], in0=modT[:, 8:16, s], scalar=1.0,
                                                               in1=n1[:, :], op0=ALU.add, op1=ALU.mult),
                 reads=[gb, b_n], writes=[gb], disjoint=True)
            k.op(k.dve, lambda: nc.vector.scalar_tensor_tensor(out=a2[:, :, s], in0=modT[:, 32:40, s], scalar=1.0,
                                                               in1=n2[:, :], op0=ALU.add, op1=ALU.mult),
                 reads=[gb, b_n], writes=[gb], disjoint=True)
    k.barrier()


def rms_scale_rows(k, x_t, x_b, ss_t, rs_t, sc_b, junk_t, junk_b, nblk, width, eps, G):
    nc = k.nc
    for tb in range(nblk):
        k.op(k.dve, lambda: nc.vector.scalar_tensor_tensor(out=junk_t[:, 0:width], in0=x_t[:, tb, :], scalar=1.0,
                                                           in1=x_t[:, tb, :], op0=ALU.mult, op1=ALU.mult,
                                                           accum_out=ss_t[:, tb:tb + 1]),
             reads=[x_b], writes=[junk_b, sc_b])
    k.op(k.pool, lambda: nc.gpsimd.tensor_scalar(out=ss_t[:, 0:nblk], in0=ss_t[:, 0:nblk], scalar1=1.0 / width,
                                                 scalar2=eps, op0=ALU.mult, op1=ALU.add),
         reads=[sc_b], writes=[sc_b])
    k.op(k.pool, lambda: nc.gpsimd.tensor_tensor(out=rs_t[:, 0:nblk], in0=ss_t[:, 0:nblk], in1=G["mhalf"][:, 0:nblk],
                                                 op=ALU.pow),
         reads=[sc_b], writes=[sc_b])


def phase_proj(k, cfg, T, G, which):
    nc = k.nc
    full = which == "F"
    xsrc = T["xf"] if full else T["xo"]
    lens = cfg.L if full else cfg.own
    offs = cfg.foff if full else cfg.ooff
    rc, rs_ = (T["ropeF_c"], T["ropeF_s"]) if full else (T["ropeO_c"], T["ropeO_s"])
    ntm = 2048 if full else 0
    with contextlib.ExitStack() as es:
        ident = G["ident"]
        if full:
            wtm = sb(nc, es, "wtm", [128, NCH, 2048], BF16)
        wfm = sb(nc, es, "wfm", [128, NCH, 512], BF16)
        wfp = sb(nc, es, "wfp", [128, NCH, 512], BF16)
        b_w = Buf()
        wb = T["w_in_b"]
        if full:
            k.dma("sp", wtm[:, :, 0:1536], wb[:, :, 0:1536], writes=[b_w])
            k.dma("sp", wtm[:, :, 1536:2048], wb[:, :, 2560:3072], writes=[b_w], disjoint=True)
            k.dma("sp", wfm[:], wb[:, :, 2048:2560], writes=[b_w], disjoint=True)
            k.dma("sp", wfp[:], T["w_perm_b"][:, :, 512:1024], writes=[b_w], disjoint=True)
        else:
            k.dma("sp", wfm[:], wb[:, :, 1536:2048], writes=[b_w])
            k.dma("sp", wfp[:], T["w_perm_b"][:, :, 0:512], writes=[b_w], disjoint=True)
        xr = Ring([sb(nc, es, f"px{i}", [128, 4, D], F32) for i in range(2)])
        xs = Ring([sb(nc, es, f"pxs{i}", [128, 4, D], BF16) for i in range(2)])
        hT = Ring([sb(nc, es, f"phT{i}", [128, NCH, 512], BF16) for i in range(2)])
        ss = Ring([sb(nc, es, f"pss{i}", [128, 8], F32) for i in range(2)])
        rsd = Ring([sb(nc, es, f"prs{i}", [128, 8], F32) for i in range(2)])
        junk = Ring([sb(nc, es, f"pjk{i}", [128, D], F32) for i in range(1)])
        ct = Ring([sb(nc, es, f"pct{i}", [128, 512], F32) for i in range(2)])
        st_ = Ring([sb(nc, es, f"pst{i}", [128, 512], F32) for i in range(2)])
        t1 = Ring([sb(nc, es, f"pt1{i}", [128, 512], F32) for i in range(2)])
        t2 = Ring([sb(nc, es, f"pt2{i}", [128, 512], F32) for i in range(2)])
        fo = Ring([sb(nc, es, f"pfo{i}", [128, 512], BF16) for i in range(3)])
        if full:
            so = Ring([sb(nc, es, f"pso{i}", [128, 4, 2048], BF16) for i in range(2)])
            zt = sb(nc, es, "pzero", [1, 1536], BF16)
            b_z = Buf()
            k.op(k.dve, lambda: nc.vector.memset(zt[:], 0.0), writes=[b_z])
        tp = Ring([pm(nc, es, f"ptp{i}", [128, 2, 512], BF16) for i in range(2)])
        mm = Ring([pm(nc, es, f"pmm{i}", [128, 512], F32) for i in range(6)])
        dst_fm = T["KT"] if full else T["QT"]

        tiles = [(s, t0) for s in range(cfg.nseq) for t0 in range(0, lens[s], 512)]

        def load_x(idx):
            s, t0 = tiles[idx]
            x_t, x_b = xr.next()
            r0 = int(offs[s]) + t0
            k.dma("sp", x_t[:], xsrc[r0:r0 + 512, :].rearrange("(tb p) c -> p tb c", p=128), writes=[x_b])
            return (x_t, x_b)

        def load_tab(idx):
            s, t0 = tiles[idx]
            r0 = int(offs[s]) + t0
            c_t, c_b = ct.next()
            s_t, s_b = st_.next()
            k.dma("sp", c_t[:], rc[:, r0:r0 + 512], writes=[c_b])
            k.dma("sp", s_t[:], rs_[:, r0:r0 + 512], writes=[s_b])
            return (c_t, c_b, s_t, s_b)

        def prepare(idx, x_t, x_b):
            s, t0 = tiles[idx]
            ss_t, sc_b = ss.next()
            rs_t, _ = rsd.next()
            j_t, j_b = junk.next()
            rms_scale_rows(k, x_t, x_b, ss_t, rs_t, sc_b, j_t, j_b, 4, D, NORM_EPS, G)
            xs_t, xs_b = xs.next()
            for tb in range(4):
                k.op(k.dve, lambda: nc.vector.tensor_scalar(out=xs_t[:, tb, :], in0=x_t[:, tb, :], scalar1=rs_t[:, tb:tb + 1],
                                                            scalar2=None, op0=ALU.mult),
                     reads=[x_b, sc_b], writes=[xs_b], disjoint=True)
            h_t, h_b = hT.next()
            for jj in range(0, NCH, 2):
                p_t, p_b = tp.next()
                for c in range(2):
                    for tb in range(4):
                        k.op(k.pe, lambda: nc.tensor.transpose(out=p_t[:, c, tb * 128:(tb + 1) * 128],
                                                               in_=xs_t[:, tb, (jj + c) * 128:(jj + c + 1) * 128],
                                                               identity=ident[:]),
                             reads=[xs_b], writes=[p_b], disjoint=True)
                for c in range(2):
                    j = jj + c
                    k.op(k.act, lambda: nc.scalar.activation(out=h_t[:, j, :], in_=p_t[:, c, :], func=AF.Identity,
                                                             scale=G["a1"][:, j, s:s + 1], bias=G["modT"][:, j, s:s + 1]),
                         reads=[p_b, G["gbuf"]], writes=[h_b], disjoint=True)
            return (h_t, h_b)

        ev = 0
        ntl = len(tiles)
        xcur = load_x(0)
        hcur = prepare(0, *xcur)
        xnext = load_x(1) if ntl > 1 else None
        tabcur = load_tab(0)
        for idx, (s, t0) in enumerate(tiles):
            h_t, h_b = hcur
            c_t, c_b, s_t, s_b = tabcur
            if idx + 1 < ntl:
                tabnext = load_tab(idx + 1)
            if full and t0 == 0:
                ub = int(offs[s]) + 2 * s
                k.dma("pool", T["U"][ub:ub + 1, :], zt[:], reads=[b_z])
                k.dma("pool", T["U"][ub + 1 + lens[s]:ub + 2 + lens[s], :], zt[:], reads=[b_z])
            if full:
                so_t, so_b = so.next()
                for tb in range(4):
                    for cc in range(4):
                        m_t, m_b = mm.next()
                        for j in range(NCH):
                            k.op(k.pe, lambda: nc.tensor.matmul(m_t[:, :], lhsT=h_t[:, j, tb * 128:(tb + 1) * 128],
                                                                rhs=wtm[:, j, cc * 512:(cc + 1) * 512],
                                                                start=(j == 0), stop=(j == NCH - 1)),
                                 reads=[h_b, b_w], writes=[m_b])
                        if ev % 2 == 0:
                            k.op(k.act, lambda: nc.scalar.copy(out=so_t[:, tb, cc * 512:(cc + 1) * 512], in_=m_t[:, :]),
                                 reads=[m_b], writes=[so_b], disjoint=True)
                        else:
                            k.op(k.dve, lambda: nc.vector.tensor_copy(out=so_t[:, tb, cc * 512:(cc + 1) * 512], in_=m_t[:, :]),
                                 reads=[m_b], writes=[so_b], disjoint=True)
                        ev += 1
                r0 = int(offs[s]) + t0
                ub = r0 + 2 * s + 1
                k.dma("pool", T["U"][ub:ub + 512, :].rearrange("(tb p) c -> p tb c", p=128), so_t[:, :, 0:1536], reads=[so_b])
                k.dma("pool", T["Vd"][r0:r0 + 512, :].rearrange("(tb p) c -> p tb c", p=128), so_t[:, :, 1536:2048], reads=[so_b])
            if idx + 1 < ntl:
                hnext = prepare(idx + 1, *xnext)
                xnext = load_x(idx + 2) if idx + 2 < ntl else None
            for n in range(4):
                m1_t, m1_b = mm.next()
                m2_t, m2_b = mm.next()
                for j in range(NCH):
                    k.op(k.pe, lambda: nc.tensor.matmul(m1_t[:, :], lhsT=wfm[:, j, n * 128:(n + 1) * 128], rhs=h_t[:, j, :],
                                                        start=(j == 0), stop=(j == NCH - 1)),
                         reads=[h_b, b_w], writes=[m1_b])
                for j in range(NCH):
                    k.op(k.pe, lambda: nc.tensor.matmul(m2_t[:, :], lhsT=wfp[:, j, n * 128:(n + 1) * 128], rhs=h_t[:, j, :],
                                                        start=(j == 0), stop=(j == NCH - 1)),
                         reads=[h_b, b_w], writes=[m2_b])
                a_t, a_b = t1.next()
                b_t, b_b = t2.next()
                f_t, f_b = fo.next()
                k.op(k.dve, lambda: nc.vector.tensor_tensor(out=a_t[:], in0=m1_t[:, :], in1=c_t[:], op=ALU.mult),
                     reads=[m1_b, c_b], writes=[a_b])
                k.op(k.dve, lambda: nc.vector.tensor_tensor(out=b_t[:], in0=m2_t[:, :], in1=s_t[:], op=ALU.mult),
                     reads=[m2_b, s_b], writes=[b_b])
                k.op(k.pool, lambda: nc.gpsimd.tensor_tensor(out=f_t[:], in0=a_t[:], in1=b_t[:], op=ALU.add),
                     reads=[a_b, b_b], writes=[f_b])
                r0 = int(offs[s]) + t0
                k.dma("pool", dst_fm[n * 128:(n + 1) * 128, r0:r0 + 512], f_t[:], reads=[f_b])
            if idx + 1 < ntl:
                hcur = hnext
                tabcur = tabnext
    k.barrier()


PHASES = ["weights", "mod", "projF", "projO", "filter", "hyena", "attn", "m1", "m2"]


def build(cfg, upto="m2"):
    nc = bass.Bass("TRN2", target_bir_lowering=False)
    T = declare(nc, cfg)
    last = PHASES.index(upto)
    with contextlib.ExitStack() as es:
        k = K(nc, es)
        ns = cfg.nseq
        G = dict(gbuf=Buf(), kh_buf={"p": Buf(), "s": Buf()})
        G["modT"] = sb(nc, es, "modT", [128, 48, ns], F32)
        G["a1"] = sb(nc, es, "a1", [128, NCH, ns], F32)
        G["a2"] = sb(nc, es, "a2", [128, NCH, ns], F32)
        G["ident"] = sb(nc, es, "ident", [128, 128], BF16)
        G["identf"] = sb(nc, es, "identf", [128, 128], F32)
        G["mhalf"] = sb(nc, es, "mhalf", [128, 8], F32)
        k.dma("sp", G["identf"][:], T["ident"][:, :], writes=[G["gbuf"]])
        k.op(k.dve, lambda: nc.vector.tensor_copy(out=G["ident"][:], in_=G["identf"][:]), reads=[G["gbuf"]], writes=[G["gbuf"]])
        k.op(k.dve, lambda: nc.vector.memset(G["mhalf"][:], -0.5), writes=[G["gbuf"]], disjoint=True)
        k.barrier()
        steps = [
            lambda: phase_weights(k, cfg, T),
            lambda: phase_mod(k, cfg, T, G),
            lambda: phase_proj(k, cfg, T, G, "F"),
            lambda: phase_proj(k, cfg, T, G, "O"),
            lambda: phase_filter(k, cfg, T, G),
            lambda: phase_hyena(k, cfg, T, G),
            lambda: phase_attn(k, cfg, T, G),
            lambda: phase_m1(k, cfg, T, G),
            lambda: phase_m2(k, cfg, T, G),
        ]
        for i, st in enumerate(steps):
            if i <= last:
                st()
        k.final_wait()
    return nc


def chunkT(v, ncols):
    return np.ascontiguousarray(np.asarray(v, np.float32).reshape(ncols, 128).T)


def host_inputs(cfg, inp, core):
    f32 = np.float32
    b, r = core // cfg.NQ, core % cfg.NQ
    own = cfg.own_p
    xs = [np.asarray(inp["x_sample"][core * cfg.NS + i], f32) for i in range(cfg.NS)]
    xp = np.asarray(inp["x_prompt"][b], f32)
    m = {}
    m["xf"] = np.ascontiguousarray(np.concatenate([xp] + xs, 0))
    m["xo"] = np.ascontiguousarray(np.concatenate([xp[r * own:(r + 1) * own]] + xs, 0))
    cs = np.stack([np.asarray(inp["c_prompt"][b], f32)] + [np.asarray(inp["c_sample"][core * cfg.NS + i], f32)
                                                             for i in range(cfg.NS)], 0)
    m["cT"] = np.ascontiguousarray(cs.reshape(cfg.nseq, NCH, 128).transpose(2, 1, 0))
    m["w_ada"] = np.ascontiguousarray(inp["w_ada"][0], f32)
    m["b_ada"] = np.ascontiguousarray(inp["b_ada"][0:1], f32)
    m["b_adaT"] = chunkT(inp["b_ada"][0], 48)
    m["nw1T"] = chunkT(inp["norm1_w"][0], NCH)
    m["nw2T"] = chunkT(inp["norm2_w"][0], NCH)
    m["hnwT"] = chunkT(inp["hyena_norm_w"][0], 4)
    w_in = np.asarray(inp["w_in"][0], f32)
    m["w_in"] = np.ascontiguousarray(w_in)
    p64 = rot_perm()
    pq = np.concatenate([1536 + h * 64 + p64 for h in range(8)])
    pk = np.concatenate([2048 + h * 64 + p64 for h in range(8)])
    m["w_perm"] = np.ascontiguousarray(w_in[:, np.concatenate([pq, pk])])
    m["w_out"] = np.ascontiguousarray(inp["w_out"][0], f32)
    m["w_mlp1"] = np.ascontiguousarray(inp["w_mlp1"][0], f32)
    m["w_mlp2"] = np.ascontiguousarray(inp["w_mlp2"][0], f32)
    m["ident"] = np.eye(128, dtype=f32)
    posF = np.concatenate([np.arange(L) for L in cfg.L])
    posO = np.concatenate([r * own + np.arange(own)] + [np.arange(cfg.Ls)] * cfg.NS)
    m["ropeF_c"], m["ropeF_s"] = rope_tables(posF)
    m["ropeO_c"], m["ropeO_s"] = rope_tables(posO)
    m["conv_w"] = np.ascontiguousarray(inp["conv_w"][0], f32)
    m["conv_b"] = np.ascontiguousarray(inp["conv_b"][0:1], f32)
    m["hyena_bias"] = np.ascontiguousarray(inp["hyena_bias"][0:1], f32)
    m["final_w"] = np.ascontiguousarray(np.asarray(inp["final_w"], f32)[None, :])
    m["subln_w"] = np.ascontiguousarray(inp["subln_w"][0:1], f32)
    m["lam4"] = np.ascontiguousarray(np.stack([inp["lambda_q1"][0], inp["lambda_k1"][0], inp["lambda_q2"][0],
                                               inp["lambda_k2"][0]], 0), f32)
    if cfg.debug and "Yh_in" in cfg.debug:
        m["Yh"] = np.ascontiguousarray(inp["_Yh"][core], f32)
    host_tables(cfg, inp, core, m)
    return m


def run(cfg, inp, upto="m2", ncores=8):
    nc = build(cfg, upto)
    maps = [host_inputs(cfg, inp, c) for c in range(ncores)]
    names = set()
    res = run_bass_kernel_spmd(nc, maps, core_ids=list(range(ncores)))
    return res.results


def kernel(**inputs):
    inp = {k_: np.asarray(v) for k_, v in inputs.items()}
    cfg = Cfg()
    res = run(cfg, inp)
    B, S, _ = inp["x_prompt"].shape
    yp = np.zeros((B, S, D), np.float32)
    ysm = np.zeros(inp["x_sample"].shape, np.float32)
    own = cfg.own_p
    for c in range(8):
        y = res[c]["y"]
        b, r = c // cfg.NQ, c % cfg.NQ
        yp[b, r * own:(r + 1) * own] = y[0:own]
        for i in range(cfg.NS):
            ysm[c * cfg.NS + i] = y[own + i * cfg.Ls: own + (i + 1) * cfg.Ls]
    return (yp, ysm)


LAMBDA_INIT = 0.8 - 0.6 * math.exp(-0.3 * 0)


def bc_ap(ap2d, nparts=128):
    n = 1
    for d in ap2d.shape:
        n *= d
    return bass.AP(tensor=ap2d.tensor, offset=ap2d.offset, ap=[[0, nparts], [1, n]])


def phase_attn(k, cfg, T, G):
    nc = k.nc
    Lmax, omax = max(cfg.L), max(cfg.own)
    with contextlib.ExitStack() as es:
        lamt = sb(nc, es, "lamt", [128, 256], F32)
        lj = sb(nc, es, "lj", [128, 64], F32)
        lacc = sb(nc, es, "lacc", [128, 4], F32)
        nlam = sb(nc, es, "nlam", [128, 1], F32)
        slw = sb(nc, es, "slw", [128, 128], F32)
        b_l, b_sl = Buf(), Buf()
        k.dma("sp", lamt[:], bc_ap(T["lam4"]), writes=[b_l])
        k.dma("sp", slw[:], bc_ap(T["subln_w"]), writes=[b_sl])
        k.op(k.dve, lambda: nc.vector.scalar_tensor_tensor(out=lj[:], in0=lamt[:, 0:64], scalar=1.0, in1=lamt[:, 64:128],
                                                           op0=ALU.mult, op1=ALU.mult, accum_out=lacc[:, 0:1]),
             reads=[b_l], writes=[b_l])
        k.op(k.dve, lambda: nc.vector.scalar_tensor_tensor(out=lj[:], in0=lamt[:, 128:192], scalar=1.0, in1=lamt[:, 192:256],
                                                           op0=ALU.mult, op1=ALU.mult, accum_out=lacc[:, 1:2]),
             reads=[b_l], writes=[b_l])
        k.op(k.act, lambda: nc.scalar.activation(out=lacc[:, 2:4], in_=lacc[:, 0:2], func=AF.Exp), reads=[b_l], writes=[b_l])
        k.op(k.dve, lambda: nc.vector.tensor_tensor(out=nlam[:], in0=lacc[:, 3:4], in1=lacc[:, 2:3], op=ALU.subtract),
             reads=[b_l], writes=[b_l])
        k.op(k.dve, lambda: nc.vector.tensor_scalar(out=nlam[:], in0=nlam[:], scalar1=-LAMBDA_INIT, scalar2=None, op0=ALU.add),
             reads=[b_l], writes=[b_l])
        k.op(k.dve, lambda: nc.vector.tensor_scalar(out=slw[:], in0=slw[:], scalar1=(1.0 - LAMBDA_INIT), scalar2=None, op0=ALU.mult),
             reads=[b_sl], writes=[b_sl])

        ktr = Ring([sb(nc, es, f"akt{i}", [128, Lmax], BF16) for i in range(2)])
        v1r = Ring([sb(nc, es, f"av1{i}", [128, Lmax // 128, 128], BF16) for i in range(2)])
        qtr = Ring([sb(nc, es, f"aqt{i}", [128, omax], BF16) for i in range(2)])
        er = Ring([sb(nc, es, f"ae{i}", [128, 512], BF16) for i in range(6)])
        osr = Ring([sb(nc, es, f"aos{i}", [128, 4, 128], BF16) for i in range(2)])
        o_r = Ring([sb(nc, es, f"ao{i}", [128, 128], F32) for i in range(2)])
        jk = sb(nc, es, "ajk", [128, 128], F32)
        b_jk = Buf()
        st = Ring([sb(nc, es, f"ast{i}", [128, 8], F32) for i in range(2)])
        zacc = [sb(nc, es, f"azc{i}", [128, 512], F32) for i in range(4)]
        zb = [Buf(), Buf(), Buf(), Buf()]
        otr = Ring([sb(nc, es, f"aot{i}", [128, 2, 512], F32) for i in range(2)])
        ones1 = sb(nc, es, "aones", [128, 1], F32)
        k.op(k.dve, lambda: nc.vector.memset(ones1[:], 1.0), writes=[b_l], disjoint=True)
        sbank = Ring([pm(nc, es, f"asb{i}", [128, 512], F32) for i in range(3)])
        obank = [pm(nc, es, f"aob{i}", [128, 512], F32) for i in range(2)]
        ob_b = [Buf(), Buf()]
        ebank = [pm(nc, es, f"aeb{i}", [128, 512], F32) for i in range(3)]
        regs = Ring([ebank[b][:, c * 129:(c + 1) * 129] for (b, c) in ((0, 0), (0, 1), (0, 2), (1, 0), (1, 1), (1, 2), (2, 0), (2, 1))])
        fence_b = Buf()

        for s in range(cfg.nseq):
            L, own = cfg.L[s], cfg.own[s]
            f0, o0 = int(cfg.foff[s]), int(cfg.ooff[s])
            nkb = L // 128
            for h in range(4):
                kt, kt_b = ktr.next()
                v1, v1_b = v1r.next()
                qt, qt_b = qtr.next()
                k.dma("sp", kt[:, 0:L], T["KT"][h * 128:(h + 1) * 128, f0:f0 + L], writes=[kt_b])
                k.dma("sp", v1[:, 0:nkb, :], T["Vd"][f0:f0 + L, h * 128:(h + 1) * 128].rearrange("(kb p) c -> p kb c", p=128),
                      writes=[v1_b])
                k.dma("sp", qt[:, 0:own], T["QT"][h * 128:(h + 1) * 128, o0:o0 + own], writes=[qt_b])
                for qc in range(own // 512):
                    def qk(kb):
                        out = []
                        for e in range(2):
                            sp_, sp_b = sbank.next()
                            lo = e * 64
                            k.op(k.pe, lambda: nc.tensor.matmul(sp_[:, :], lhsT=kt[lo:lo + 64, kb * 128:(kb + 1) * 128],
                                                                rhs=qt[lo:lo + 64, qc * 512:(qc + 1) * 512], start=True, stop=True),
                                 reads=[kt_b, qt_b], writes=[sp_b])
                            e_t, e_b = er.next()
                            k.op(k.act, lambda: nc.scalar.activation(out=e_t[:], in_=sp_[:, :], func=AF.Exp, scale=0.125),
                                 reads=[sp_b], writes=[e_b])
                            out.append((e_t, e_b))
                        return out
                    pend = qk(0)
                    used_pool = False
                    used_pool0 = False
                    for kb in range(nkb):
                        nxt = qk(kb + 1) if kb + 1 < nkb else None
                        for e in range(2):
                            e_t, e_b = pend[e]
                            k.op(k.pe, lambda: nc.tensor.matmul(obank[e][:, :], lhsT=v1[:, kb, :], rhs=e_t[:], start=(kb == 0), stop=(kb == nkb - 1)),
                                 reads=[e_b, v1_b], writes=[ob_b[e]])
                            zi, E = e, k.dve
                            first = (kb == 0)
                            if first:
                                k.op(E, lambda: E.eng.tensor_copy(out=zacc[zi][:], in_=e_t[:]), reads=[e_b], writes=[zb[zi]])
                            else:
                                k.op(E, lambda: E.eng.tensor_tensor(out=zacc[zi][:], in0=zacc[zi][:], in1=e_t[:], op=ALU.add),
                                     reads=[e_b, zb[zi]], writes=[zb[zi]])
                        pend = nxt
                    ot, ot_b = otr.next()
                    for e in range(2):
                        k.op(k.act, lambda: nc.scalar.copy(out=ot[:, e, :], in_=obank[e][:, :]), reads=[ob_b[e]], writes=[ot_b], disjoint=True)
                    rr = []
                    for qs in range(4):
                        pair = []
                        for e in range(2):
                            rg, rg_b = regs.next()
                            k.op(k.pe, lambda: nc.tensor.transpose(out=rg[:, 0:128], in_=ot[:, e, qs * 128:(qs + 1) * 128], identity=G["identf"][:]),
                                 reads=[ot_b, G["gbuf"]], writes=[rg_b])
                            zl = ([0, 3] if used_pool0 else [0]) if e == 0 else ([1, 2] if used_pool else [1])
                            for zi_, zi in enumerate(zl):
                                k.op(k.pe, lambda: nc.tensor.matmul(rg[:, 128:129], lhsT=zacc[zi][:, qs * 128:(qs + 1) * 128], rhs=ones1[:, 0:1],
                                                                    start=(zi_ == 0), stop=(zi_ == len(zl) - 1), skip_group_check=True),
                                     reads=[zb[zi], b_l], writes=[rg_b], disjoint=True)
                            pair.append((rg, rg_b))
                        rr.append(pair)
                        if qs % 2 == 1:
                            k.op(k.pe, lambda: nc.tensor.matmul(ebank[2][:, 400:401], lhsT=G["identf"][:, :], rhs=ones1[:, 0:1], start=True, stop=True,
                                                                skip_group_check=True),
                                 reads=[b_l, G["gbuf"]], writes=[fence_b])
                    os_t, os_b = osr.next()
                    for qs in range(4):
                        (a1_, a1b), (a2_, a2b) = rr[qs]
                        s_t, s_b = st.next()
                        o_t, o_b = o_r.next()
                        k.op(k.dve, lambda: nc.vector.reciprocal(out=s_t[:, 0:1], in_=a1_[:, 128:129]), reads=[a1b, a2b, fence_b], writes=[s_b])
                        k.op(k.dve, lambda: nc.vector.reciprocal(out=s_t[:, 1:2], in_=a2_[:, 128:129]), reads=[a2b], writes=[s_b])
                        k.op(k.dve, lambda: nc.vector.tensor_tensor(out=s_t[:, 2:3], in0=s_t[:, 1:2], in1=nlam[:], op=ALU.mult),
                             reads=[s_b, b_l], writes=[s_b])
                        k.op(k.dve, lambda: nc.vector.tensor_scalar(out=o_t[:], in0=a1_[:, 0:128], scalar1=s_t[:, 0:1], scalar2=None,
                                                                    op0=ALU.mult), reads=[a1b, s_b], writes=[o_b])
                        k.op(k.dve, lambda: nc.vector.scalar_tensor_tensor(out=o_t[:], in0=a2_[:, 0:128], scalar=s_t[:, 2:3], in1=o_t[:],
                                                                           op0=ALU.mult, op1=ALU.add),
                             reads=[a2b, s_b, o_b], writes=[o_b])
                        k.op(k.dve, lambda: nc.vector.scalar_tensor_tensor(out=jk[:], in0=o_t[:], scalar=1.0, in1=o_t[:], op0=ALU.mult,
                                                                           op1=ALU.mult, accum_out=s_t[:, 3:4]),
                             reads=[o_b], writes=[b_jk, s_b])
                        k.op(k.pool, lambda: nc.gpsimd.tensor_scalar(out=s_t[:, 4:5], in0=s_t[:, 3:4], scalar1=1.0 / 128, scalar2=SUBLN_EPS,
                                                                     op0=ALU.mult, op1=ALU.add), reads=[s_b], writes=[s_b])
                        k.op(k.pool, lambda: nc.gpsimd.tensor_tensor(out=s_t[:, 5:6], in0=s_t[:, 4:5], in1=G["mhalf"][:, 0:1], op=ALU.pow),
                             reads=[s_b], writes=[s_b])
                        k.op(k.dve, lambda: nc.vector.scalar_tensor_tensor(out=os_t[:, qs, :], in0=o_t[:], scalar=s_t[:, 5:6], in1=slw[:],
                                                                           op0=ALU.mult, op1=ALU.mult),
                             reads=[o_b, s_b, b_sl], writes=[os_b], disjoint=True)
                    r0 = o0 + qc * 512
                    k.dma("pool", T["Oa"][r0:r0 + 512, h * 128:(h + 1) * 128].rearrange("(qs p) c -> p qs c", p=128), os_t[:], reads=[os_b])
    k.barrier()


def transposes_to(k, G, src_t, src_b, tp, dst_t, dst_b, evac):
    nc = k.nc
    for jj in range(0, NCH, 2):
        p_t, p_b = tp.next()
        for c in range(2):
            for tb in range(4):
                k.op(k.pe, lambda: nc.tensor.transpose(out=p_t[:, c, tb * 128:(tb + 1) * 128],
                                                       in_=src_t[:, tb, (jj + c) * 128:(jj + c + 1) * 128], identity=G["ident"][:]),
                     reads=[src_b], writes=[p_b], disjoint=True)
        for c in range(2):
            evac(jj + c, dst_t[:, jj + c, :], p_t[:, c, :], p_b, dst_b)


def phase_m1(k, cfg, T, G):
    nc = k.nc
    with contextlib.ExitStack() as es:
        wo = sb(nc, es, "wo", [128, NCH, D], BF16)
        hnw = sb(nc, es, "hnw", [128, 4], F32)
        b_w = Buf()
        k.dma("sp", wo[:], T["w_out_b"][:, :, :], writes=[b_w])
        k.dma("sp", hnw[:], T["hnwT"][:, :], writes=[b_w], disjoint=True)
        g1 = Ring([sb(nc, es, f"g1{i}", [128, D], F32) for i in range(2)])
        xr = Ring([sb(nc, es, f"mx{i}", [128, 4, D], F32) for i in range(2)])
        yr = Ring([sb(nc, es, f"my{i}", [128, 4, 512], F32) for i in range(2)])
        mix = Ring([sb(nc, es, f"mmix{i}", [128, 4, D], BF16) for i in range(2)])
        mT = Ring([sb(nc, es, f"mmT{i}", [128, NCH, 512], BF16) for i in range(2)])
        ss = Ring([sb(nc, es, f"mss{i}", [128, 8], F32) for i in range(2)])
        rsd = Ring([sb(nc, es, f"mrs{i}", [128, 8], F32) for i in range(2)])
        junk = Ring([sb(nc, es, "mjk", [128, D], F32)])
        tmp = Ring([sb(nc, es, f"mtmp{i}", [128, 512], F32) for i in range(2)])
        tp = Ring([pm(nc, es, f"mtp{i}", [128, 2, 512], BF16) for i in range(2)])
        mm = Ring([pm(nc, es, f"mmm{i}", [128, 512], F32) for i in range(4)])
        for s in range(cfg.nseq):
            g_t, g_b = g1.next()
            k.dma("sp", g_t[:], bc_ap(T["Gd"][s, 0:1, :]), writes=[g_b])
            for t0 in range(0, cfg.own[s], 512):
                r0 = int(cfg.ooff[s]) + t0
                x_t, x_b = xr.next()
                y_t, y_b = yr.next()
                m_t, m_b = mix.next()
                k.dma("sp", x_t[:], T["xo"][r0:r0 + 512, :].rearrange("(tb p) c -> p tb c", p=128), writes=[x_b])
                k.dma("sp", y_t[:], T["Yh"][r0:r0 + 512, :].rearrange("(tb p) c -> p tb c", p=128), writes=[y_b])
                k.dma("sp", m_t[:, :, 512:1024], T["Oa"][r0:r0 + 512, :].rearrange("(tb p) c -> p tb c", p=128), writes=[m_b])
                ss_t, sc_b = ss.next()
                rs_t, _ = rsd.next()
                j_t, j_b = junk.next()
                rms_scale_rows(k, y_t, y_b, ss_t, rs_t, sc_b, j_t, j_b, 4, 512, NORM_EPS, G)
                for tb in range(4):
                    k.op(k.dve, lambda: nc.vector.tensor_scalar(out=m_t[:, tb, 0:512], in0=y_t[:, tb, :], scalar1=rs_t[:, tb:tb + 1],
                                                                scalar2=None, op0=ALU.mult),
                         reads=[y_b, sc_b], writes=[m_b], disjoint=True)
                t_t, t_b = mT.next()

                def evac(j, o, i, pb, db):
                    if j < 4:
                        k.op(k.act, lambda: nc.scalar.activation(out=o, in_=i, func=AF.Identity, scale=hnw[:, j:j + 1]),
                             reads=[pb, b_w], writes=[db], disjoint=True)
                    else:
                        k.op(k.act, lambda: nc.scalar.copy(out=o, in_=i), reads=[pb], writes=[db], disjoint=True)
                transposes_to(k, G, m_t, m_b, tp, t_t, t_b, evac)
                for tb in range(4):
                    for cc in range(2):
                        p_t, p_b = mm.next()
                        for j in range(NCH):
                            k.op(k.pe, lambda: nc.tensor.matmul(p_t[:, :], lhsT=t_t[:, j, tb * 128:(tb + 1) * 128],
                                                                rhs=wo[:, j, cc * 512:(cc + 1) * 512], start=(j == 0), stop=(j == NCH - 1)),
                                 reads=[t_b, b_w], writes=[p_b])
                        q_t, q_b = tmp.next()
                        k.op(k.dve, lambda: nc.vector.tensor_tensor(out=q_t[:], in0=p_t[:, :], in1=g_t[:, cc * 512:(cc + 1) * 512], op=ALU.mult),
                             reads=[p_b, g_b], writes=[q_b])
                        k.op(k.pool, lambda: nc.gpsimd.tensor_tensor(out=x_t[:, tb, cc * 512:(cc + 1) * 512], in0=q_t[:],
                                                                     in1=x_t[:, tb, cc * 512:(cc + 1) * 512], op=ALU.add),
                             reads=[q_b, x_b], writes=[x_b], disjoint=True)
                k.dma("pool", T["X1"][r0:r0 + 512, :].rearrange("(tb p) c -> p tb c", p=128), x_t[:], reads=[x_b])
    k.barrier()


def phase_m2(k, cfg, T, G):
    nc = k.nc
    with contextlib.ExitStack() as es:
        w2 = sb(nc, es, "w2", [128, 32, D], BF16)
        fw = sb(nc, es, "fw", [128, D], F32)
        b_w = Buf()
        for i in range(4):
            k.dma("sp", w2[:, i * 8:(i + 1) * 8, :], T["w_mlp2_b"][:, i * 8:(i + 1) * 8, :], writes=[b_w], disjoint=True)
        k.dma("sp", fw[:], bc_ap(T["final_w"]), writes=[b_w], disjoint=True)
        w1r = Ring([sb(nc, es, f"w1r{i}", [128, NCH, 512], BF16) for i in range(3)])
        g2 = Ring([sb(nc, es, f"g2{i}", [128, D], F32) for i in range(2)])
        xr = Ring([sb(nc, es, f"nx{i}", [128, 4, D], F32) for i in range(2)])
        xs = Ring([sb(nc, es, f"nxs{i}", [128, 4, D], BF16) for i in range(1)])
        hT = Ring([sb(nc, es, f"nhT{i}", [128, NCH, 512], BF16) for i in range(1)])
        aT = Ring([sb(nc, es, f"naT{i}", [128, 32, 512], BF16) for i in range(1)])
        rl = Ring([sb(nc, es, f"nrl{i}", [128, 512], F32) for i in range(3)])
        ss = Ring([sb(nc, es, f"nss{i}", [128, 8], F32) for i in range(2)])
        rsd = Ring([sb(nc, es, f"nrs{i}", [128, 8], F32) for i in range(2)])
        junk = Ring([sb(nc, es, "njk", [128, D], F32)])
        tmp = Ring([sb(nc, es, f"ntmp{i}", [128, 512], F32) for i in range(2)])
        tp = Ring([pm(nc, es, f"ntp{i}", [128, 2, 512], BF16) for i in range(2)])
        mm = Ring([pm(nc, es, f"nmm{i}", [128, 512], F32) for i in range(6)])
        ev = 0
        for s in range(cfg.nseq):
            g_t, g_b = g2.next()
            k.dma("sp", g_t[:], bc_ap(T["Gd"][s, 1:2, :]), writes=[g_b])
            for t0 in range(0, cfg.own[s], 512):
                r0 = int(cfg.ooff[s]) + t0
                x_t, x_b = xr.next()
                k.dma("sp", x_t[:], T["X1"][r0:r0 + 512, :].rearrange("(tb p) c -> p tb c", p=128), writes=[x_b])
                ss_t, sc_b = ss.next()
                rs_t, _ = rsd.next()
                j_t, j_b = junk.next()
                rms_scale_rows(k, x_t, x_b, ss_t, rs_t, sc_b, j_t, j_b, 4, D, NORM_EPS, G)
                xs_t, xs_b = xs.next()
                for tb in range(4):
                    k.op(k.dve, lambda: nc.vector.tensor_scalar(out=xs_t[:, tb, :], in0=x_t[:, tb, :], scalar1=rs_t[:, tb:tb + 1],
                                                                scalar2=None, op0=ALU.mult),
                         reads=[x_b, sc_b], writes=[xs_b], disjoint=True)
                h_t, h_b = hT.next()

                def evac(j, o, i, pb, db):
                    k.op(k.act, lambda: nc.scalar.activation(out=o, in_=i, func=AF.Identity, scale=G["a2"][:, j, s:s + 1],
                                                             bias=G["modT"][:, 24 + j, s:s + 1]),
                         reads=[pb, G["gbuf"]], writes=[db], disjoint=True)
                transposes_to(k, G, xs_t, xs_b, tp, h_t, h_b, evac)
                a_t, a_b = aT.next()
                for mc in range(8):
                    w_t, w_b = w1r.next()
                    k.dma("sp", w_t[:], T["w_mlp1_b"][:, :, mc * 512:(mc + 1) * 512], writes=[w_b])
                    for mi in range(4):
                        m = mc * 4 + mi
                        p_t, p_b = mm.next()
                        for j in range(NCH):
                            k.op(k.pe, lambda: nc.tensor.matmul(p_t[:, :], lhsT=w_t[:, j, mi * 128:(mi + 1) * 128], rhs=h_t[:, j, :],
                                                                start=(j == 0), stop=(j == NCH - 1)),
                                 reads=[w_b, h_b], writes=[p_b])
                        r_t, r_b = rl.next()
                        k.op(k.act, lambda: nc.scalar.activation(out=r_t[:], in_=p_t[:, :], func=AF.Relu), reads=[p_b], writes=[r_b])
                        E = k.dve if ev % 2 == 0 else k.pool
                        ev += 1
                        k.op(E, lambda: E.eng.tensor_tensor(out=a_t[:, m, :], in0=r_t[:], in1=r_t[:], op=ALU.mult),
                             reads=[r_b], writes=[a_b], disjoint=True)
                for tb in range(4):
                    for cc in range(2):
                        p_t, p_b = mm.next()
                        for m in range(32):
                            k.op(k.pe, lambda: nc.tensor.matmul(p_t[:, :], lhsT=a_t[:, m, tb * 128:(tb + 1) * 128],
                                                                rhs=w2[:, m, cc * 512:(cc + 1) * 512], start=(m == 0), stop=(m == 31)),
                                 reads=[a_b, b_w], writes=[p_b])
                        q_t, q_b = tmp.next()
                        k.op(k.dve, lambda: nc.vector.tensor_tensor(out=q_t[:], in0=p_t[:, :], in1=g_t[:, cc * 512:(cc + 1) * 512], op=ALU.mult),
                             reads=[p_b, g_b], writes=[q_b])
                        k.op(k.pool, lambda: nc.gpsimd.tensor_tensor(out=x_t[:, tb, cc * 512:(cc + 1) * 512], in0=q_t[:],
                                                                     in1=x_t[:, tb, cc * 512:(cc + 1) * 512], op=ALU.add),
                             reads=[q_b, x_b], writes=[x_b], disjoint=True)
                ss_t, sc_b = ss.next()
                rs_t, _ = rsd.next()
                rms_scale_rows(k, x_t, x_b, ss_t, rs_t, sc_b, j_t, j_b, 4, D, NORM_EPS, G)
                for tb in range(4):
                    k.op(k.dve, lambda: nc.vector.scalar_tensor_tensor(out=x_t[:, tb, :], in0=x_t[:, tb, :], scalar=rs_t[:, tb:tb + 1],
                                                                       in1=fw[:], op0=ALU.mult, op1=ALU.mult),
                         reads=[x_b, sc_b, b_w], writes=[x_b], disjoint=True)
                k.dma("pool", T["y"][r0:r0 + 512, :].rearrange("(tb p) c -> p tb c", p=128), x_t[:], reads=[x_b])
    k.barrier()


def fft_tables(L, n1_start, m_own):
    f32 = np.float32
    J = L // 128
    N = 2 * L
    G = 128 // J if J <= 128 else 1
    t = {}
    n1 = np.arange(256)[:, None].astype(np.float64)
    k1 = np.arange(256)[None, :].astype(np.float64)
    ph = 2 * np.pi * n1 * k1 / 256.0
    FA = np.stack([np.cos(ph), -np.sin(ph)], 0).reshape(2, 2, 128, 256)
    t["FA"] = FA.astype(f32)
    n2 = np.arange(J)[None, :].astype(np.float64)
    kk = np.arange(256)[:, None].astype(np.float64)
    th = 2 * np.pi * n2 * kk / N
    tw = np.stack([np.cos(th), -np.sin(th), np.sin(th)], 0)
    t["twA"] = np.ascontiguousarray(tw.reshape(3, 2, 128, J).transpose(2, 0, 1, 3)).astype(f32)
    a = np.arange(J)[:, None].astype(np.float64)
    b = np.arange(J)[None, :].astype(np.float64)
    pj = 2 * np.pi * a * b / J
    FBr, FBi = np.cos(pj), -np.sin(pj)
    blk = lambda M: np.kron(np.eye(G), M)
    t["FB"] = np.stack([blk(FBr), blk(FBi), blk(-FBi)], 0).astype(f32)
    q = np.arange(128)
    k1p, n2q = q // J, q % J
    g = np.arange(256 // G)
    k1g = g[None, :] * G + k1p[:, None]
    thb = 2 * np.pi * n2q[:, None] * k1g / N
    t["twB"] = np.stack([np.cos(thb), -np.sin(thb), np.sin(thb)], 1).astype(f32)
    n1o = (n1_start + np.arange(m_own))[None, :].astype(np.float64)
    k1c = np.arange(256)[:, None].astype(np.float64)
    ph2 = 2 * np.pi * n1o * k1c / 256.0
    IA = np.stack([np.cos(ph2) / N, -np.sin(ph2) / N], 0).reshape(2, 2, 128, m_own)
    t["IA"] = IA.astype(f32)
    sel = np.zeros((128, m_own), f32)
    sel[n1_start + np.arange(m_own), np.arange(m_own)] = 1.0
    t["Sel"] = sel
    z, decay = filter_feats(L)
    t["zTf"] = np.ascontiguousarray(z.T)
    zb = np.concatenate([z[0:1], z[:0:-1]], 0)
    t["zTb"] = np.ascontiguousarray(zb.T)
    t["decF"] = decay
    db = np.concatenate([np.zeros((1, HW), f32), decay[:0:-1]], 0)
    t["decB"] = np.ascontiguousarray(db)
    return t


def host_tables(cfg, inp, core, m):
    r = core % cfg.NQ
    mo_p = cfg.own_p // (cfg.Lp // 128)
    tp_ = fft_tables(cfg.Lp, r * mo_p, mo_p)
    ts_ = fft_tables(cfg.Ls, 0, 128)
    for k_, v in tp_.items():
        m[k_ + "_p"] = v
    for k_, v in ts_.items():
        m[k_ + "_s"] = v
    f32 = np.float32
    m["filt_w1"] = np.ascontiguousarray(inp["filt_w1"][0], f32)
    m["filt_w2"] = np.ascontiguousarray(inp["filt_w2"][0], f32)
    m["filt_w3"] = np.ascontiguousarray(inp["filt_w3"][0], f32)
    m["filt_w4"] = np.ascontiguousarray(inp["filt_w4"][0], f32)
    m["filt_bf"] = np.ascontiguousarray(np.stack([inp["filt_b1"][0], inp["filt_b2"][0], inp["filt_b3"][0], inp["filt_freq"][0]], 1), f32)


def declare_hyena(nc, cfg, T, inp, scr):
    for ty, L in (("p", cfg.Lp), ("s", cfg.Ls)):
        J = L // 128
        G = 128 // J
        mo = (cfg.own_p // J) if ty == "p" else 128
        inp("FA_" + ty, [2, 2, 128, 256]); inp("twA_" + ty, [128, 3, 2, J]); inp("FB_" + ty, [3, 128, 128])
        inp("twB_" + ty, [128, 3, 256 // G]); inp("IA_" + ty, [2, 2, 128, mo]); inp("Sel_" + ty, [128, mo])
        inp("zTf_" + ty, [FILTER_EMB, L]); inp("zTb_" + ty, [FILTER_EMB, L]); inp("decF_" + ty, [L, HW]); inp("decB_" + ty, [L, HW])
        scr("Bd_" + ty, [2, 256 * J, 512]); scr("Cd_" + ty, [2, 256 * J, 512]); scr("Kh_" + ty, [2, 256 * J, 512])
    inp("filt_w1", [FILTER_EMB, FORDER]); inp("filt_w2", [FORDER, FORDER]); inp("filt_w3", [FORDER, FORDER])
    inp("filt_w4", [FORDER, 2 * HW]); inp("filt_bf", [FORDER, 4])
    scr("XS", [2, cfg.TO, 512])


def fft_consts(k, es, T, ty, J, mo, need_inv):
    nc = k.nc
    ng = 2 * J
    c = {}
    b = Buf()
    c["buf"] = b
    stg = sb(nc, es, "fst", [128, 1024], F32)
    b_s = Buf()
    c["FA"] = sb(nc, es, "FA", [128, 2, 2, 256], BF16)
    k.dma("sp", stg[:, 0:1024].rearrange("n (a h k) -> n a h k", a=2, h=2), T["FA_" + ty].rearrange("a h n k -> n a h k"), writes=[b_s])
    k.op(k.dve, lambda: nc.vector.tensor_copy(out=c["FA"][:].rearrange("n a h k -> n (a h k)"), in_=stg[:, 0:1024]), reads=[b_s], writes=[b, b_s])
    c["FB"] = sb(nc, es, "FB", [128, 3, 128], BF16)
    k.dma("sp", stg[:, 0:384].rearrange("n (a k) -> n a k", a=3), T["FB_" + ty].rearrange("a n k -> n a k"), writes=[b_s])
    k.op(k.dve, lambda: nc.vector.tensor_copy(out=c["FB"][:].rearrange("n a k -> n (a k)"), in_=stg[:, 0:384]), reads=[b_s], writes=[b, b_s], disjoint=True)
    c["twA"] = sb(nc, es, "twA", [128, 3, 2, J], F32)
    k.dma("sp", c["twA"][:], T["twA_" + ty][:, :, :, :], writes=[b], disjoint=True)
    c["twB"] = sb(nc, es, "twB", [128, 3, ng], F32)
    k.dma("sp", c["twB"][:], T["twB_" + ty][:, :, :], writes=[b], disjoint=True)
    if need_inv:
        c["IA"] = sb(nc, es, "IA", [128, 2, 2, mo], BF16)
        k.dma("sp", stg[:, 0:4 * mo].rearrange("n (a h m) -> n a h m", a=2, h=2), T["IA_" + ty].rearrange("a h n m -> n a h m"), writes=[b_s])
        k.op(k.dve, lambda: nc.vector.tensor_copy(out=c["IA"][:].rearrange("n a h m -> n (a h m)"), in_=stg[:, 0:4 * mo]), reads=[b_s], writes=[b, b_s], disjoint=True)
        c["Sel"] = sb(nc, es, "Sel", [128, mo], BF16)
        k.dma("sp", stg[:, 0:mo], T["Sel_" + ty][:, :], writes=[b_s])
        k.op(k.dve, lambda: nc.vector.tensor_copy(out=c["Sel"][:], in_=stg[:, 0:mo]), reads=[b_s], writes=[b, b_s], disjoint=True)
    return c


def tw_evac(k, p_re, b_re, p_im, b_im, a, bb, cc, cb, o_re, o_im, ob, tmp):
    nc = k.nc
    t1, t1b = tmp.next()
    P = o_re.shape[0]
    k.op(k.act, lambda: nc.scalar.activation(out=t1[0:P, :], in_=p_re, func=AF.Identity, scale=a), reads=[b_re, cb], writes=[t1b])
    k.op(k.dve, lambda: nc.vector.scalar_tensor_tensor(out=o_re, in0=p_im, scalar=cc, in1=t1[0:P, :], op0=ALU.mult, op1=ALU.add),
         reads=[b_im, t1b, cb], writes=[ob], disjoint=True)
    t2, t2b = tmp.next()
    k.op(k.act, lambda: nc.scalar.activation(out=t2[0:P, :], in_=p_re, func=AF.Identity, scale=bb), reads=[b_re, cb], writes=[t2b])
    k.op(k.dve, lambda: nc.vector.scalar_tensor_tensor(out=o_im, in0=p_im, scalar=a, in1=t2[0:P, :], op0=ALU.mult, op1=ALU.add),
         reads=[b_im, t2b, cb], writes=[ob], disjoint=True)


def stage_a(k, c, J, n2, rhs_list, rhs_bufs, mm, tmp, bout, dstB, dst_buf):
    nc = k.nc
    nh = len(rhs_list)
    for ch in range(2):
        pr, prb = mm.next()
        pi, pib = mm.next()
        for cs, (pt, ptb) in enumerate(((pr, prb), (pi, pib))):
            for h in range(nh):
                k.op(k.pe, lambda: nc.tensor.matmul(pt[:, :], lhsT=c["FA"][:, cs, h, ch * 128:(ch + 1) * 128], rhs=rhs_list[h],
                                                    start=(h == 0), stop=(h == nh - 1)),
                     reads=[c["buf"]] + rhs_bufs, writes=[ptb])
        o_t, o_b = bout.next()
        tw = c["twA"]
        tw_evac(k, pr[:, :], prb, pi[:, :], pib, tw[:, 0, ch, n2:n2 + 1], tw[:, 1, ch, n2:n2 + 1], tw[:, 2, ch, n2:n2 + 1], c["buf"],
                o_t[:, 0, :], o_t[:, 1, :], o_b, tmp)
        r0 = ch * 128 * J + n2
        R = 256 * J
        d0 = dstB[0]
        dst = bass.AP(tensor=d0.tensor, offset=d0.offset + r0 * 512, ap=[[J * 512, 128], [R * 512, 2], [1, 512]])
        k.dma("pool", dst, o_t[:], reads=[o_b], writes=[dst_buf], disjoint=True)


def phase_filter(k, cfg, T, G):
    nc = k.nc
    PI = math.pi
    for ty, L in (("p", cfg.Lp), ("s", cfg.Ls)):
        J = L // 128
        ng = 2 * J
        with contextlib.ExitStack() as es:
            c = fft_consts(k, es, T, ty, J, 128, False)
            w1 = sb(nc, es, "fw1", [FILTER_EMB, FORDER], F32)
            w2 = sb(nc, es, "fw2", [FORDER, FORDER], F32)
            w3 = sb(nc, es, "fw3", [FORDER, FORDER], F32)
            w4 = sb(nc, es, "fw4", [FORDER, 2 * HW], F32)
            bf = sb(nc, es, "fbf", [FORDER, 4], F32)
            frb = sb(nc, es, "frb", [FORDER, 4], F32)
            npi = sb(nc, es, "npi", [FORDER, 1], F32)
            b_w = Buf()
            k.dma("sp", w1[:], T["filt_w1"][:, :], writes=[b_w])
            k.dma("sp", w2[:], T["filt_w2"][:, :], writes=[b_w], disjoint=True)
            k.dma("sp", w3[:], T["filt_w3"][:, :], writes=[b_w], disjoint=True)
            k.dma("sp", w4[:], T["filt_w4"][:, :], writes=[b_w], disjoint=True)
            k.dma("sp", bf[:], T["filt_bf"][:, :], writes=[b_w], disjoint=True)
            k.op(k.dve, lambda: nc.vector.tensor_scalar(out=frb[:, 0:3], in0=bf[:, 0:3], scalar1=bf[:, 3:4], scalar2=None, op0=ALU.mult),
                 reads=[b_w], writes=[b_w])
            k.op(k.dve, lambda: nc.vector.memset(npi[:], 0.0), writes=[b_w], disjoint=True)
            h3 = [sb(nc, es, f"h3{d}", [FORDER, L], BF16) for d in range(2)]
            w4b = sb(nc, es, "fw4b", [FORDER, 2 * HW], BF16)
            k.op(k.dve, lambda: nc.vector.tensor_copy(out=w4b[:], in_=w4[:]), reads=[b_w], writes=[b_w])
            h3b = [Buf(), Buf()]
            zr = Ring([sb(nc, es, f"fz{i}", [FILTER_EMB, 512], F32) for i in range(2)])
            ar = Ring([sb(nc, es, f"fa{i}", [FORDER, 512], F32) for i in range(2)])
            mr = Ring([sb(nc, es, f"fm{i}", [FORDER, 512], F32) for i in range(2)])
            hr = Ring([sb(nc, es, f"fh{i}", [FORDER, 512], F32) for i in range(2)])
            mm = Ring([pm(nc, es, f"fmm{i}", [128, 512], F32) for i in range(8)])
            ws = [w1, w2, w3]
            for d in range(2):
                zsrc = T[("zTf_" if d == 0 else "zTb_") + ty]
                for c0 in range(0, L, 512):
                    z_t, z_b = zr.next()
                    k.dma("sp", z_t[:], zsrc[:, c0:c0 + 512], writes=[z_b])
                    cur, cur_b, kdim = z_t, z_b, FILTER_EMB
                    for layer in range(3):
                        p_t, p_b = mm.next()
                        k.op(k.pe, lambda: nc.tensor.matmul(p_t[0:FORDER, :], lhsT=ws[layer][0:kdim, :], rhs=cur[0:kdim, :], start=True, stop=True),
                             reads=[b_w, cur_b], writes=[p_b])
                        a_t, a_b = ar.next()
                        m_t, m_b = mr.next()
                        k.op(k.dve, lambda: nc.vector.tensor_scalar(out=a_t[:], in0=p_t[0:FORDER, :], scalar1=bf[:, 3:4], scalar2=frb[:, layer:layer + 1],
                                                                    op0=ALU.mult, op1=ALU.add), reads=[p_b, b_w], writes=[a_b])
                        k.op(k.dve, lambda: nc.vector.tensor_scalar(out=m_t[:], in0=a_t[:], scalar1=PI, scalar2=-2 * PI, op0=ALU.is_gt, op1=ALU.mult),
                             reads=[a_b], writes=[m_b])
                        k.op(k.dve, lambda: nc.vector.tensor_tensor(out=a_t[:], in0=a_t[:], in1=m_t[:], op=ALU.add), reads=[a_b, m_b], writes=[a_b])
                        k.op(k.dve, lambda: nc.vector.tensor_scalar(out=m_t[:], in0=a_t[:], scalar1=-PI, scalar2=2 * PI, op0=ALU.is_lt, op1=ALU.mult),
                             reads=[a_b], writes=[m_b])
                        k.op(k.dve, lambda: nc.vector.tensor_tensor(out=a_t[:], in0=a_t[:], in1=m_t[:], op=ALU.add), reads=[a_b, m_b], writes=[a_b])
                        if layer < 2:
                            h_t, h_b = hr.next()
                            k.op(k.act, lambda: nc.scalar.activation(out=h_t[:], in_=a_t[:], func=AF.Sin), reads=[a_b], writes=[h_b])
                            cur, cur_b, kdim = h_t, h_b, FORDER
                        else:
                            k.op(k.act, lambda: nc.scalar.activation(out=h3[d][:, c0:c0 + 512], in_=a_t[:], func=AF.Sin), reads=[a_b],
                                 writes=[h3b[d]], disjoint=True)
            dr = Ring([sb(nc, es, f"fd{i}", [128, 2, 512], F32) for i in range(2)])
            kf = Ring([sb(nc, es, f"fkf{i}", [128, 512], F32) for i in range(2)])
            kc = Ring([sb(nc, es, f"fkc{i}", [128, 2, 512], BF16) for i in range(2)])
            tmp = Ring([sb(nc, es, f"ftm{i}", [128, 512], F32) for i in range(4)])
            bout = Ring([sb(nc, es, f"fbo{i}", [128, 2, 512], BF16) for i in range(3)])
            Bd = [T["Bd_" + ty][0], T["Bd_" + ty][1]]
            Kh = [T["Kh_" + ty][0], T["Kh_" + ty][1]]
            bd_buf = Buf()
            for n2 in range(J):
                d_t, d_b = dr.next()
                k.dma("sp", d_t[:, 0, :], T["decF_" + ty][n2:n2 + 127 * J + 1:J, :], writes=[d_b])
                k.dma("sp", d_t[:, 1, :], T["decB_" + ty][n2:n2 + 127 * J + 1:J, :], writes=[d_b], disjoint=True)
                pf, pfb = mm.next()
                pb, pbb = mm.next()
                k.op(k.pe, lambda: nc.tensor.matmul(pf[:, :], lhsT=h3[0][:, n2:n2 + 127 * J + 1:J], rhs=w4b[:, 0:512], start=True, stop=True),
                     reads=[h3b[0], b_w], writes=[pfb])
                k.op(k.pe, lambda: nc.tensor.matmul(pb[:, :], lhsT=h3[1][:, n2:n2 + 127 * J + 1:J], rhs=w4b[:, 512:1024], start=True, stop=True),
                     reads=[h3b[1], b_w], writes=[pbb])
                kc_t, kc_b = kc.next()
                if n2 == 0:
                    p0, p0b = mm.next()
                    k.op(k.pe, lambda: nc.tensor.matmul(p0[0:1, :], lhsT=h3[0][:, 0:1], rhs=w4b[:, 512:1024], start=True, stop=True),
                         reads=[h3b[0], b_w], writes=[p0b])
                    kf_t, kf_b = kf.next()
                    k.op(k.dve, lambda: nc.vector.tensor_tensor(out=kf_t[:], in0=pf[:, :], in1=d_t[:, 0, :], op=ALU.mult), reads=[pfb, d_b], writes=[kf_b])
                    k.op(k.dve, lambda: nc.vector.tensor_tensor(out=kf_t[0:1, :], in0=kf_t[0:1, :], in1=p0[0:1, :], op=ALU.add),
                         reads=[kf_b, p0b], writes=[kf_b])
                    k.op(k.dve, lambda: nc.vector.tensor_copy(out=kc_t[:, 0, :], in_=kf_t[:]), reads=[kf_b], writes=[kc_b])
                else:
                    k.op(k.dve, lambda: nc.vector.tensor_tensor(out=kc_t[:, 0, :], in0=pf[:, :], in1=d_t[:, 0, :], op=ALU.mult), reads=[pfb, d_b], writes=[kc_b])
                k.op(k.dve, lambda: nc.vector.tensor_tensor(out=kc_t[:, 1, :], in0=pb[:, :], in1=d_t[:, 1, :], op=ALU.mult), reads=[pbb, d_b],
                     writes=[kc_b], disjoint=True)
                stage_a(k, c, J, n2, [kc_t[:, 0, :], kc_t[:, 1, :]], [kc_b], mm, tmp, bout, Bd, bd_buf)
            br = Ring([sb(nc, es, f"fbr{i}", [128, 2, 512], BF16) for i in range(2)])
            kh_buf = G["kh_buf"][ty]
            for g in range(ng):
                b_t, b_b = br.next()
                for e in range(2):
                    k.dma("sp", b_t[:, e, :], Bd[e][g * 128:(g + 1) * 128, :], reads=[bd_buf], writes=[b_b], disjoint=True)
                xr, xrb = mm.next()
                xi, xib = mm.next()
                FB = c["FB"]
                k.op(k.pe, lambda: nc.tensor.matmul(xr[:, :], lhsT=FB[:, 0, :], rhs=b_t[:, 0, :], start=True, stop=False), reads=[b_b, c["buf"]], writes=[xrb])
                k.op(k.pe, lambda: nc.tensor.matmul(xr[:, :], lhsT=FB[:, 2, :], rhs=b_t[:, 1, :], start=False, stop=True), reads=[b_b, c["buf"]], writes=[xrb])
                k.op(k.pe, lambda: nc.tensor.matmul(xi[:, :], lhsT=FB[:, 1, :], rhs=b_t[:, 0, :], start=True, stop=False), reads=[b_b, c["buf"]], writes=[xib])
                k.op(k.pe, lambda: nc.tensor.matmul(xi[:, :], lhsT=FB[:, 0, :], rhs=b_t[:, 1, :], start=False, stop=True), reads=[b_b, c["buf"]], writes=[xib])
                o_t, o_b = bout.next()
                k.op(k.act, lambda: nc.scalar.copy(out=o_t[:, 0, :], in_=xr[:, :]), reads=[xrb], writes=[o_b])
                k.op(k.dve, lambda: nc.vector.tensor_copy(out=o_t[:, 1, :], in_=xi[:, :]), reads=[xib], writes=[o_b], disjoint=True)
                for e in range(2):
                    k.dma("pool", Kh[e][g * 128:(g + 1) * 128, :], o_t[:, e, :], reads=[o_b], writes=[kh_buf], disjoint=True)
        k.barrier()


def phase_hyena(k, cfg, T, G):
    nc = k.nc
    for ty, seqs in (("p", [0]), ("s", list(range(1, cfg.nseq)))):
        L = cfg.Lp if ty == "p" else cfg.Ls
        J = L // 128
        ng = 2 * J
        mo = (cfg.own_p // J) if ty == "p" else 128
        JC = min(J, 4)
        with contextlib.ExitStack() as es:
            c = fft_consts(k, es, T, ty, J, mo, True)
            cw = sb(nc, es, "hcw", [128, 3, 1536], F32)
            cb = sb(nc, es, "hcb", [128, 1536], F32)
            hb = sb(nc, es, "hhb", [128, 512], F32)
            b_w = Buf()
            k.dma("sp", cw[:].rearrange("p a c -> p (a c)"), bc_ap(T["conv_w"]), writes=[b_w])
            k.dma("sp", cb[:], bc_ap(T["conv_b"]), writes=[b_w], disjoint=True)
            k.dma("sp", hb[:], bc_ap(T["hyena_bias"]), writes=[b_w], disjoint=True)
            ur = Ring([sb(nc, es, f"hu{i}", [128, JC + 2, 1536], BF16) for i in range(2)])
            ta = Ring([sb(nc, es, f"hta{i}", [128, 1024], F32) for i in range(2)])
            tb = Ring([sb(nc, es, f"htb{i}", [128, 1024], F32) for i in range(2)])
            tc = Ring([sb(nc, es, f"htc{i}", [128, 1024], F32) for i in range(2)])
            s32 = Ring([sb(nc, es, f"hs32{i}", [128, 512], F32) for i in range(2)])
            sg = Ring([sb(nc, es, f"hsg{i}", [128, 3, 512], BF16) for i in range(3)])
            xso = Ring([sb(nc, es, f"hxo{i}", [128, 2, 512], BF16) for i in range(3)])
            tmp = Ring([sb(nc, es, f"htm{i}", [128, 512], F32) for i in range(4)])
            bout = Ring([sb(nc, es, f"hbo{i}", [128, 2, 512], BF16) for i in range(3)])
            br = Ring([sb(nc, es, f"hbr{i}", [128, 4, 512], BF16) for i in range(2)])
            pw = Ring([sb(nc, es, f"hpw{i}", [128, 512], F32) for i in range(4)])
            yy = Ring([sb(nc, es, f"hyy{i}", [128, 2, 512], BF16) for i in range(3)])
            dr_ = Ring([sb(nc, es, f"hdr{i}", [128, 4, 512], BF16) for i in range(2)])
            xl = Ring([sb(nc, es, f"hxl{i}", [128, 2, 512], BF16) for i in range(2)])
            yo = Ring([sb(nc, es, f"hyo{i}", [128, 512], F32) for i in range(2)])
            mm = Ring([pm(nc, es, f"hmm{i}", [128, 512], F32) for i in range(8)])
            Bd = [T["Bd_" + ty][0], T["Bd_" + ty][1]]
            Cd = [T["Cd_" + ty][0], T["Cd_" + ty][1]]
            Kh = [T["Kh_" + ty][0], T["Kh_" + ty][1]]
            XS = [T["XS"][0], T["XS"][1]]
            bd_buf, cd_buf, xs_buf = Buf(), Buf(), Buf()
            kh_buf = G["kh_buf"][ty]
            U = T["U"]
            for s in seqs:
                ub = int(cfg.foff[s]) + 2 * s
                o0 = int(cfg.ooff[s])
                def conv_part(u_t, u_b, jj):
                    a_t, a_b = ta.next()
                    b_t, b_b = tb.next()
                    c_t, c_b = tc.next()
                    k.op(k.dve, lambda: nc.vector.tensor_tensor(out=a_t[:], in0=u_t[:, jj, 512:1536], in1=cw[:, 0, 512:1536], op=ALU.mult), reads=[u_b, b_w], writes=[a_b])
                    k.op(k.dve, lambda: nc.vector.tensor_tensor(out=b_t[:], in0=u_t[:, jj + 1, 512:1536], in1=cw[:, 1, 512:1536], op=ALU.mult), reads=[u_b, b_w], writes=[b_b])
                    k.op(k.dve, lambda: nc.vector.tensor_tensor(out=a_t[:], in0=a_t[:], in1=b_t[:], op=ALU.add), reads=[a_b, b_b], writes=[a_b])
                    k.op(k.dve, lambda: nc.vector.tensor_tensor(out=b_t[:], in0=u_t[:, jj + 2, 512:1536], in1=cw[:, 2, 512:1536], op=ALU.mult), reads=[u_b, b_w, a_b], writes=[b_b])
                    k.op(k.dve, lambda: nc.vector.tensor_tensor(out=a_t[:], in0=a_t[:], in1=b_t[:], op=ALU.add), reads=[a_b, b_b], writes=[a_b])
                    k.op(k.dve, lambda: nc.vector.tensor_tensor(out=a_t[:], in0=a_t[:], in1=cb[:, 512:1536], op=ALU.add), reads=[a_b, b_w], writes=[a_b])
                    x0c, tq = c_t[:, 0:512], c_t[:, 512:1024]
                    k.op(k.pool, lambda: nc.gpsimd.tensor_tensor(out=x0c, in0=u_t[:, jj, 0:512], in1=cw[:, 0, 0:512], op=ALU.mult), reads=[u_b, b_w], writes=[c_b])
                    k.op(k.pool, lambda: nc.gpsimd.tensor_tensor(out=tq, in0=u_t[:, jj + 1, 0:512], in1=cw[:, 1, 0:512], op=ALU.mult), reads=[u_b, b_w, c_b], writes=[c_b])
                    k.op(k.pool, lambda: nc.gpsimd.tensor_tensor(out=x0c, in0=x0c, in1=tq, op=ALU.add), reads=[c_b], writes=[c_b])
                    k.op(k.pool, lambda: nc.gpsimd.tensor_tensor(out=tq, in0=u_t[:, jj + 2, 0:512], in1=cw[:, 2, 0:512], op=ALU.mult), reads=[u_b, b_w, c_b], writes=[c_b])
                    k.op(k.pool, lambda: nc.gpsimd.tensor_tensor(out=x0c, in0=x0c, in1=tq, op=ALU.add), reads=[c_b], writes=[c_b])
                    k.op(k.pool, lambda: nc.gpsimd.tensor_tensor(out=x0c, in0=x0c, in1=cb[:, 0:512], op=ALU.add), reads=[c_b, b_w], writes=[c_b])
                    s_t, s_b = s32.next()
                    g_t, g_b = sg.next()
                    k.op(k.dve, lambda: nc.vector.tensor_tensor(out=s_t[:], in0=a_t[:, 0:512], in1=a_t[:, 512:1024], op=ALU.mult), reads=[a_b], writes=[s_b])
                    k.op(k.act, lambda: nc.scalar.copy(out=g_t[:, 0, :], in_=s_t[:]), reads=[s_b], writes=[g_b])
                    k.op(k.act, lambda: nc.scalar.copy(out=g_t[:, 1, :], in_=x0c), reads=[c_b], writes=[g_b], disjoint=True)
                    k.op(k.pool, lambda: nc.gpsimd.tensor_tensor(out=tq, in0=s_t[:], in1=hb[:], op=ALU.mult), reads=[s_b, b_w, c_b], writes=[c_b])
                    k.op(k.pool, lambda: nc.gpsimd.tensor_tensor(out=g_t[:, 2, :], in0=tq, in1=x0c, op=ALU.mult), reads=[c_b], writes=[g_b], disjoint=True)
                    return g_t, g_b

                def post_part(n2, g_t, g_b):
                    x_t, x_b = xso.next()
                    for e in range(2):
                        p_t, p_b = mm.next()
                        k.op(k.pe, lambda: nc.tensor.matmul(p_t[0:mo, :], lhsT=c["Sel"][:, :], rhs=g_t[:, 1 + e, :], start=True, stop=True),
                             reads=[g_b, c["buf"]], writes=[p_b])
                        k.op(k.act, lambda: nc.scalar.copy(out=x_t[0:mo, e, :], in_=p_t[0:mo, :]), reads=[p_b], writes=[x_b], disjoint=True)
                    x0_ = XS[0]
                    xdst = bass.AP(tensor=x0_.tensor, offset=x0_.offset + (o0 + n2) * 512, ap=[[J * 512, mo], [cfg.TO * 512, 2], [1, 512]])
                    k.dma("pool", xdst, x_t[0:mo, :, :], reads=[x_b], writes=[xs_buf], disjoint=True)
                    stage_a(k, c, J, n2, [g_t[:, 0, :]], [g_b], mm, tmp, bout, Bd, bd_buf)

                def load_u(j0):
                    u_t, u_b = ur.next()
                    src = bass.AP(tensor=U.tensor, offset=U.offset + (ub + j0) * 1536, ap=[[J * 1536, 128], [1536, JC + 2], [1, 1536]])
                    k.dma("sp", u_t[:], src, writes=[u_b])
                    return u_t, u_b

                prev = None
                j0s = list(range(0, J, JC))
                unext = load_u(j0s[0])
                for ci, j0 in enumerate(j0s):
                    u_t, u_b = unext
                    if ci + 1 < len(j0s):
                        unext = load_u(j0s[ci + 1])
                    for jj in range(JC):
                        cur = (j0 + jj,) + conv_part(u_t, u_b, jj)
                        if prev is not None:
                            post_part(*prev)
                        prev = cur
                post_part(*prev)
                FB = c["FB"]

                def front(g):
                        b_t, b_b = br.next()
                        for e in range(2):
                            k.dma("sp", b_t[:, e, :], Bd[e][g * 128:(g + 1) * 128, :], reads=[bd_buf], writes=[b_b], disjoint=True)
                            k.dma("sp", b_t[:, 2 + e, :], Kh[e][g * 128:(g + 1) * 128, :], reads=[kh_buf], writes=[b_b], disjoint=True)
                        xr, xrb = mm.next()
                        xi, xib = mm.next()
                        FB = c["FB"]
                        k.op(k.pe, lambda: nc.tensor.matmul(xr[:, :], lhsT=FB[:, 0, :], rhs=b_t[:, 0, :], start=True, stop=False), reads=[b_b, c["buf"]], writes=[xrb])
                        k.op(k.pe, lambda: nc.tensor.matmul(xr[:, :], lhsT=FB[:, 2, :], rhs=b_t[:, 1, :], start=False, stop=True), reads=[b_b, c["buf"]], writes=[xrb])
                        k.op(k.pe, lambda: nc.tensor.matmul(xi[:, :], lhsT=FB[:, 1, :], rhs=b_t[:, 0, :], start=True, stop=False), reads=[b_b, c["buf"]], writes=[xib])
                        k.op(k.pe, lambda: nc.tensor.matmul(xi[:, :], lhsT=FB[:, 0, :], rhs=b_t[:, 1, :], start=False, stop=True), reads=[b_b, c["buf"]], writes=[xib])
                        m1, m1b = pw.next()
                        m2, m2b = pw.next()
                        m3, m3b = pw.next()
                        m4, m4b = pw.next()
                        y_t, y_b = yy.next()
                        k.op(k.dve, lambda: nc.vector.tensor_tensor(out=m1[:], in0=xr[:, :], in1=b_t[:, 2, :], op=ALU.mult), reads=[xrb, b_b], writes=[m1b])
                        k.op(k.dve, lambda: nc.vector.tensor_tensor(out=m2[:], in0=xi[:, :], in1=b_t[:, 3, :], op=ALU.mult), reads=[xib, b_b], writes=[m2b])
                        k.op(k.pool, lambda: nc.gpsimd.tensor_tensor(out=y_t[:, 0, :], in0=m1[:], in1=m2[:], op=ALU.subtract), reads=[m1b, m2b], writes=[y_b])
                        k.op(k.dve, lambda: nc.vector.tensor_tensor(out=m3[:], in0=xr[:, :], in1=b_t[:, 3, :], op=ALU.mult), reads=[xrb, b_b], writes=[m3b])
                        k.op(k.dve, lambda: nc.vector.tensor_tensor(out=m4[:], in0=xi[:, :], in1=b_t[:, 2, :], op=ALU.mult), reads=[xib, b_b], writes=[m4b])
                        k.op(k.pool, lambda: nc.gpsimd.tensor_tensor(out=y_t[:, 1, :], in0=m3[:], in1=m4[:], op=ALU.add), reads=[m3b, m4b], writes=[y_b], disjoint=True)
                        return y_t, y_b

                def back(g, y_t, y_b):
                        cr, crb = mm.next()
                        ci, cib = mm.next()
                        k.op(k.pe, lambda: nc.tensor.matmul(cr[:, :], lhsT=FB[:, 0, :], rhs=y_t[:, 0, :], start=True, stop=False), reads=[y_b, c["buf"]], writes=[crb])
                        k.op(k.pe, lambda: nc.tensor.matmul(cr[:, :], lhsT=FB[:, 1, :], rhs=y_t[:, 1, :], start=False, stop=True), reads=[y_b, c["buf"]], writes=[crb])
                        k.op(k.pe, lambda: nc.tensor.matmul(ci[:, :], lhsT=FB[:, 0, :], rhs=y_t[:, 1, :], start=True, stop=False), reads=[y_b, c["buf"]], writes=[cib])
                        k.op(k.pe, lambda: nc.tensor.matmul(ci[:, :], lhsT=FB[:, 2, :], rhs=y_t[:, 0, :], start=False, stop=True), reads=[y_b, c["buf"]], writes=[cib])
                        o_t, o_b = bout.next()
                        tw = c["twB"]
                        tw_evac(k, cr[:, :], crb, ci[:, :], cib, tw[:, 0, g:g + 1], tw[:, 2, g:g + 1], tw[:, 1, g:g + 1], c["buf"],
                                o_t[:, 0, :], o_t[:, 1, :], o_b, tmp)
                        c0_ = Cd[0]
                        cdst = bass.AP(tensor=c0_.tensor, offset=c0_.offset + g * 128 * 512, ap=[[512, 128], [256 * J * 512, 2], [1, 512]])
                        k.dma("pool", cdst, o_t[:], reads=[o_b], writes=[cd_buf], disjoint=True)

                prevg = None
                for g in range(ng):
                    curg = (g,) + front(g)
                    if prevg is not None:
                        back(*prevg)
                    prevg = curg
                back(*prevg)
                for n2 in range(J):
                    d_t, d_b = dr_.next()
                    for ch in range(2):
                        r0 = ch * 128 * J + n2
                        for e in range(2):
                            k.dma("sp", d_t[:, ch * 2 + e, :], Cd[e][r0:r0 + 127 * J + 1:J, :], reads=[cd_buf], writes=[d_b], disjoint=True)
                    x_t, x_b = xl.next()
                    for e in range(2):
                        k.dma("sp", x_t[0:mo, e, :], XS[e][o0 + n2:o0 + n2 + (mo - 1) * J + 1:J, :], reads=[xs_buf], writes=[x_b], disjoint=True)
                    p_t, p_b = mm.next()
                    i = 0
                    for ch in range(2):
                        for e in range(2):
                            k.op(k.pe, lambda: nc.tensor.matmul(p_t[0:mo, :], lhsT=c["IA"][:, e, ch, :], rhs=d_t[:, ch * 2 + e, :], start=(i == 0), stop=(i == 3)),
                                 reads=[d_b, c["buf"]], writes=[p_b])
                            i += 1
                    q_t, q_b = tmp.next()
                    y_t, y_b = yo.next()
                    k.op(k.dve, lambda: nc.vector.tensor_tensor(out=q_t[0:mo, :], in0=p_t[0:mo, :], in1=x_t[0:mo, 0, :], op=ALU.mult), reads=[p_b, x_b], writes=[q_b])
                    k.op(k.pool, lambda: nc.gpsimd.tensor_tensor(out=y_t[0:mo, :], in0=q_t[0:mo, :], in1=x_t[0:mo, 1, :], op=ALU.add), reads=[q_b, x_b], writes=[y_b])
                    k.dma("pool", T["Yh"][o0 + n2:o0 + n2 + (mo - 1) * J + 1:J, :], y_t[0:mo, :], reads=[y_b])
        k.barrier()
```
